# Optimizing a Trainium2 kernel written in Bass

```python
import math
import jax, jax.numpy as jnp
from jax import lax
import numpy as np

D_MODEL = 1024
BATCH = 16
SEQ = 2048
DEPTH = 1

GDN_HEADS = 8
GDN_DK = 128
GDN_DV = 128
GDN_CONV = 4
GDN_CHUNK = 64
DIFF_HEADS = 8
DIFF_D = 64
DIFF_DV = 2 * DIFF_D
Q_BLOCK = 128
ROPE_THETA = 500000.0
ROPE_DIM = DIFF_D // 4
D_FF = 2816
FFN_CONV = 3
EPS = 1e-6

GDN_QK = GDN_HEADS * GDN_DK
GDN_V = GDN_HEADS * GDN_DV
DIFF_QK = DIFF_HEADS * 2 * DIFF_D
DIFF_V = DIFF_HEADS * DIFF_DV
SPLIT_SIZES = (GDN_QK, GDN_QK, GDN_V, GDN_V, GDN_HEADS, GDN_HEADS,
               DIFF_QK, DIFF_QK, DIFF_V, 2 * D_MODEL)
D_IN = GDN_QK * 2 + GDN_V * 2 + GDN_HEADS * 2 + DIFF_QK * 2 + DIFF_V + 2 * D_MODEL

kernel_name = "hybrid_gdn_diffattn_convglu"


def rmsnorm(x, w):
    xf = x.astype(jnp.float32)
    y = xf * lax.rsqrt(jnp.mean(xf * xf, axis=-1, keepdims=True) + EPS) * w.astype(jnp.float32)
    return y.astype(x.dtype)


def l2norm(x):
    xf = x.astype(jnp.float32)
    return xf * lax.rsqrt(jnp.sum(xf * xf, axis=-1, keepdims=True) + EPS)


def split_cols(u, sizes):
    offsets = [int(o) for o in np.cumsum(np.array(sizes))[:-1]]
    return jnp.split(u, offsets, axis=-1)


def causal_dwconv(x, w):
    K, C = w.shape
    return lax.conv_general_dilated(
        x, w[:, None, :].astype(x.dtype), window_strides=(1,), padding=[(K - 1, 0)],
        dimension_numbers=("NWC", "WIO", "NWC"), feature_group_count=C)


def partial_rope(x, cos, sin):
    half = ROPE_DIM // 2
    xf = x.astype(jnp.float32)
    x1, x2, rest = xf[..., :half], xf[..., half:ROPE_DIM], xf[..., ROPE_DIM:]
    c = cos[None, :, None, None, :]
    s = sin[None, :, None, None, :]
    return jnp.concatenate([x1 * c - x2 * s, x2 * c + x1 * s, rest], axis=-1).astype(x.dtype)


def gated_delta_rule(q, k, v, g, beta):
    B, T, H, dk = q.shape
    dv = v.shape[-1]
    C = GDN_CHUNK
    N = T // C
    f32 = jnp.float32
    q = q.astype(f32) * (dk ** -0.5)
    k, v, g, beta = k.astype(f32), v.astype(f32), g.astype(f32), beta.astype(f32)

    def chunk(a):
        a = a.reshape((B, N, C, H) + a.shape[3:])
        return jnp.moveaxis(a, 3, 1)

    qc, kc, vc = chunk(q), chunk(k), chunk(v)
    gc = jnp.cumsum(chunk(g), axis=-1)
    bc = chunk(beta)[..., None]
    kb, vb = kc * bc, vc * bc

    tri = jnp.tril(jnp.ones((C, C), dtype=bool))
    strict = jnp.tril(jnp.ones((C, C), dtype=bool), -1)
    diff = gc[..., :, None] - gc[..., None, :]
    decay = jnp.exp(jnp.where(tri, diff, -jnp.inf))

    L = jnp.where(strict, jnp.einsum("bhncd,bhnsd->bhncs", kb, kc) * decay, 0.0)
    A = L + jnp.eye(C, dtype=f32)
    u = lax.linalg.triangular_solve(A, vb, left_side=True, lower=True, unit_diagonal=True)
    w = lax.linalg.triangular_solve(A, kb * jnp.exp(gc)[..., None], left_side=True,
                                    lower=True, unit_diagonal=True)
    intra = jnp.einsum("bhncd,bhnsd->bhncs", qc, kc) * decay
    g_last = gc[..., -1]
    q_dec = qc * jnp.exp(gc)[..., None]
    k_dec = kc * jnp.exp(g_last[..., None] - gc)[..., None]

    xs = tuple(jnp.moveaxis(a, 2, 0) for a in (q_dec, k_dec, u, w, intra, jnp.exp(g_last)))

    def step(S, inp):
        qe, kd, u_i, w_i, a_i, dl = inp
        v_new = u_i - jnp.einsum("bhcd,bhde->bhce", w_i, S)
        o = jnp.einsum("bhcd,bhde->bhce", qe, S) + jnp.einsum("bhcs,bhse->bhce", a_i, v_new)
        S = S * dl[..., None, None] + jnp.einsum("bhcd,bhce->bhde", kd, v_new)
        return S, o

    S0 = jnp.zeros((B, H, dk, dv), f32)
    _, o = lax.scan(step, S0, xs)
    return jnp.transpose(o, (1, 0, 3, 2, 4)).reshape(B, T, H, dv)


def gdn_branch(q, k, v, z, b, a, conv_w, A_log, dt_bias, norm_w):
    B, T, _ = q.shape
    qkv = jax.nn.silu(causal_dwconv(jnp.concatenate([q, k, v], axis=-1), conv_w))
    q, k, v = split_cols(qkv, (GDN_QK, GDN_QK, GDN_V))
    q = l2norm(q.reshape(B, T, GDN_HEADS, GDN_DK))
    k = l2norm(k.reshape(B, T, GDN_HEADS, GDN_DK))
    v = v.reshape(B, T, GDN_HEADS, GDN_DV)
    beta = jax.nn.sigmoid(b.astype(jnp.float32))
    g = -jnp.exp(A_log.astype(jnp.float32)) * jax.nn.softplus(
        a.astype(jnp.float32) + dt_bias.astype(jnp.float32))
    o = gated_delta_rule(q, k, v, g, beta)
    o = rmsnorm(o, norm_w) * jax.nn.silu(z.reshape(B, T, GDN_HEADS, GDN_DV).astype(jnp.float32))
    return o.reshape(B, T, GDN_V).astype(z.dtype)


def diff_attention_core(q, k, v, lam):
    B, T, H, _, d = q.shape
    nb = T // Q_BLOCK
    scale = d ** -0.5
    qb = jnp.moveaxis(q.reshape(B, nb, Q_BLOCK, H, 2, d), 1, 0)
    kpos = jnp.arange(T)

    def block(args):
        qi, i = args
        s = jnp.einsum("bqhmd,bkhmd->bhmqk", qi, k,
                       preferred_element_type=jnp.float32) * scale
        qpos = i * Q_BLOCK + jnp.arange(Q_BLOCK)
        mask = kpos[None, :] <= qpos[:, None]
        p = jax.nn.softmax(jnp.where(mask, s, -jnp.inf), axis=-1)
        o = jnp.einsum("bhmqk,bkhe->bqhme", p, v.astype(jnp.float32))
        return o[..., 0, :] - lam * o[..., 1, :]

    o = lax.map(block, (qb, jnp.arange(nb)))
    return jnp.moveaxis(o, 0, 1).reshape(B, T, H, v.shape[-1])


def diff_branch(q, k, v, q_norm_w, k_norm_w, lq1, lk1, lq2, lk2, subln_w, lambda_init):
    B, T, _ = q.shape
    pos = jnp.arange(T, dtype=jnp.float32)
    inv_freq = ROPE_THETA ** (-jnp.arange(0, ROPE_DIM, 2, dtype=jnp.float32) / ROPE_DIM)
    ang = pos[:, None] * inv_freq[None, :]
    cos, sin = jnp.cos(ang), jnp.sin(ang)
    q = partial_rope(rmsnorm(q.reshape(B, T, DIFF_HEADS, 2, DIFF_D), q_norm_w), cos, sin)
    k = partial_rope(rmsnorm(k.reshape(B, T, DIFF_HEADS, 2, DIFF_D), k_norm_w), cos, sin)
    v = v.reshape(B, T, DIFF_HEADS, DIFF_DV)
    f32 = jnp.float32
    lam = (jnp.exp(jnp.sum(lq1.astype(f32) * lk1.astype(f32)))
           - jnp.exp(jnp.sum(lq2.astype(f32) * lk2.astype(f32))) + lambda_init)
    o = diff_attention_core(q, k, v, lam)
    o = rmsnorm(o, subln_w) * (1.0 - lambda_init)
    return o.reshape(B, T, DIFF_V).astype(q.dtype)


def conv_glu_ffn(h, w_up, conv_w, conv_b, w_down):
    u = h @ w_up
    u = causal_dwconv(u, conv_w) + conv_b
    gate, val = jnp.split(u, 2, axis=-1)
    return (jax.nn.silu(gate) * val) @ w_down


def setup_inputs(seed: int = 0) -> dict:
    key = jax.random.key(seed)
    ks = jax.random.split(key, 24)
    f32 = jnp.float32
    L = DEPTH

    def nrm(k, shape, fan_in):
        return jax.random.normal(k, shape, f32) * (fan_in ** -0.5)

    def gain(k, shape):
        return 1.0 + 0.02 * jax.random.normal(k, shape, f32)

    dt = jnp.exp(jax.random.uniform(ks[6], (L, GDN_HEADS), f32, math.log(1e-3), math.log(1e-1)))
    return {
        "x": jax.random.normal(ks[0], (BATCH, SEQ, D_MODEL), f32),
        "norm1_w": gain(ks[1], (L, D_MODEL)),
        "w_in": nrm(ks[2], (L, D_MODEL, D_IN), D_MODEL),
        "b_gate": 0.02 * jax.random.normal(ks[3], (L, 2 * D_MODEL), f32),
        "gdn_conv_w": nrm(ks[4], (L, GDN_CONV, GDN_QK * 2 + GDN_V), GDN_CONV),
        "gdn_A_log": jnp.log(jax.random.uniform(ks[5], (L, GDN_HEADS), f32, 1.0, 16.0)),
        "gdn_dt_bias": dt + jnp.log(-jnp.expm1(-dt)),
        "gdn_norm_w": gain(ks[7], (L, GDN_DV)),
        "diff_q_norm_w": gain(ks[8], (L, DIFF_D)),
        "diff_k_norm_w": gain(ks[9], (L, DIFF_D)),
        "lambda_q1": 0.1 * jax.random.normal(ks[10], (L, DIFF_D), f32),
        "lambda_k1": 0.1 * jax.random.normal(ks[11], (L, DIFF_D), f32),
        "lambda_q2": 0.1 * jax.random.normal(ks[12], (L, DIFF_D), f32),
        "lambda_k2": 0.1 * jax.random.normal(ks[13], (L, DIFF_D), f32),
        "diff_subln_w": gain(ks[14], (L, DIFF_DV)),
        "w_gdn_out": nrm(ks[15], (L, GDN_V, D_MODEL), GDN_V),
        "w_diff_out": nrm(ks[16], (L, DIFF_V, D_MODEL), DIFF_V),
        "w_o": nrm(ks[17], (L, D_MODEL, D_MODEL), D_MODEL),
        "norm2_w": gain(ks[18], (L, D_MODEL)),
        "w_up": nrm(ks[19], (L, D_MODEL, 2 * D_FF), D_MODEL),
        "ffn_conv_w": nrm(ks[20], (L, FFN_CONV, 2 * D_FF), FFN_CONV),
        "ffn_conv_b": 0.02 * jax.random.normal(ks[21], (L, 2 * D_FF), f32),
        "w_down": nrm(ks[22], (L, D_FF, D_MODEL), D_FF),
    }


def reference(x, norm1_w, w_in, b_gate, gdn_conv_w, gdn_A_log, gdn_dt_bias, gdn_norm_w,
              diff_q_norm_w, diff_k_norm_w, lambda_q1, lambda_k1, lambda_q2, lambda_k2,
              diff_subln_w, w_gdn_out, w_diff_out, w_o, norm2_w, w_up, ffn_conv_w,
              ffn_conv_b, w_down):
    for layer in range(DEPTH):
        lambda_init = 0.8 - 0.6 * math.exp(-0.3 * layer)
        h = rmsnorm(x, norm1_w[layer])
        u = h @ w_in[layer]
        (g_q, g_k, g_v, g_z, g_b, g_a, d_q, d_k, d_v, gates) = split_cols(u, SPLIT_SIZES)
        y_a = gdn_branch(g_q, g_k, g_v, g_z, g_b, g_a, gdn_conv_w[layer], gdn_A_log[layer],
                         gdn_dt_bias[layer], gdn_norm_w[layer]) @ w_gdn_out[layer]
        y_b = diff_branch(d_q, d_k, d_v, diff_q_norm_w[layer], diff_k_norm_w[layer],
                          lambda_q1[layer], lambda_k1[layer], lambda_q2[layer], lambda_k2[layer],
                          diff_subln_w[layer], lambda_init) @ w_diff_out[layer]
        gate_a, gate_b = jnp.split(jax.nn.sigmoid(gates + b_gate[layer]), 2, axis=-1)
        x = x + (gate_a * y_a + gate_b * y_b) @ w_o[layer]
        h = rmsnorm(x, norm2_w[layer])
        x = x + conv_glu_ffn(h, w_up[layer], ffn_conv_w[layer], ffn_conv_b[layer], w_down[layer])
    return x
```

```python
import contextlib
import math
import numpy as np
import concourse.bass as bass
import concourse.mybir as mybir
from concourse.bass_utils import run_bass_kernel_spmd

F32 = mybir.dt.float32
BF16 = mybir.dt.bfloat16
AF = mybir.ActivationFunctionType
ALU = mybir.AluOpType

D = 1024
D_IN = 9232
D_FF = 2816
NH = 8
EPS = 1e-6
OFF_GQ, OFF_GK, OFF_GV, OFF_GZ, OFF_GB, OFF_GA = 0, 1024, 2048, 3072, 4096, 4104
OFF_DQ, OFF_DK, OFF_DV, OFF_GATE = 4112, 5136, 6160, 7184
LAMBDA_INIT = 0.8 - 0.6 * math.exp(0.0)
ARENA_BYTES = 188 * 1024
V_N1, V_N2, V_BG, V_GCW, V_FCW, V_FCB = 0, 8, 16, 32, 128, 260
V_GNW, V_SLW, V_QNW, V_KNW, V_ALOG, V_DTB, V_LAM = 304, 305, 306, 307, 308, 316, 324
NV = 328


class Res:
    __slots__ = ("name", "last_w", "readers", "sem", "cnt", "excl")

    def __init__(self, name, excl=False):
        self.name = name
        self.excl = excl
        self.last_w = None
        self.readers = {}
        self.sem = None
        self.cnt = 0


class Op:
    __slots__ = ("eng", "fn", "deps", "dma", "sig", "sig_cnt", "marked", "tick", "idx", "pre_drain", "t_end", "cost")


class Sched:
    ENGS = ("pe", "act", "dve", "pool", "sp")
    SAME_WIN = 3
    LAT = 250.0

    def __init__(self, nc):
        self.nc = nc
        self.ops = {e: [] for e in self.ENGS}
        self.last_dma = {}
        self.free = {e: 0.0 for e in self.ENGS}

    def _ready(self, eng, reads, writes):
        t = self.free[eng]
        for r in reads:
            d = r.last_w
            if d is not None:
                t = max(t, d.t_end + (0.0 if (d.eng == eng and not d.dma) else self.LAT))
        for r in writes:
            d = r.last_w
            if d is not None:
                t = max(t, d.t_end + (0.0 if (d.eng == eng and not d.dma) else self.LAT))
            for d in r.readers.values():
                t = max(t, d.t_end + (0.0 if (d.eng == eng and not d.dma) else self.LAT))
        return t

    def peek(self, eng, reads, writes):
        if any(r.excl for r in reads):
            writes = list(writes) + [r for r in reads if r.excl]
        return self._ready(eng, reads, writes)

    def op(self, eng, fn, reads=(), writes=(), dma=False, sig=None, extra=(), cost=300.0):
        o = Op()
        o.eng = eng; o.fn = fn; o.dma = dma; o.marked = False; o.tick = 0
        o.sig = None; o.sig_cnt = 0; o.pre_drain = False
        deps = list(extra)
        if any(r.excl for r in reads):
            writes = list(writes) + [r for r in reads if r.excl]
            reads = [r for r in reads if not r.excl]
        t0 = self._ready(eng, reads, writes)
        for d in extra:
            t0 = max(t0, d.t_end + self.LAT)
        if dma:
            self.free[eng] = t0 + 60.0
        else:
            self.free[eng] = t0 + cost
        o.t_end = t0 + cost
        o.cost = cost
        for r in reads:
            if r.last_w is not None:
                deps.append(r.last_w)
        for r in writes:
            if r.last_w is not None:
                deps.append(r.last_w)
            deps.extend(r.readers.values())
        o.deps = []
        seen = set()
        for d in deps:
            if id(d) in seen:
                continue
            seen.add(id(d))
            if d.dma:
                o.deps.append(d)
            elif d.eng != eng:
                d.marked = True
                o.deps.append(d)
            elif eng != "pe" and len(self.ops[eng]) - d.idx <= self.SAME_WIN and d.cost < 300.0:
                o.pre_drain = True
        o.idx = len(self.ops[eng])
        if dma:
            sig.cnt += 1
            o.sig = sig; o.sig_cnt = sig.cnt
            self.last_dma[id(sig)] = o
        for r in writes:
            r.last_w = o
            r.readers = {}
        for r in reads:
            if r.last_w is not o:
                r.readers[("d", id(sig)) if dma else eng] = o
        self.ops[eng].append(o)
        return o

    def barrier(self):
        lasts = []
        for e in ("pe", "act", "dve", "pool"):
            if self.ops[e]:
                lasts.append(self.op(e, lambda eng: eng.drain(), writes=[Res("bar")]))
        b = self.op("sp", lambda eng: eng.nop(), writes=[Res("bar")],
                    extra=lasts + list(self.last_dma.values()))
        for e in ("pe", "act", "dve", "pool"):
            self.op(e, lambda eng: eng.nop(), extra=[b])

    def emit(self, es, final_res=()):
        nc = self.nc
        esem = {e: es.enter_context(nc.semaphore("eng_" + e)) for e in self.ENGS}
        dres = {}
        for e in self.ENGS:
            for o in self.ops[e]:
                if o.dma and id(o.sig) not in dres:
                    dres[id(o.sig)] = o.sig
        for i, r in enumerate(dres.values()):
            r.sem = es.enter_context(nc.semaphore("d%d_%s" % (i, r.name)))
        self.nsem = len(dres) + len(self.ENGS)
        for e in self.ENGS:
            t = 0
            for o in self.ops[e]:
                if o.marked and not o.dma:
                    t += 1
                o.tick = t
        block = es.enter_context(nc.Block())
        engobj = {"pe": block.tensor, "act": block.scalar, "dve": block.vector,
                  "pool": block.gpsimd, "sp": block.sync}
        self.stats = {}

        def make(e):
            def body(eng):
                seen_e = {}
                seen_d = {}
                nw = 0
                nd = [0]
                for o in self.ops[e]:
                    for d in o.deps:
                        if d.dma:
                            k = id(d.sig); v = d.sig_cnt * 16
                            if seen_d.get(k, 0) >= v:
                                continue
                            seen_d[k] = v
                            eng.wait_ge(d.sig.sem, v); nw += 1
                        else:
                            if seen_e.get(d.eng, 0) >= d.tick:
                                continue
                            seen_e[d.eng] = d.tick
                            eng.wait_ge(esem[d.eng], d.tick); nw += 1
                    if o.pre_drain:
                        eng.drain(); nd[0] += 1
                    ins = o.fn(eng)
                    if o.dma:
                        ins.then_inc(o.sig.sem, 16)
                    elif o.marked:
                        ins.then_inc(esem[e], 1)
                if e == "sp":
                    for r in final_res:
                        eng.wait_ge(r.sem, r.cnt * 16)
                self.stats[e] = (len(self.ops[e]), nw, nd[0])
            return body

        for e in self.ENGS:
            engobj[e](make(e))


def build_program(T, NSEQ, FB, dbg=False, stop_after=9):
    nc = bass.Bass("TRN2", target_bir_lowering=False)
    NB = T // 512
    NCH = T // 64
    NG = NCH // 8
    NTB = T // 128
    NFB = T // FB
    FSUB = FB // 512
    NCF = 192
    NCB = 512

    def din(name, shape):
        return nc.dram_tensor(name, shape, F32, kind="ExternalInput").ap()

    xT = din("xT", [NSEQ, D, T])
    w_in = din("w_in", [D, D_IN])
    w_go = din("w_gdn_out", [D, D])
    w_do = din("w_diff_out", [D, D])
    w_o = din("w_o", [D, D])
    w_up = din("w_up", [D, 2 * D_FF])
    w_dn = din("w_down", [D_FF, D])
    vecs_d = din("vecs", [128, NV])
    cF_d = din("cF", [128, NCF])
    cB_d = din("cB", [128, NCB])
    cR_d = din("cR", [128, 2 * T])
    outT = nc.dram_tensor("outT", [NSEQ, D, T], F32, kind="ExternalOutput").ap()
    dbg_t = {}
    if dbg:
        for nm in ("ygT", "ydT", "x1T", "hnT", "mgT"):
            dbg_t[nm] = nc.dram_tensor("dbg_" + nm, [NSEQ, D, T], F32, kind="ExternalOutput").ap()

    es = contextlib.ExitStack()
    S = Sched(nc)
    RES = {}

    def getres(name):
        if name not in RES:
            RES[name] = Res(name)
        return RES[name]

    def sb(name, shape, dt):
        return es.enter_context(nc.sbuf_tensor(name, shape, dt))

    vecs = sb("vecs_sb", [128, NV], F32); r_vecs = Res("vecs")
    cF = sb("cF_sb", [128, NCF], F32); r_cF = Res("cF")
    cB = sb("cB_sb", [128, NCB], BF16); r_cB = Res("cB")
    cR = sb("cR_sb", [128, 2 * T], BF16)
    ones_f = sb("ones_f", [128, 128], F32)
    ones_b = sb("ones_b", [128, 128], BF16)
    small = sb("small", [128, 64], F32); r_small = Res("small")
    r_ones = Res("ones")
    arena = sb("arena", [128, ARENA_BYTES // 4], F32)
    psum = [es.enter_context(nc.psum_tensor("ps%d" % i, [128, 512], F32)) for i in range(8)]
    r_bank = [Res("bank%d" % i, excl=True) for i in range(8)]

    ident = cB[:, 0:128]
    blockones = cB[:, 128:256]
    ropeP = cB[:, 256:384]
    attmask = cB[:, 384:512]
    triU = cF[0:64, 0:64]
    maskneg = cF[0:64, 64:128]
    strict = cF[0:64, 128:192]
    ropeC = cR[:, 0:T]
    ropeS = cR[:, T:2 * T]
    c_eps = small[:, 0:1]; c_one = small[:, 1:2]; c_nlam = small[:, 2:3]; c_slw8 = small[:, 3:4]
    c_e12 = small[:, 4:6]; c_prod = small[:, 6:8]; c_nA = small[:, 8:16]
    out_res = []

    CUR = [None]

    def EM(eng, fn, reads, writes, cost, dma=False, sig=None):
        if CUR[0] is None:
            S.op(eng, fn, reads, writes, dma=dma, sig=sig, cost=cost)
        else:
            CUR[0].append((eng, fn, tuple(reads), tuple(writes), cost, dma, sig))

    def fsz(ap):
        n = 1
        for d in ap.shape[1:]:
            n *= d
        return n

    def ecost(eng, ap):
        f = fsz(ap)
        if eng == "act":
            return 200.0 + 0.8 * f
        if eng == "pool":
            return 150.0 + 2.0 * f
        return 100.0 + 1.1 * f

    def MM(out, lhsT, rhs, start, stop, reads, writes):
        c = max(64.0, fsz(out) / 4.0) * (4.0 if lhsT.dtype == F32 else 1.0)
        EM("pe", lambda e: e.matmul(out, lhsT, rhs, start=start, stop=stop), reads, writes, c)

    def TR(out, in_, idn, reads, writes):
        EM("pe", lambda e: e.transpose(out, in_, idn), reads, writes, 100.0)

    def ACT(out, in_, func, reads, writes, **kw):
        EM("act", lambda e: e.activation(out=out, in_=in_, func=func, **kw), reads, writes, ecost("act", out))

    def TT(eng, out, in0, in1, op, reads, writes):
        EM(eng, lambda e: e.tensor_tensor(out=out, in0=in0, in1=in1, op=op), reads, writes, ecost(eng, out))

    def STT(eng, out, in0, scalar, in1, op0, op1, reads, writes):
        EM(eng, lambda e: e.scalar_tensor_tensor(out=out, in0=in0, scalar=scalar, in1=in1,
                                                 op0=op0, op1=op1), reads, writes, ecost(eng, out))

    def TS(eng, out, in0, s1, op0, reads, writes, s2=None, op1=None):
        if op1 is None:
            EM(eng, lambda e: e.tensor_scalar(out=out, in0=in0, scalar1=s1, scalar2=None, op0=op0),
               reads, writes, ecost(eng, out))
        else:
            EM(eng, lambda e: e.tensor_scalar(out=out, in0=in0, scalar1=s1, scalar2=s2, op0=op0,
                                              op1=op1), reads, writes, ecost(eng, out))

    def CP(eng, out, in_, reads, writes):
        if eng == "act":
            EM("act", lambda e: e.activation(out=out, in_=in_, func=AF.Copy), reads, writes, ecost("act", out))
        else:
            EM(eng, lambda e: e.tensor_copy(out=out, in_=in_), reads, writes, ecost(eng, out))

    def MSET(eng, ap, val, writes):
        EM(eng, lambda e: e.memset(ap, val), (), writes, 100.0)

    def DMA(q, out, in_, reads, writes, sig):
        EM(q, lambda e: e.dma_start(out=out, in_=in_), reads, writes, 2000.0 + 2.0 * fsz(out), dma=True, sig=sig)

    class Arena:
        def __init__(self):
            self.off = 0

        def alloc(self, name, free_shape, dt, parts=128):
            esz = 4 if dt == F32 else 2
            n = 1
            for s in free_shape:
                n *= s
            nbytes = (n * esz + 63) // 64 * 64
            assert self.off + nbytes <= ARENA_BYTES, (name, self.off, nbytes)
            w0 = self.off // 4
            ap = arena[0:parts, w0:w0 + nbytes // 4]
            if dt != F32:
                ap = ap.bitcast(dt)
            ap = ap[:, 0:n]
            if len(free_shape) == 2:
                ap = ap.rearrange("p (a b) -> p a b", a=free_shape[0])
            elif len(free_shape) == 3:
                ap = ap.rearrange("p (a b c) -> p a b c", a=free_shape[0], b=free_shape[1])
            self.off += nbytes
            return ap, getres(name)

    AR = Arena()

    class Banks:
        def __init__(self):
            self.pool = list(range(8)); self.i = 0

        def set(self, pool):
            self.pool = list(pool); self.i = 0

        def next(self):
            b = self.pool[self.i % len(self.pool)]
            self.i += 1
            return psum[b], r_bank[b]

    BK = Banks()

    class WStream:
        def __init__(self, name, nslots, kchunks, specs, hold=1):
            self.hold = hold
            self.slots = [AR.alloc("%s%d" % (name, i), [kchunks, 128], BF16) for i in range(nslots)]
            self.specs = specs
            self.issued = 0
            self.ns = nslots

        def get(self, i):
            while self.issued < len(self.specs) and self.issued <= i + self.ns - self.hold:
                k = self.issued
                ap, r = self.slots[k % self.ns]
                DMA("pool", ap, self.specs[k], [], [r], r)
                self.issued += 1
            return self.slots[i % self.ns]

    def run_streams(funcs):
        lists = []
        for f in funcs:
            CUR[0] = []
            f()
            lists.append(CUR[0])
        CUR[0] = None
        idx = [0] * len(lists)
        live = [i for i in range(len(lists)) if lists[i]]
        while live:
            best, bt = None, None
            for i in live:
                eng, fn, reads, writes, cost, dma, sig = lists[i][idx[i]]
                t = S.peek(eng, reads, writes)
                if bt is None or t < bt:
                    best, bt = i, t
            eng, fn, reads, writes, cost, dma, sig = lists[best][idx[best]]
            S.op(eng, fn, reads, writes, dma=dma, sig=sig, cost=cost)
            idx[best] += 1
            if idx[best] == len(lists[best]):
                live.remove(best)

    yg_scr = nc.dram_tensor("yg_scr", [NSEQ, D, T], BF16, kind="Internal").ap()
    yd_scr = nc.dram_tensor("yd_scr", [NSEQ, D, T], BF16, kind="Internal").ap()

    def wcols(w, c0, n=128):
        return w[:, c0:c0 + n].rearrange("(c p) n -> p c n", p=128)

    DMA("sp", vecs[:], vecs_d[:, :], [], [r_vecs], r_vecs)
    DMA("sp", cF[:], cF_d[:, :], [], [r_cF], r_cF)
    DMA("pool", cB[:], cB_d[:, :], [], [r_cB], r_cB)
    DMA("pool", cR[:], cR_d[:, :], [], [r_cB], r_cB)
    MSET("dve", ones_f[:], 1.0, [r_ones])
    MSET("dve", ones_b[:], 1.0, [r_ones])
    MSET("dve", small[:], 0.0, [r_small])
    MSET("dve", c_eps, EPS, [r_small])
    MSET("dve", c_one, 1.0, [r_small])
    TS("dve", c_slw8, vecs[:, V_SLW:V_SLW + 1], 1.0 - LAMBDA_INIT, ALU.mult, [r_vecs, r_small], [r_small])
    ACT(c_nA, vecs[:, V_ALOG:V_ALOG + 8], AF.Exp, [r_vecs, r_small], [r_small])
    TS("dve", c_nA, c_nA, -1.0, ALU.mult, [r_small], [r_small])
    TT("dve", c_prod[:, 0:1], vecs[:, V_LAM:V_LAM + 1], vecs[:, V_LAM + 1:V_LAM + 2], ALU.mult, [r_vecs, r_small], [r_small])
    TT("dve", c_prod[:, 1:2], vecs[:, V_LAM + 2:V_LAM + 3], vecs[:, V_LAM + 3:V_LAM + 4], ALU.mult, [r_vecs, r_small], [r_small])
    MM(psum[0][:, 0:64], ones_f[:, :], small[:, 0:64], True, True, [r_small, r_ones], [r_bank[0]])
    ACT(c_e12, psum[0][:, 6:8], AF.Exp, [r_bank[0], r_small], [r_small])
    TT("dve", c_nlam, c_e12[:, 1:2], c_e12[:, 0:1], ALU.subtract, [r_small], [r_small])
    TS("dve", c_nlam, c_nlam, -LAMBDA_INIT, ALU.add, [r_small], [r_small])
    r_const = [r_vecs, r_cF, r_cB, r_ones, r_small]
    if dbg:
        dsm = nc.dram_tensor("dbg_small", [128, 64], F32, kind="ExternalOutput").ap()
        rr_ = getres("dbg_small")
        DMA("sp", dsm[:, :], small[:, :], [r_small], [rr_], rr_)
        out_res.append(rr_)

    def dbg_dump(nm, s, src, r_src, is_bf16):
        if not dbg:
            return
        dst = dbg_t[nm][s].rearrange("(c p) t -> p c t", p=128)
        rr = getres("dbg_" + nm)
        for c in range(8):
            DMA("pool" if is_bf16 else "sp", dst[:, c, :], src[:, c, :], r_src, [rr], rr)
        if rr not in out_res:
            out_res.append(rr)

    for s in range(NSEQ):
        S.barrier()
        AR.off = 0
        hnT, r_hn = AR.alloc("hnT", [8, T], BF16)
        mark_mixer = AR.off
        xsrc = xT[s].rearrange("(c p) t -> p c t", p=128)

        BK.set(range(8))
        xb = [AR.alloc("xb%d" % i, [8, 512], F32) for i in range(2)]
        sqb, r_sqb = AR.alloc("sq1", [8, 512], BF16)
        rt1, r_rt1 = AR.alloc("rt1", [512], F32)
        for blk in range(NB):
            xa, r_xa = xb[blk % 2]
            for c in range(8):
                DMA("sp", xa[:, c, :], xsrc[:, c, blk * 512:(blk + 1) * 512], [], [r_xa], r_xa)
            ACT(sqb, xa, AF.Square, [r_xa], [r_sqb])
            pb, r_pb = BK.next()
            for c in range(8):
                MM(pb[:, :], ones_b[:, :], sqb[:, c, :], c == 0, c == 7, [r_sqb, r_ones], [r_pb])
            ACT(rt1, pb[:, :], AF.Ln, [r_pb, r_small], [r_rt1], scale=1.0 / D, bias=c_eps)
            ACT(rt1, rt1, AF.Exp, [r_rt1], [r_rt1], scale=-0.5)
            for c in range(8):
                STT("dve", hnT[:, c, blk * 512:(blk + 1) * 512], xa[:, c, :], vecs[:, V_N1 + c:V_N1 + c + 1],
                    rt1, ALU.mult, ALU.mult, [r_xa, r_rt1, r_vecs], [r_hn])
        dbg_dump("hnT", s, hnT, [r_hn], True)

        if stop_after < 2:
            continue
        S.barrier()
        AR.off = mark_mixer
        BKg = Banks(); BKg.set([0, 1, 2, 3])
        BKd = Banks(); BKd.set([6, 7])
        ytl = [AR.alloc("ytl%d" % i, [512], BF16) for i in range(4)]
        r_scr = getres("yscr")

        def gdn_stream():
            wgb, r_wgb = AR.alloc("wgb", [8, 16], BF16)
            DMA("pool", wgb, wcols(w_in, OFF_GB, 16), [], [r_wgb], r_wgb)
            beta, r_beta = AR.alloc("beta", [8, NCH], F32, parts=64)
            gtab, r_g = AR.alloc("gtab", [8, NCH], F32, parts=64)
            gcol, r_gcol = AR.alloc("gcol", [8, NCH], F32, parts=64)
            negegc, r_negegc = AR.alloc("negegc", [8, NCH], F32, parts=64)
            kdecay, r_kdecay = AR.alloc("kdecay", [8, NCH], F32, parts=64)
            egl, r_egl = AR.alloc("egl", [8, NCH], F32)
            sptmp, r_sptmp = AR.alloc("sptmp", [NCH, 8], F32, parts=64)
            pb, r_pb = BKg.next()
            for n in range(NCH):
                for c in range(8):
                    MM(pb[0:64, n * 16:(n + 1) * 16], hnT[:, c, n * 64:(n + 1) * 64], wgb[:, c, :], c == 0, c == 7,
                       [r_hn, r_wgb], [r_pb])
            pv = pb[0:64, 0:NCH * 16].rearrange("p (n k) -> p n k", k=16)
            ACT(beta.rearrange("p h n -> p n h"), pv[:, :, 0:8], AF.Sigmoid, [r_pb], [r_beta])
            TT("dve", sptmp, pv[:, :, 8:16], vecs[0:64, V_DTB:V_DTB + 8].unsqueeze(1).to_broadcast([64, NCH, 8]),
               ALU.add, [r_pb, r_vecs], [r_sptmp])
            ACT(sptmp, sptmp, AF.Exp, [r_sptmp], [r_sptmp])
            ACT(sptmp, sptmp, AF.Ln, [r_sptmp, r_small], [r_sptmp], bias=c_one[0:64, :], scale=1.0)
            TT("dve", gtab.rearrange("p h n -> p n h"), sptmp, c_nA[0:64, :].unsqueeze(1).to_broadcast([64, NCH, 8]),
               ALU.mult, [r_sptmp, r_small], [r_g])
            gflat = gtab.rearrange("p h n -> p (h n)")
            pb, r_pb = BKg.next()
            MM(pb[0:64, 0:8 * NCH], triU, gflat, True, True, [r_g, r_cF], [r_pb])
            CP("dve", gcol.rearrange("p h n -> p (h n)"), pb[0:64, 0:8 * NCH], [r_pb], [r_gcol])
            ACT(negegc.rearrange("p h n -> p (h n)"), pb[0:64, 0:8 * NCH], AF.Exp, [r_pb], [r_negegc])
            TS("dve", negegc.rearrange("p h n -> p (h n)"), negegc.rearrange("p h n -> p (h n)"), -1.0, ALU.mult,
               [r_negegc], [r_negegc])
            pb2, r_pb2 = BKg.next()
            MM(pb2[:, 0:8 * NCH], ones_f[0:64, :], gflat, True, True, [r_g, r_ones], [r_pb2])
            ACT(egl.rearrange("p h n -> p (h n)"), pb2[:, 0:8 * NCH], AF.Exp, [r_pb2], [r_egl])
            TT("dve", kdecay.rearrange("p h n -> p (h n)"), pb2[0:64, 0:8 * NCH], gcol.rearrange("p h n -> p (h n)"),
               ALU.subtract, [r_pb2, r_gcol], [r_kdecay])
            ACT(kdecay.rearrange("p h n -> p (h n)"), kdecay.rearrange("p h n -> p (h n)"), AF.Exp, [r_kdecay], [r_kdecay])

            ub = [AR.alloc("ub%d" % i, [T + 4], BF16) for i in range(2)]
            for u_ap, r_u in ub:
                MSET("pool", u_ap[:, 0:4], 0.0, [r_u])
            qT, r_qT = AR.alloc("qT", [T], BF16)
            kT, r_kT = AR.alloc("kT", [T], BF16)
            vTf, r_vTf = AR.alloc("vTf", [T], BF16)
            zs, r_zs = AR.alloc("zs", [T], BF16)
            qdT, r_qdT = AR.alloc("qdT", [T], BF16)
            tmpq, r_tmpq = AR.alloc("tmpq", [T], BF16)
            r_qdTg = [getres("qdT_g%d" % g) for g in range(NG)]
            sq2 = [AR.alloc("sq2%d" % i, [512], BF16) for i in range(2)]
            rs2 = [AR.alloc("rs2%d" % i, [512], F32) for i in range(2)]
            Kd, r_Kd = AR.alloc("Kd", [NCH, 128], BF16, parts=64)
            Vt, r_Vt = AR.alloc("Vt", [NCH, 128], BF16, parts=64)
            gA, r_gA = AR.alloc("gA", [8, 64], F32, parts=64)
            decT, r_decT = AR.alloc("decT", [8, 64], F32, parts=64)
            nbs, r_nbs = AR.alloc("nbs", [8, 64], F32, parts=64)
            gB, r_gB = AR.alloc("gB", [8, 64], F32, parts=64)
            egcb, r_egcb = AR.alloc("egcb", [512], F32)
            Pb = [AR.alloc("P%d" % i, [8, 64], BF16, parts=64) for i in range(4)]
            Xb = [AR.alloc("X%d" % i, [8, 64], BF16, parts=64) for i in range(2)]
            Xall, r_Xall = AR.alloc("Xall", [NCH, 64], BF16, parts=64)
            intraT, r_intraT = AR.alloc("intraT", [NCH, 64], BF16, parts=64)
            oT, r_oT = AR.alloc("oT", [T], BF16)
            r_Xg = [getres("Xall_g%d" % g) for g in range(NG)]
            r_iTg = [getres("intraT_g%d" % g) for g in range(NG)]
            Sf, r_Sf = AR.alloc("Sf", [128], F32)
            Sb, r_Sb = AR.alloc("Sb", [128], BF16)
            Rb = [AR.alloc("R%d" % i, [128], BF16, parts=64) for i in range(2)]
            vnb = [AR.alloc("vn%d" % i, [128], BF16, parts=64) for i in range(2)]
            dg = [AR.alloc("dg%d" % i, [4, 128], BF16) for i in range(2)]
            ytmp = [AR.alloc("ytmp%d" % i, [512], F32) for i in range(1)] * 2
            specs = []
            for h in range(NH):
                for off in (OFF_GQ, OFF_GK, OFF_GV, OFF_GZ):
                    specs.append(wcols(w_in, off + h * 128))
            WS = WStream("wg", 3, 8, specs)
            dgi = 0
            for h in range(NH):
                for ti, (dst, r_dst) in enumerate(((tmpq, r_tmpq), (kT, r_kT), (vTf, r_vTf))):
                    wt, r_wt = WS.get(h * 4 + ti)
                    u_ap, r_u = ub[ti % 2]
                    for blk in range(NB):
                        pb, r_pb = BKg.next()
                        for c in range(8):
                            MM(pb[:, :], wt[:, c, :], hnT[:, c, blk * 512:(blk + 1) * 512], c == 0, c == 7,
                               [r_wt, r_hn], [r_pb])
                        CP("act" if blk % 2 == 0 else "dve", u_ap[:, 4 + blk * 512:4 + (blk + 1) * 512], pb[:, :], [r_pb], [r_u])
                    dga, r_dga = dg[dgi % 2]; dgi += 1
                    tile = ti * 8 + h
                    for j in range(4):
                        TS("dve", dga[:, j, :], ident, vecs[:, V_GCW + tile * 4 + j:V_GCW + tile * 4 + j + 1], ALU.mult,
                           [r_cB, r_vecs], [r_dga])
                    for blk in range(NB):
                        pb, r_pb = BKg.next()
                        for j in range(4):
                            MM(pb[:, :], dga[:, j, :], u_ap[:, 1 + j + blk * 512:1 + j + (blk + 1) * 512], j == 0, j == 3,
                               [r_dga, r_u], [r_pb])
                        ACT(dst[:, blk * 512:(blk + 1) * 512], pb[:, :], AF.Silu, [r_pb], [r_dst])
                wt, r_wt = WS.get(h * 4 + 3)
                for blk in range(NB):
                    pb, r_pb = BKg.next()
                    for c in range(8):
                        MM(pb[:, :], wt[:, c, :], hnT[:, c, blk * 512:(blk + 1) * 512], c == 0, c == 7, [r_wt, r_hn], [r_pb])
                    ACT(zs[:, blk * 512:(blk + 1) * 512], pb[:, :], AF.Silu, [r_pb], [r_zs])
                ii = 0
                for (src, r_src, dst, r_dst, scl) in ((tmpq, r_tmpq, qT, r_qT, 128 ** -0.5), (kT, r_kT, kT, r_kT, 1.0)):
                    for blk in range(NB):
                        sl = slice(blk * 512, (blk + 1) * 512)
                        sqa, r_sqa = sq2[ii % 2]; rsa, r_rsa = rs2[ii % 2]; ii += 1
                        TT("pool", sqa, src[:, sl], src[:, sl], ALU.mult, [r_src], [r_sqa])
                        pb, r_pb = BKg.next()
                        MM(pb[:, :], ones_b[:, :], sqa, True, True, [r_sqa, r_ones], [r_pb])
                        ACT(rsa, pb[:, :], AF.Ln, [r_pb, r_small], [r_rsa], scale=1.0, bias=c_eps)
                        ACT(rsa, rsa, AF.Exp, [r_rsa], [r_rsa], scale=-0.5)
                        STT("dve", dst[:, sl], src[:, sl], scl, rsa, ALU.mult, ALU.mult, [r_src, r_rsa], [r_dst])
                for (src, r_src, dst, r_dst, isk) in ((kT, r_kT, Kd, r_Kd, True), (vTf, r_vTf, Vt, r_Vt, False)):
                    for g in range(NG):
                        pb, r_pb = BKg.next()
                        pbb = pb[:].bitcast(BF16)
                        for j in range(8):
                            n = g * 8 + j
                            TR(pbb[0:64, j * 128:(j + 1) * 128], src[:, n * 64:(n + 1) * 64], ident, [r_src, r_cB], [r_pb])
                        pv3 = pbb[0:64, 0:1024].rearrange("p (a b) -> p a b", a=8)
                        if isk:
                            TT("dve", dst[:, g * 8:(g + 1) * 8, :], pv3,
                               kdecay[:, h, g * 8:(g + 1) * 8].unsqueeze(2).to_broadcast([64, 8, 128]), ALU.mult,
                               [r_pb, r_kdecay], [r_dst])
                        else:
                            CP("act", dst[:, g * 8:(g + 1) * 8, :], pv3, [r_pb], [r_dst])
                def inv_group(g, h=h):
                    ns = slice(g * 8, (g + 1) * 8)
                    cs = slice(g * 512, (g + 1) * 512)
                    TT("dve", gA, triU.unsqueeze(1).to_broadcast([64, 8, 64]),
                       gtab[:, h, ns].unsqueeze(2).to_broadcast([64, 8, 64]), ALU.mult, [r_cF, r_g], [r_gA])
                    pbc, r_pbc = BKg.next()
                    MM(pbc[:, :], ones_f[0:64, :], gA.rearrange("p a b -> p (a b)"), True, True, [r_gA, r_ones], [r_pbc])
                    pbc3 = pbc[0:64, :].rearrange("p (a b) -> p a b", a=8)
                    ACT(egcb, pbc[:, :], AF.Exp, [r_pbc], [r_egcb])
                    TT("dve", qdT[:, cs], qT[:, cs], egcb, ALU.mult, [r_qT, r_egcb], [r_qdTg[g]])
                    TT("dve", gA, pbc3, gcol[:, h, ns].unsqueeze(2).to_broadcast([64, 8, 64]), ALU.subtract,
                       [r_pbc, r_gcol], [r_gA])
                    TT("dve", gA, gA, maskneg.unsqueeze(1).to_broadcast([64, 8, 64]), ALU.min, [r_gA, r_cF], [r_gA])
                    ACT(decT, gA, AF.Exp, [r_gA], [r_decT])
                    TT("pool", nbs, strict.unsqueeze(1).to_broadcast([64, 8, 64]),
                       beta[:, h, ns].unsqueeze(2).to_broadcast([64, 8, 64]), ALU.mult, [r_cF, r_beta], [r_nbs])
                    pkk, r_pkk = BKg.next()
                    pqk, r_pqk = BKg.next()
                    for j in range(8):
                        n = g * 8 + j
                        tsl = slice(n * 64, (n + 1) * 64)
                        MM(pkk[0:64, j * 64:(j + 1) * 64], kT[:, tsl], kT[:, tsl], True, True, [r_kT], [r_pkk])
                    for j in range(8):
                        n = g * 8 + j
                        tsl = slice(n * 64, (n + 1) * 64)
                        MM(pqk[0:64, j * 64:(j + 1) * 64], kT[:, tsl], qT[:, tsl], True, True, [r_kT, r_qT], [r_pqk])
                    pkk3 = pkk[0:64, :].rearrange("p (a b) -> p a b", a=8)
                    pqk3 = pqk[0:64, :].rearrange("p (a b) -> p a b", a=8)
                    TT("dve", gB, pkk3, decT, ALU.mult, [r_pkk, r_decT], [r_gB])
                    P1, r_P1 = Pb[0]; P1T, r_P1T = Pb[1]
                    STT("dve", P1, gB, -1.0, nbs, ALU.mult, ALU.mult, [r_gB, r_nbs], [r_P1])
                    TT("dve", intraT[:, ns, :], pqk3, decT, ALU.mult, [r_pqk, r_decT], [r_iTg[g]])
                    ptb, r_ptb = BKg.next()
                    ptbb = ptb[:].bitcast(BF16)
                    for j in range(8):
                        TR(ptbb[0:64, j * 64:(j + 1) * 64], P1[:, j, :], ident[0:64, 0:64], [r_P1, r_cB], [r_ptb])
                    CP("act", P1T, ptbb[0:64, 0:512].rearrange("p (a b) -> p a b", a=8), [r_ptb], [r_P1T])
                    Xc, r_Xc = Xb[0]
                    TT("pool", Xc, P1, ident[0:64, 0:64].unsqueeze(1).to_broadcast([64, 8, 64]), ALU.add, [r_P1, r_cB], [r_Xc])
                    cur = 0
                    xi = 0
                    for lev in range(5):
                        (Pc, r_Pc), (PTc, r_PTc) = Pb[cur * 2], Pb[cur * 2 + 1]
                        (Pn, r_Pn), (PTn, r_PTn) = Pb[(1 - cur) * 2], Pb[(1 - cur) * 2 + 1]
                        last = lev == 4
                        if not last:
                            pa, r_pa = BKg.next()
                            for j in range(8):
                                MM(pa[0:64, j * 64:(j + 1) * 64], PTc[:, j, :], Pc[:, j, :], True, True, [r_Pc, r_PTc], [r_pa])
                        pt, r_pt = BKg.next()
                        for j in range(8):
                            MM(pt[0:64, j * 64:(j + 1) * 64], Pc[:, j, :], PTc[:, j, :], True, True, [r_Pc, r_PTc], [r_pt])
                        if not last:
                            CP("act", Pn, pa[0:64, :].rearrange("p (a b) -> p a b", a=8), [r_pa], [r_Pn])
                        CP("dve", PTn, pt[0:64, :].rearrange("p (a b) -> p a b", a=8), [r_pt], [r_PTn])
                        (Xc, r_Xc), (Xn, r_Xn) = Xb[xi], Xb[1 - xi]
                        px, r_px = BKg.next()
                        for j in range(8):
                            MM(px[0:64, j * 64:(j + 1) * 64], PTn[:, j, :], Xc[:, j, :], True, True, [r_PTn, r_Xc], [r_px])
                        if last:
                            TT("dve", Xall[:, ns, :], px[0:64, :].rearrange("p (a b) -> p a b", a=8), Xc, ALU.add,
                               [r_px, r_Xc], [r_Xg[g]])
                        else:
                            TT("dve", Xn, px[0:64, :].rearrange("p (a b) -> p a b", a=8), Xc, ALU.add, [r_px, r_Xc], [r_Xn])
                        xi = 1 - xi
                        cur = 1 - cur
                def scan_steps(g, h=h):
                    for n in range(g * 8, g * 8 + 8):
                        tsl = slice(n * 64, (n + 1) * 64)
                        pks, r_pks = psum[0][0:64, 0:128], r_bank[0]
                        py, r_py = psum[0][0:64, 128:256], r_bank[0]
                        po, r_po = psum[1][:, 0:64], r_bank[1]
                        pd, r_pd = psum[1][:, 64:192], r_bank[1]
                        Ra, r_Ra = Rb[n % 2]; vna, r_vna = vnb[n % 2]
                        col = slice(n, n + 1)
                        MM(pks, kT[:, tsl], Sb, True, True, [r_kT, r_Sb], [r_pks])
                        STT("dve", Ra, pks, negegc[:, h, col], Vt[:, n, :], ALU.mult, ALU.add,
                            [r_pks, r_negegc, r_Vt], [r_Ra])
                        MM(py, Xall[:, n, :], Ra, True, True, [r_Xg[g], r_Ra], [r_py])
                        TS("dve", vna, py, beta[:, h, col], ALU.mult, [r_py, r_beta], [r_vna])
                        MM(po, Sb, qdT[:, tsl], True, False, [r_Sb, r_qdTg[g]], [r_po])
                        MM(po, vna, intraT[:, n, :], False, True, [r_vna, r_iTg[g]], [r_po])
                        MM(pd, Kd[:, n, :], vna, True, True, [r_Kd, r_vna], [r_pd])
                        STT("dve", Sb, Sf, egl[:, h, col], pd, ALU.mult, ALU.add, [r_Sf, r_egl, r_pd], [r_Sb])
                        STT("dve", Sf, Sf, egl[:, h, col], pd, ALU.mult, ALU.add, [r_Sf, r_egl, r_pd], [r_Sf])
                        CP("act", oT[:, tsl], po, [r_po], [r_oT])

                def record(f):
                    saved = CUR[0]
                    CUR[0] = []
                    f()
                    out = CUR[0]
                    CUR[0] = saved
                    return out

                def splice(A, B):
                    ca = sum(x[4] for x in A) or 1.0
                    cb = sum(x[4] for x in B) or 1.0
                    ia = ib = 0
                    fa = fb_ = 0.0
                    while ia < len(A) or ib < len(B):
                        if ib >= len(B) or (ia < len(A) and fa / ca <= fb_ / cb):
                            CUR[0].append(A[ia]); fa += A[ia][4]; ia += 1
                        else:
                            CUR[0].append(B[ib]); fb_ += B[ib][4]; ib += 1

                BKg.set([2, 3])
                inv_group(0)
                MSET("dve", Sf, 0.0, [r_Sf])
                MSET("pool", Sb, 0.0, [r_Sb])
                for g in range(NG):
                    A = record(lambda: inv_group(g + 1)) if g + 1 < NG else []
                    B = record(lambda: scan_steps(g))
                    splice(A, B)
                BKg.set([0, 1, 2, 3])
                for blk in range(NB):
                    sl = slice(blk * 512, (blk + 1) * 512)
                    sqa, r_sqa = sq2[blk % 2]; rsa, r_rsa = rs2[blk % 2]; yta, r_yta = ytmp[blk % 2]
                    ACT(sqa, oT[:, sl], AF.Square, [r_oT], [r_sqa])
                    pb, r_pb = BKg.next()
                    MM(pb[:, :], ones_b[:, :], sqa, True, True, [r_sqa, r_ones], [r_pb])
                    ACT(rsa, pb[:, :], AF.Ln, [r_pb, r_small], [r_rsa], scale=1.0 / 128, bias=c_eps)
                    ACT(rsa, rsa, AF.Exp, [r_rsa], [r_rsa], scale=-0.5)
                    STT("dve", yta, oT[:, sl], vecs[:, V_GNW:V_GNW + 1], rsa, ALU.mult, ALU.mult, [r_oT, r_rsa, r_vecs], [r_yta])
                    yt_, r_yt_ = ytl[blk % 2]
                    TT("pool", yt_, yta, zs[:, sl], ALU.mult, [r_yta, r_zs], [r_yt_])
                    DMA("sp", yg_scr[s][h * 128:(h + 1) * 128, sl], yt_, [r_yt_], [r_scr], r_yt_)

        def diff_stream():
            qz = [AR.alloc("dqz%d" % i, [T], BF16) for i in range(2)]
            MSET("pool", qz[0][0][64:128, :], 0.0, [qz[0][1]])
            MSET("pool", qz[1][0][0:64, :], 0.0, [qz[1][1]])
            qT, r_qT = None, None
            kT, r_kT = AR.alloc("dkT", [T], BF16)
            Vk, r_Vk = AR.alloc("Vk", [NTB, 128], BF16)
            sq3 = [AR.alloc("sq3%d" % i, [512], BF16) for i in range(2)]
            rs3 = [AR.alloc("rs3%d" % i, [512], F32) for i in range(2)]
            qn = [AR.alloc("qn%d" % i, [512], BF16) for i in range(2)]
            t1b = [AR.alloc("t1%d" % i, [512], F32) for i in range(2)]
            t2b = [AR.alloc("t2%d" % i, [512], F32) for i in range(2)]
            pTb = [AR.alloc("pT%d" % i, [512], BF16) for i in range(3)]
            rcp = [AR.alloc("rcp%d" % i, [512], F32) for i in range(1)] * 2
            a12 = [AR.alloc("a12%d" % i, [512], F32) for i in range(2)]
            dTt, r_dT = AR.alloc("dTt", [512], F32)
            specs = []
            for h in range(NH):
                for off in (OFF_DQ, OFF_DK, OFF_DV):
                    specs.append(wcols(w_in, off + h * 128))
            WS = WStream("wd", 3, 8, specs)
            ii = 0
            pti = 0
            for h in range(NH):
                BKd.set([4, 5, 6, 7])
                for ti, (dst, r_dst, vcol) in enumerate(((qT, r_qT, V_QNW), (kT, r_kT, V_KNW))):
                    wt, r_wt = WS.get(h * 3 + ti)
                    for blk in range(NB):
                        sl = slice(blk * 512, (blk + 1) * 512)
                        sqa, r_sqa = sq3[ii % 2]; rsa, r_rsa = rs3[ii % 2]; qna, r_qna = qn[ii % 2]
                        t1, r_t1 = t1b[ii % 2]; t2, r_t2 = t2b[ii % 2]; ii += 1
                        pu, r_pu = BKd.next()
                        for c in range(8):
                            MM(pu[:, :], wt[:, c, :], hnT[:, c, sl], c == 0, c == 7, [r_wt, r_hn], [r_pu])
                        ACT(sqa, pu[:, :], AF.Square, [r_pu], [r_sqa])
                        pss, r_pss = BKd.next()
                        MM(pss[:, :], blockones, sqa, True, True, [r_sqa, r_cB], [r_pss])
                        ACT(rsa, pss[:, :], AF.Ln, [r_pss, r_small], [r_rsa], scale=1.0 / 64, bias=c_eps)
                        ACT(rsa, rsa, AF.Exp, [r_rsa], [r_rsa], scale=-0.5)
                        STT("dve", qna, pu[:, :], vecs[:, vcol:vcol + 1], rsa, ALU.mult, ALU.mult, [r_pu, r_rsa, r_vecs], [r_qna])
                        pr, r_pr = BKd.next()
                        MM(pr[:, :], ropeP, qna, True, True, [r_qna, r_cB], [r_pr])
                        TT("dve", t1, pr[:, :], ropeS[:, sl], ALU.mult, [r_pr, r_cB], [r_t1])
                        TT("pool", t2, qna, ropeC[:, sl], ALU.mult, [r_qna, r_cB], [r_t2])
                        if ti == 0:
                            TT("pool", qz[0][0][0:64, sl], t1[0:64, :], t2[0:64, :], ALU.add, [r_t1, r_t2], [qz[0][1]])
                            TT("pool", qz[1][0][64:128, sl], t1[64:128, :], t2[64:128, :], ALU.add, [r_t1, r_t2], [qz[1][1]])
                        else:
                            TT("pool", dst[:, sl], t1, t2, ALU.add, [r_t1, r_t2], [r_dst])
                wt, r_wt = WS.get(h * 3 + 2)
                for tb4 in range(NTB // 4):
                    pv_, r_pv = BKd.next()
                    for q4 in range(4):
                        tb = tb4 * 4 + q4
                        for c in range(8):
                            MM(pv_[:, q4 * 128:(q4 + 1) * 128], hnT[:, c, tb * 128:(tb + 1) * 128], wt[:, c, :], c == 0, c == 7,
                               [r_hn, r_wt], [r_pv])
                    CP("act" if tb4 % 2 == 0 else "dve", Vk[:, tb4 * 4:(tb4 + 1) * 4, :],
                       pv_[:, :].rearrange("p (a b) -> p a b", a=4), [r_pv], [r_Vk])
                BKd.set([6, 7])
                for qb in range(NB):
                    sl = slice(qb * 512, (qb + 1) * 512)
                    for m in range(2):
                        pO, r_pO = psum[4], r_bank[4]
                        pL, r_pL = psum[5], r_bank[5]
                        nkb = (qb + 1) * 4
                        ms = slice(64 * m, 64 * m + 64)
                        def stage1(kb):
                            nonlocal pti
                            j = kb - qb * 4
                            c0 = j * 128 if j >= 0 else 0
                            psc, r_psc = BKd.next()
                            pT, r_pT = pTb[pti % 3]; pti += 1
                            MM(psc[:, c0:512], kT[:, kb * 128:(kb + 1) * 128], qz[m][0][:, qb * 512 + c0:(qb + 1) * 512], True, True,
                               [r_kT, qz[m][1]], [r_psc])
                            ACT(pT[:, c0:512], psc[:, c0:512], AF.Exp, [r_psc], [r_pT], scale=0.125)
                            if j >= 0:
                                TT("pool", pT[:, c0:c0 + 128], pT[:, c0:c0 + 128], attmask, ALU.mult, [r_pT, r_cB], [r_pT])
                            return pT, r_pT, c0

                        cur_st = stage1(0)
                        for kb in range(nkb):
                            nxt_st = stage1(kb + 1) if kb + 1 < nkb else None
                            pT, r_pT, c0 = cur_st
                            MM(pO[:, c0:512], Vk[:, kb, :], pT[:, c0:512], kb == 0, kb == nkb - 1, [r_Vk, r_pT], [r_pO])
                            MM(pL[:, c0:512], ones_b[:, :], pT[:, c0:512], kb == 0, kb == nkb - 1, [r_ones, r_pT], [r_pL])
                            cur_st = nxt_st
                        rc, r_rc = rcp[m]; aa, r_aa = a12[m]
                        ACT(rc, pL[:, :], AF.Ln, [r_pL], [r_rc])
                        ACT(rc, rc, AF.Exp, [r_rc], [r_rc], scale=-1.0)
                        TT("dve", aa, pO[:, :], rc, ALU.mult, [r_pO, r_rc], [r_aa])
                    STT("dve", dTt, a12[1][0], c_nlam, a12[0][0], ALU.mult, ALU.add, [a12[1][1], a12[0][1], r_small], [r_dT])
                    sqa, r_sqa = sq3[ii % 2]; rsa, r_rsa = rs3[ii % 2]; ii += 1
                    ACT(sqa, dTt, AF.Square, [r_dT], [r_sqa])
                    pss, r_pss = BKd.next()
                    MM(pss[:, :], ones_b[:, :], sqa, True, True, [r_sqa, r_ones], [r_pss])
                    ACT(rsa, pss[:, :], AF.Ln, [r_pss, r_small], [r_rsa], scale=1.0 / 128, bias=c_eps)
                    ACT(rsa, rsa, AF.Exp, [r_rsa], [r_rsa], scale=-0.5)
                    yt_, r_yt_ = ytl[2 + qb % 2]
                    STT("dve", yt_, dTt, c_slw8, rsa, ALU.mult, ALU.mult, [r_dT, r_rsa, r_small], [r_yt_])
                    DMA("sp", yd_scr[s][h * 128:(h + 1) * 128, sl], yt_, [r_yt_], [r_scr], r_yt_)

        run_streams([gdn_stream, diff_stream])

        if stop_after < 4:
            continue
        S.barrier()
        BK.set(range(8))
        AR.off = mark_mixer
        ygT, r_yg = AR.alloc("ygT", [8, T], BF16)
        ydT, r_yd = AR.alloc("ydT", [8, T], BF16)
        for c in range(8):
            DMA("sp", ygT[:, c, :], yg_scr[s][c * 128:(c + 1) * 128, :], [r_scr], [r_yg], r_yg)
            DMA("sp", ydT[:, c, :], yd_scr[s][c * 128:(c + 1) * 128, :], [r_scr], [r_yd], r_yd)
        dbg_dump("ygT", s, ygT, [r_yg], True)
        dbg_dump("ydT", s, ydT, [r_yd], True)
        mgT, r_mg = AR.alloc("mgT", [8, T], BF16)
        gsb = [AR.alloc("gs%d" % i, [512], F32) for i in range(4)]
        m12 = [AR.alloc("m12%d" % i, [512], F32) for i in range(4)]
        specs = []
        for m in range(8):
            specs += [wcols(w_in, OFF_GATE + m * 128), wcols(w_in, OFF_GATE + D + m * 128),
                      wcols(w_go, m * 128), wcols(w_do, m * 128)]
        WS = WStream("wm", 8, 8, specs, hold=4)
        ii = 0
        for m in range(8):
            wts = [WS.get(m * 4 + k) for k in range(4)]
            for blk in range(NB):
                sl = slice(blk * 512, (blk + 1) * 512)
                mm_ = []
                for k in range(2):
                    (wg_, r_wg_), (wy_, r_wy_) = wts[k], wts[2 + k]
                    ysrc, r_ysrc = ((ygT, r_yg), (ydT, r_yd))[k]
                    ga, r_ga = gsb[ii % 4]; ma, r_ma = m12[ii % 4]; ii += 1
                    pg, r_pg = BK.next()
                    for c in range(8):
                        MM(pg[:, :], wg_[:, c, :], hnT[:, c, sl], c == 0, c == 7, [r_wg_, r_hn], [r_pg])
                    ACT(ga, pg[:, :], AF.Sigmoid, [r_pg, r_vecs], [r_ga], bias=vecs[:, V_BG + k * 8 + m:V_BG + k * 8 + m + 1], scale=1.0)
                    py_, r_py_ = BK.next()
                    for c in range(8):
                        MM(py_[:, :], wy_[:, c, :], ysrc[:, c, sl], c == 0, c == 7, [r_wy_, r_ysrc], [r_py_])
                    TT("dve", ma, py_[:, :], ga, ALU.mult, [r_py_, r_ga], [r_ma])
                    mm_.append((ma, r_ma))
                TT("pool", mgT[:, m, sl], mm_[0][0], mm_[1][0], ALU.add, [mm_[0][1], mm_[1][1]], [r_mg])
        dbg_dump("mgT", s, mgT, [r_mg], True)

        S.barrier()
        AR.off = 0
        x1T = AR.alloc("x1T", [8, T], F32)[0]
        r_x1 = [getres("x1_%d" % c) for c in range(8)]
        assert AR.off <= mark_mixer + 2 * 8 * T * 2
        AR.off = mark_mixer + 3 * 8 * T * 2
        specs = [wcols(w_o, m * 128) for m in range(8)]
        WS = WStream("wo", 3, 8, specs)
        for m in range(8):
            DMA("sp", x1T[:, m, :], xsrc[:, m, :], [], [r_x1[m]], r_x1[m])
        for m in range(8):
            wt, r_wt = WS.get(m)
            for blk in range(NB):
                sl = slice(blk * 512, (blk + 1) * 512)
                pb, r_pb = BK.next()
                for c in range(8):
                    MM(pb[:, :], wt[:, c, :], mgT[:, c, sl], c == 0, c == 7, [r_wt, r_mg], [r_pb])
                TT("dve", x1T[:, m, sl], x1T[:, m, sl], pb[:, :], ALU.add, [r_x1[m], r_pb], [r_x1[m]])
        if dbg:
            dst = dbg_t["x1T"][s].rearrange("(c p) t -> p c t", p=128)
            rr = getres("dbg_x1")
            for c in range(8):
                DMA("sp", dst[:, c, :], x1T[:, c, :], [r_x1[c]], [rr], rr)
            if rr not in out_res:
                out_res.append(rr)

        if stop_after < 5:
            continue
        S.barrier()
        AR.off = 8 * T * 4
        h2T, r_h2 = AR.alloc("h2T", [8, T], BF16)
        sqb, r_sqb = AR.alloc("sq5", [8, 512], BF16)
        rt5, r_rt5 = AR.alloc("rt5", [512], F32)
        for blk in range(NB):
            sl = slice(blk * 512, (blk + 1) * 512)
            ACT(sqb, x1T[:, :, sl], AF.Square, r_x1, [r_sqb])
            pb, r_pb = BK.next()
            for c in range(8):
                MM(pb[:, :], ones_b[:, :], sqb[:, c, :], c == 0, c == 7, [r_sqb, r_ones], [r_pb])
            ACT(rt5, pb[:, :], AF.Ln, [r_pb, r_small], [r_rt5], scale=1.0 / D, bias=c_eps)
            ACT(rt5, rt5, AF.Exp, [r_rt5], [r_rt5], scale=-0.5)
            for c in range(8):
                STT("dve", h2T[:, c, sl], x1T[:, c, sl], vecs[:, V_N2 + c:V_N2 + c + 1], rt5, ALU.mult, ALU.mult,
                    [r_x1[c], r_rt5, r_vecs], [r_h2])
        S.barrier()
        AR.off = 8 * T * 4 + 8 * T * 2
        actT = AR.alloc("actT", [22, FB], BF16)[0]
        r_act = [getres("act0"), getres("act1")]
        stg = [AR.alloc("stg%d" % i, [FB + 4], BF16) for i in range(4)]
        sgb = [AR.alloc("sg%d" % i, [512], BF16) for i in range(4)]
        dgf = [AR.alloc("dgf%d" % i, [3, 128], BF16) for i in range(4)]
        r_out = getres("out")
        if r_out not in out_res:
            out_res.append(r_out)
        odst = outT[s].rearrange("(c p) t -> p c t", p=128)
        mark_w = AR.off
        for fb in range(NFB):
            t0 = fb * FB
            AR.off = mark_w
            WD = WStream("wdn", 2, 22, [wcols(w_dn, m * 128) for m in range(8)])

            def up_stream(par):
                jl = list(range(par, 22, 2))
                specs = []
                for j in jl:
                    specs += [wcols(w_up, j * 128), wcols(w_up, D_FF + j * 128)]
                WSx = WStream("wu%d" % par, 3, 8, specs)
                bk = Banks(); bk.set([0, 1, 2, 3] if par == 0 else [4, 5, 6, 7])
                si = 0
                for ji, j in enumerate(jl):
                    for k in range(2):
                        wt, r_wt = WSx.get(ji * 2 + k)
                        st, r_st = stg[par * 2 + si % 2]; dga, r_dga = dgf[par * 2 + si % 2]; si += 1
                        tile = k * 22 + j
                        for tap in range(3):
                            TS("dve", dga[:, tap, :], ident, vecs[:, V_FCW + tile * 3 + tap:V_FCW + tile * 3 + tap + 1],
                               ALU.mult, [r_cB, r_vecs], [r_dga])
                        if fb == 0:
                            MSET("pool", st[:, 0:4], 0.0, [r_st])
                        else:
                            ph, r_ph = bk.next()
                            for c in range(8):
                                MM(ph[:, 0:2], wt[:, c, :], h2T[:, c, t0 - 2:t0], c == 0, c == 7, [r_wt, r_h2], [r_ph])
                            CP("dve", st[:, 2:4], ph[:, 0:2], [r_ph], [r_st])
                        for sub in range(FSUB):
                            pb, r_pb = bk.next()
                            for c in range(8):
                                MM(pb[:, :], wt[:, c, :], h2T[:, c, t0 + sub * 512:t0 + (sub + 1) * 512], c == 0, c == 7,
                                   [r_wt, r_h2], [r_pb])
                            CP("act" if (sub + k) % 2 == 0 else "dve", st[:, 4 + sub * 512:4 + (sub + 1) * 512], pb[:, :], [r_pb], [r_st])
                        for sub in range(FSUB):
                            pb, r_pb = bk.next()
                            for tap in range(3):
                                MM(pb[:, :], dga[:, tap, :], st[:, 2 + tap + sub * 512:2 + tap + (sub + 1) * 512], tap == 0, tap == 2,
                                   [r_dga, r_st], [r_pb])
                            sg_, r_sg_ = sgb[par * 2 + sub % 2]
                            if k == 0:
                                ACT(sg_, pb[:, :], AF.Silu, [r_pb, r_vecs], [r_sg_], bias=vecs[:, V_FCB + tile:V_FCB + tile + 1], scale=1.0)
                            else:
                                STT("dve", actT[:, j, sub * 512:(sub + 1) * 512], pb[:, :], vecs[:, V_FCB + tile:V_FCB + tile + 1], sg_,
                                    ALU.add, ALU.mult, [r_pb, r_sg_, r_vecs], [r_act[par]])

            run_streams([lambda: up_stream(0), lambda: up_stream(1)])
            for m in range(8):
                wt, r_wt = WD.get(m)
                for sub in range(FSUB):
                    sl = slice(t0 + sub * 512, t0 + (sub + 1) * 512)
                    pb, r_pb = BK.next()
                    for c in range(22):
                        MM(pb[:, :], wt[:, c, :], actT[:, c, sub * 512:(sub + 1) * 512], c == 0, c == 21, [r_wt, r_act[c % 2]], [r_pb])
                    TT("dve", x1T[:, m, sl], x1T[:, m, sl], pb[:, :], ALU.add, [r_x1[m], r_pb], [r_x1[m]])
                DMA("sp", odst[:, m, t0:t0 + FB], x1T[:, m, t0:t0 + FB], [r_x1[m]], [r_out], r_out)

    S.emit(es, final_res=out_res)
    es.close()
    return nc, S


def _consts(T):
    cF = np.zeros((128, 192), np.float32)
    cR = np.zeros((128, 2 * T), np.float32)
    k = np.arange(64)
    cF[:64, 0:64] = (k[:, None] <= k[None, :]).astype(np.float32)
    cF[:64, 64:128] = np.where(k[None, :] >= k[:, None], 0.0, -1e30).astype(np.float32)
    cF[:64, 128:192] = (k[:, None] < k[None, :]).astype(np.float32)
    pos = np.arange(T, dtype=np.float32)
    inv_freq = (np.float32(500000.0) ** (-(np.arange(0, 16, 2, dtype=np.float32)) / np.float32(16))).astype(np.float32)
    ang = (pos[:, None] * inv_freq[None, :]).astype(np.float32)
    cos = np.cos(ang.astype(np.float64)).astype(np.float32).T
    sin = np.sin(ang.astype(np.float64)).astype(np.float32).T
    C = np.ones((128, T), np.float32)
    Sg = np.zeros((128, T), np.float32)
    P = np.zeros((128, 128), np.float32)
    for base in (0, 64):
        C[base:base + 8] = cos; C[base + 8:base + 16] = cos
        Sg[base:base + 8] = -sin; Sg[base + 8:base + 16] = sin
        for r in range(8):
            P[base + r, base + r + 8] = 1.0
            P[base + r + 8, base + r] = 1.0
    cR[:, 0:T] = C
    cR[:, T:] = Sg
    cB = np.zeros((128, 512), np.float32)
    cB[:, 0:128] = np.eye(128, dtype=np.float32)
    cB[:64, 128:192] = 1.0
    cB[64:, 192:256] = 1.0
    cB[:, 256:384] = P
    p = np.arange(128)
    cB[:, 384:512] = (p[:, None] <= p[None, :]).astype(np.float32)
    return cF, cB, cR


def _vecs(inp):
    v = np.zeros((128, NV), np.float32)
    g = lambda k: np.asarray(inp[k], np.float32)[0]
    v[:, V_N1:V_N1 + 8] = g("norm1_w").reshape(8, 128).T
    v[:, V_N2:V_N2 + 8] = g("norm2_w").reshape(8, 128).T
    v[:, V_BG:V_BG + 16] = g("b_gate").reshape(16, 128).T
    v[:, V_GCW:V_GCW + 96] = g("gdn_conv_w").reshape(4, 24, 128).transpose(2, 1, 0).reshape(128, 96)
    v[:, V_FCW:V_FCW + 132] = g("ffn_conv_w").reshape(3, 44, 128).transpose(2, 1, 0).reshape(128, 132)
    v[:, V_FCB:V_FCB + 44] = g("ffn_conv_b").reshape(44, 128).T
    v[:, V_GNW] = g("gdn_norm_w")
    v[:, V_SLW] = g("diff_subln_w")
    v[:, V_QNW] = np.tile(g("diff_q_norm_w"), 2)
    v[:, V_KNW] = np.tile(g("diff_k_norm_w"), 2)
    v[:, V_ALOG:V_ALOG + 8] = g("gdn_A_log")[None, :]
    v[:, V_DTB:V_DTB + 8] = g("gdn_dt_bias")[None, :]
    v[:64, V_LAM + 0] = g("lambda_q1"); v[:64, V_LAM + 1] = g("lambda_k1")
    v[:64, V_LAM + 2] = g("lambda_q2"); v[:64, V_LAM + 3] = g("lambda_k2")
    return v


_CACHE = {}


def run(inputs, T, B, FB, dbg=False, ncores=8):
    NSEQ = B // ncores
    key = (T, NSEQ, FB, dbg)
    if key not in _CACHE:
        _CACHE[key] = build_program(T, NSEQ, FB, dbg)
    nc, S = _CACHE[key]
    x = np.asarray(inputs["x"], np.float32)
    cF, cB, cR = _consts(T)
    vecs = _vecs(inputs)
    shared = {
        "w_in": np.ascontiguousarray(np.asarray(inputs["w_in"], np.float32)[0]),
        "w_gdn_out": np.ascontiguousarray(np.asarray(inputs["w_gdn_out"], np.float32)[0]),
        "w_diff_out": np.ascontiguousarray(np.asarray(inputs["w_diff_out"], np.float32)[0]),
        "w_o": np.ascontiguousarray(np.asarray(inputs["w_o"], np.float32)[0]),
        "w_up": np.ascontiguousarray(np.asarray(inputs["w_up"], np.float32)[0]),
        "w_down": np.ascontiguousarray(np.asarray(inputs["w_down"], np.float32)[0]),
        "vecs": vecs, "cF": cF, "cB": cB, "cR": cR,
    }
    in_maps = []
    for c in range(ncores):
        m = dict(shared)
        m["xT"] = np.ascontiguousarray(x[c * NSEQ:(c + 1) * NSEQ].transpose(0, 2, 1))
        in_maps.append(m)
    res = run_bass_kernel_spmd(nc, in_maps, core_ids=list(range(ncores)))
    out = np.concatenate([r["outT"].transpose(0, 2, 1) for r in res.results], axis=0)
    if dbg:
        extra = {}
        for nm in ("ygT", "ydT", "x1T", "hnT", "mgT"):
            extra[nm] = np.concatenate([r["dbg_" + nm].transpose(0, 2, 1) for r in res.results], axis=0)
        extra["small"] = res.results[0]["dbg_small"]
        return np.ascontiguousarray(out), extra
    return np.ascontiguousarray(out)


def kernel(**inputs):
    return run(inputs, T=2048, B=16, FB=1024)
```

```python
import contextlib
import math
import numpy as np
import concourse.bass as bass
import concourse.mybir as mybir
from concourse.bass_utils import run_bass_kernel_spmd

F32 = mybir.dt.float32
BF16 = mybir.dt.bfloat16
AF = mybir.ActivationFunctionType
ALU = mybir.AluOpType

D = 1024
D_IN = 9232
D_FF = 2816
NH = 8
EPS = 1e-6
OFF_GQ, OFF_GK, OFF_GV, OFF_GZ, OFF_GB, OFF_GA = 0, 1024, 2048, 3072, 4096, 4104
OFF_DQ, OFF_DK, OFF_DV, OFF_GATE = 4112, 5136, 6160, 7184
LAMBDA_INIT = 0.8 - 0.6 * math.exp(0.0)
ARENA_BYTES = 188 * 1024
V_N1, V_N2, V_BG, V_GCW, V_FCW, V_FCB = 0, 8, 16, 32, 128, 260
V_GNW, V_SLW, V_QNW, V_KNW, V_ALOG, V_DTB, V_LAM = 304, 305, 306, 307, 308, 316, 324
NV = 328


class Res:
    __slots__ = ("name", "last_w", "readers", "sem", "cnt", "excl")

    def __init__(self, name, excl=False):
        self.name = name
        self.excl = excl
        self.last_w = None
        self.readers = {}
        self.sem = None
        self.cnt = 0


class Op:
    __slots__ = ("eng", "fn", "deps", "dma", "sig", "sig_cnt", "marked", "tick", "idx", "pre_drain", "t_end", "cost")


class Sched:
    ENGS = ("pe", "act", "dve", "pool", "sp")
    SAME_WIN = 3
    LAT = 150.0

    def __init__(self, nc):
        self.nc = nc
        self.ops = {e: [] for e in self.ENGS}
        self.last_dma = {}
        self.free = {e: 0.0 for e in self.ENGS}

    def _ready(self, eng, reads, writes):
        t = self.free[eng]
        for r in reads:
            d = r.last_w
            if d is not None:
                t = max(t, d.t_end + (0.0 if (d.eng == eng and not d.dma) else self.LAT))
        for r in writes:
            d = r.last_w
            if d is not None:
                t = max(t, d.t_end + (0.0 if (d.eng == eng and not d.dma) else self.LAT))
            for d in r.readers.values():
                t = max(t, d.t_end + (0.0 if (d.eng == eng and not d.dma) else self.LAT))
        return t

    def peek(self, eng, reads, writes):
        if any(r.excl for r in reads):
            writes = list(writes) + [r for r in reads if r.excl]
        return self._ready(eng, reads, writes)

    def op(self, eng, fn, reads=(), writes=(), dma=False, sig=None, extra=(), cost=300.0):
        o = Op()
        o.eng = eng; o.fn = fn; o.dma = dma; o.marked = False; o.tick = 0
        o.sig = None; o.sig_cnt = 0; o.pre_drain = False
        deps = list(extra)
        if any(r.excl for r in reads):
            writes = list(writes) + [r for r in reads if r.excl]
            reads = [r for r in reads if not r.excl]
        t0 = self._ready(eng, reads, writes)
        for d in extra:
            t0 = max(t0, d.t_end + self.LAT)
        if dma:
            self.free[eng] = t0 + 60.0
        else:
            self.free[eng] = t0 + cost
        o.t_end = t0 + cost
        o.cost = cost
        for r in reads:
            if r.last_w is not None:
                deps.append(r.last_w)
        for r in writes:
            if r.last_w is not None:
                deps.append(r.last_w)
            deps.extend(r.readers.values())
        o.deps = []
        seen = set()
        for d in deps:
            if id(d) in seen:
                continue
            seen.add(id(d))
            if d.dma:
                o.deps.append(d)
            elif d.eng != eng:
                d.marked = True
                o.deps.append(d)
            elif eng != "pe" and len(self.ops[eng]) - d.idx <= self.SAME_WIN and d.cost < 300.0:
                o.pre_drain = True
        o.idx = len(self.ops[eng])
        if dma:
            sig.cnt += 1
            o.sig = sig; o.sig_cnt = sig.cnt
            self.last_dma[id(sig)] = o
        for r in writes:
            r.last_w = o
            r.readers = {}
        for r in reads:
            if r.last_w is not o:
                r.readers[("d", id(sig)) if dma else eng] = o
        self.ops[eng].append(o)
        return o

    def barrier(self):
        lasts = []
        for e in ("pe", "act", "dve", "pool"):
            if self.ops[e]:
                lasts.append(self.op(e, lambda eng: eng.drain(), writes=[Res("bar")]))
        b = self.op("sp", lambda eng: eng.nop(), writes=[Res("bar")],
                    extra=lasts + list(self.last_dma.values()))
        for e in ("pe", "act", "dve", "pool"):
            self.op(e, lambda eng: eng.nop(), extra=[b])

    def emit(self, es, final_res=()):
        nc = self.nc
        esem = {e: es.enter_context(nc.semaphore("eng_" + e)) for e in self.ENGS}
        dres = {}
        for e in self.ENGS:
            for o in self.ops[e]:
                if o.dma and id(o.sig) not in dres:
                    dres[id(o.sig)] = o.sig
        for i, r in enumerate(dres.values()):
            r.sem = es.enter_context(nc.semaphore("d%d_%s" % (i, r.name)))
        self.nsem = len(dres) + len(self.ENGS)
        for e in self.ENGS:
            t = 0
            for o in self.ops[e]:
                if o.marked and not o.dma:
                    t += 1
                o.tick = t
        block = es.enter_context(nc.Block())
        engobj = {"pe": block.tensor, "act": block.scalar, "dve": block.vector,
                  "pool": block.gpsimd, "sp": block.sync}
        self.stats = {}

        def make(e):
            def body(eng):
                seen_e = {}
                seen_d = {}
                nw = 0
                nd = [0]
                for o in self.ops[e]:
                    for d in o.deps:
                        if d.dma:
                            k = id(d.sig); v = d.sig_cnt * 16
                            if seen_d.get(k, 0) >= v:
                                continue
                            seen_d[k] = v
                            eng.wait_ge(d.sig.sem, v); nw += 1
                        else:
                            if seen_e.get(d.eng, 0) >= d.tick:
                                continue
                            seen_e[d.eng] = d.tick
                            eng.wait_ge(esem[d.eng], d.tick); nw += 1
                    if o.pre_drain:
                        eng.drain(); nd[0] += 1
                    ins = o.fn(eng)
                    if o.dma:
                        ins.then_inc(o.sig.sem, 16)
                    elif o.marked:
                        ins.then_inc(esem[e], 1)
                if e == "sp":
                    for r in final_res:
                        eng.wait_ge(r.sem, r.cnt * 16)
                self.stats[e] = (len(self.ops[e]), nw, nd[0])
            return body

        for e in self.ENGS:
            engobj[e](make(e))


def build_program(T, NSEQ, FB, dbg=False, stop_after=9):
    nc = bass.Bass("TRN2", target_bir_lowering=False)
    NB = T // 512
    NCH = T // 64
    NG = NCH // 8
    NTB = T // 128
    NFB = T // FB
    FSUB = FB // 512
    NCF = 192
    NCB = 512

    def din(name, shape):
        return nc.dram_tensor(name, shape, F32, kind="ExternalInput").ap()

    xT = din("xT", [NSEQ, D, T])
    w_in = din("w_in", [D, D_IN])
    w_go = din("w_gdn_out", [D, D])
    w_do = din("w_diff_out", [D, D])
    w_o = din("w_o", [D, D])
    w_up = din("w_up", [D, 2 * D_FF])
    w_dn = din("w_down", [D_FF, D])
    vecs_d = din("vecs", [128, NV])
    cF_d = din("cF", [128, NCF])
    cB_d = din("cB", [128, NCB])
    cR_d = din("cR", [128, 2 * T])
    outT = nc.dram_tensor("outT", [NSEQ, D, T], F32, kind="ExternalOutput").ap()
    dbg_t = {}
    if dbg:
        for nm in ("ygT", "ydT", "x1T", "hnT", "mgT"):
            dbg_t[nm] = nc.dram_tensor("dbg_" + nm, [NSEQ, D, T], F32, kind="ExternalOutput").ap()

    es = contextlib.ExitStack()
    S = Sched(nc)
    RES = {}

    def getres(name):
        if name not in RES:
            RES[name] = Res(name)
        return RES[name]

    def sb(name, shape, dt):
        return es.enter_context(nc.sbuf_tensor(name, shape, dt))

    vecs = sb("vecs_sb", [128, NV], F32); r_vecs = Res("vecs")
    cF = sb("cF_sb", [128, NCF], F32); r_cF = Res("cF")
    cB = sb("cB_sb", [128, NCB], BF16); r_cB = Res("cB")
    cR = sb("cR_sb", [128, 2 * T], BF16)
    ones_f = sb("ones_f", [128, 128], F32)
    ones_b = sb("ones_b", [128, 128], BF16)
    small = sb("small", [128, 64], F32); r_small = Res("small")
    r_ones = Res("ones")
    arena = sb("arena", [128, ARENA_BYTES // 4], F32)
    psum = [es.enter_context(nc.psum_tensor("ps%d" % i, [128, 512], F32)) for i in range(8)]
    r_bank = [Res("bank%d" % i, excl=True) for i in range(8)]

    ident = cB[:, 0:128]
    blockones = cB[:, 128:256]
    ropeP = cB[:, 256:384]
    attmask = cB[:, 384:512]
    triU = cF[0:64, 0:64]
    maskneg = cF[0:64, 64:128]
    strict = cF[0:64, 128:192]
    ropeC = cR[:, 0:T]
    ropeS = cR[:, T:2 * T]
    c_eps = small[:, 0:1]; c_one = small[:, 1:2]; c_nlam = small[:, 2:3]; c_slw8 = small[:, 3:4]
    c_e12 = small[:, 4:6]; c_prod = small[:, 6:8]; c_nA = small[:, 8:16]
    out_res = []

    CUR = [None]

    def EM(eng, fn, reads, writes, cost, dma=False, sig=None):
        if CUR[0] is None:
            S.op(eng, fn, reads, writes, dma=dma, sig=sig, cost=cost)
        else:
            CUR[0].append((eng, fn, tuple(reads), tuple(writes), cost, dma, sig))

    def fsz(ap):
        n = 1
        for d in ap.shape[1:]:
            n *= d
        return n

    def ecost(eng, ap):
        f = fsz(ap)
        if eng == "act":
            return 200.0 + 0.6 * f
        if eng == "pool":
            return 150.0 + 1.5 * f
        return 100.0 + 0.8 * f

    def MM(out, lhsT, rhs, start, stop, reads, writes):
        c = max(64.0, fsz(out) / 3.5) * (4.0 if lhsT.dtype == F32 else 1.0)
        EM("pe", lambda e: e.matmul(out, lhsT, rhs, start=start, stop=stop), reads, writes, c)

    def TR(out, in_, idn, reads, writes):
        EM("pe", lambda e: e.transpose(out, in_, idn), reads, writes, 100.0)

    def ACT(out, in_, func, reads, writes, **kw):
        EM("act", lambda e: e.activation(out=out, in_=in_, func=func, **kw), reads, writes, ecost("act", out))

    def TT(eng, out, in0, in1, op, reads, writes):
        EM(eng, lambda e: e.tensor_tensor(out=out, in0=in0, in1=in1, op=op), reads, writes, ecost(eng, out))

    def STT(eng, out, in0, scalar, in1, op0, op1, reads, writes):
        EM(eng, lambda e: e.scalar_tensor_tensor(out=out, in0=in0, scalar=scalar, in1=in1,
                                                 op0=op0, op1=op1), reads, writes, ecost(eng, out))

    def TS(eng, out, in0, s1, op0, reads, writes, s2=None, op1=None):
        if op1 is None:
            EM(eng, lambda e: e.tensor_scalar(out=out, in0=in0, scalar1=s1, scalar2=None, op0=op0),
               reads, writes, ecost(eng, out))
        else:
            EM(eng, lambda e: e.tensor_scalar(out=out, in0=in0, scalar1=s1, scalar2=s2, op0=op0,
                                              op1=op1), reads, writes, ecost(eng, out))

    def CP(eng, out, in_, reads, writes):
        if eng == "act":
            EM("act", lambda e: e.activation(out=out, in_=in_, func=AF.Copy), reads, writes, ecost("act", out))
        else:
            EM(eng, lambda e: e.tensor_copy(out=out, in_=in_), reads, writes, ecost(eng, out))

    def MSET(eng, ap, val, writes):
        EM(eng, lambda e: e.memset(ap, val), (), writes, 100.0)

    def DMA(q, out, in_, reads, writes, sig):
        EM(q, lambda e: e.dma_start(out=out, in_=in_), reads, writes, 2000.0 + 2.0 * fsz(out), dma=True, sig=sig)

    class Arena:
        def __init__(self):
            self.off = 0

        def alloc(self, name, free_shape, dt, parts=128):
            esz = 4 if dt == F32 else 2
            n = 1
            for s in free_shape:
                n *= s
            nbytes = (n * esz + 63) // 64 * 64
            assert self.off + nbytes <= ARENA_BYTES, (name, self.off, nbytes)
            w0 = self.off // 4
            ap = arena[0:parts, w0:w0 + nbytes // 4]
            if dt != F32:
                ap = ap.bitcast(dt)
            ap = ap[:, 0:n]
            if len(free_shape) == 2:
                ap = ap.rearrange("p (a b) -> p a b", a=free_shape[0])
            elif len(free_shape) == 3:
                ap = ap.rearrange("p (a b c) -> p a b c", a=free_shape[0], b=free_shape[1])
            self.off += nbytes
            return ap, getres(name)

    AR = Arena()

    class Banks:
        def __init__(self):
            self.pool = list(range(8)); self.i = 0

        def set(self, pool):
            self.pool = list(pool); self.i = 0

        def next(self):
            b = self.pool[self.i % len(self.pool)]
            self.i += 1
            return psum[b], r_bank[b]

    BK = Banks()

    class WStream:
        def __init__(self, name, nslots, kchunks, specs, hold=1):
            self.hold = hold
            self.slots = [AR.alloc("%s%d" % (name, i), [kchunks, 128], BF16) for i in range(nslots)]
            self.specs = specs
            self.issued = 0
            self.ns = nslots

        def get(self, i):
            while self.issued < len(self.specs) and self.issued <= i + self.ns - self.hold:
                k = self.issued
                ap, r = self.slots[k % self.ns]
                DMA("pool", ap, self.specs[k], [], [r], r)
                self.issued += 1
            return self.slots[i % self.ns]

    def run_streams(funcs):
        lists = []
        for f in funcs:
            CUR[0] = []
            f()
            lists.append(CUR[0])
        CUR[0] = None
        idx = [0] * len(lists)
        live = [i for i in range(len(lists)) if lists[i]]
        while live:
            best, bt = None, None
            for i in live:
                eng, fn, reads, writes, cost, dma, sig = lists[i][idx[i]]
                t = S.peek(eng, reads, writes)
                if bt is None or t < bt:
                    best, bt = i, t
            eng, fn, reads, writes, cost, dma, sig = lists[best][idx[best]]
            S.op(eng, fn, reads, writes, dma=dma, sig=sig, cost=cost)
            idx[best] += 1
            if idx[best] == len(lists[best]):
                live.remove(best)

    yg_scr = nc.dram_tensor("yg_scr", [NSEQ, D, T], BF16, kind="Internal").ap()
    yd_scr = nc.dram_tensor("yd_scr", [NSEQ, D, T], BF16, kind="Internal").ap()

    def wcols(w, c0, n=128):
        return w[:, c0:c0 + n].rearrange("(c p) n -> p c n", p=128)

    DMA("sp", vecs[:], vecs_d[:, :], [], [r_vecs], r_vecs)
    DMA("sp", cF[:], cF_d[:, :], [], [r_cF], r_cF)
    DMA("pool", cB[:], cB_d[:, :], [], [r_cB], r_cB)
    DMA("pool", cR[:], cR_d[:, :], [], [r_cB], r_cB)
    MSET("dve", ones_f[:], 1.0, [r_ones])
    MSET("dve", ones_b[:], 1.0, [r_ones])
    MSET("dve", small[:], 0.0, [r_small])
    MSET("dve", c_eps, EPS, [r_small])
    MSET("dve", c_one, 1.0, [r_small])
    TS("dve", c_slw8, vecs[:, V_SLW:V_SLW + 1], 1.0 - LAMBDA_INIT, ALU.mult, [r_vecs, r_small], [r_small])
    ACT(c_nA, vecs[:, V_ALOG:V_ALOG + 8], AF.Exp, [r_vecs, r_small], [r_small])
    TS("dve", c_nA, c_nA, -1.0, ALU.mult, [r_small], [r_small])
    TT("dve", c_prod[:, 0:1], vecs[:, V_LAM:V_LAM + 1], vecs[:, V_LAM + 1:V_LAM + 2], ALU.mult, [r_vecs, r_small], [r_small])
    TT("dve", c_prod[:, 1:2], vecs[:, V_LAM + 2:V_LAM + 3], vecs[:, V_LAM + 3:V_LAM + 4], ALU.mult, [r_vecs, r_small], [r_small])
    MM(psum[0][:, 0:64], ones_f[:, :], small[:, 0:64], True, True, [r_small, r_ones], [r_bank[0]])
    ACT(c_e12, psum[0][:, 6:8], AF.Exp, [r_bank[0], r_small], [r_small])
    TT("dve", c_nlam, c_e12[:, 1:2], c_e12[:, 0:1], ALU.subtract, [r_small], [r_small])
    TS("dve", c_nlam, c_nlam, -LAMBDA_INIT, ALU.add, [r_small], [r_small])
    r_const = [r_vecs, r_cF, r_cB, r_ones, r_small]
    if dbg:
        dsm = nc.dram_tensor("dbg_small", [128, 64], F32, kind="ExternalOutput").ap()
        rr_ = getres("dbg_small")
        DMA("sp", dsm[:, :], small[:, :], [r_small], [rr_], rr_)
        out_res.append(rr_)

    def dbg_dump(nm, s, src, r_src, is_bf16):
        if not dbg:
            return
        dst = dbg_t[nm][s].rearrange("(c p) t -> p c t", p=128)
        rr = getres("dbg_" + nm)
        for c in range(8):
            DMA("pool" if is_bf16 else "sp", dst[:, c, :], src[:, c, :], r_src, [rr], rr)
        if rr not in out_res:
            out_res.append(rr)

    for s in range(NSEQ):
        S.barrier()
        AR.off = 0
        hnT, r_hn = AR.alloc("hnT", [8, T], BF16)
        mark_mixer = AR.off
        xsrc = xT[s].rearrange("(c p) t -> p c t", p=128)

        BK.set(range(8))
        xb = [AR.alloc("xb%d" % i, [8, 512], F32) for i in range(2)]
        sqb, r_sqb = AR.alloc("sq1", [8, 512], BF16)
        rt1, r_rt1 = AR.alloc("rt1", [512], F32)
        for blk in range(NB):
            xa, r_xa = xb[blk % 2]
            for c in range(8):
                DMA("sp", xa[:, c, :], xsrc[:, c, blk * 512:(blk + 1) * 512], [], [r_xa], r_xa)
            ACT(sqb, xa, AF.Square, [r_xa], [r_sqb])
            pb, r_pb = BK.next()
            for c in range(8):
                MM(pb[:, :], ones_b[:, :], sqb[:, c, :], c == 0, c == 7, [r_sqb, r_ones], [r_pb])
            ACT(rt1, pb[:, :], AF.Ln, [r_pb, r_small], [r_rt1], scale=1.0 / D, bias=c_eps)
            ACT(rt1, rt1, AF.Exp, [r_rt1], [r_rt1], scale=-0.5)
            for c in range(8):
                STT("dve", hnT[:, c, blk * 512:(blk + 1) * 512], xa[:, c, :], vecs[:, V_N1 + c:V_N1 + c + 1],
                    rt1, ALU.mult, ALU.mult, [r_xa, r_rt1, r_vecs], [r_hn])
        dbg_dump("hnT", s, hnT, [r_hn], True)

        if stop_after < 2:
            continue
        S.barrier()
        AR.off = mark_mixer
        BKg = Banks(); BKg.set([0, 1, 2, 3])
        BKd = Banks(); BKd.set([6, 7])
        ytl = [AR.alloc("ytl%d" % i, [512], BF16) for i in range(4)]
        r_scr = getres("yscr")

        def gdn_stream():
            wgb, r_wgb = AR.alloc("wgb", [8, 16], BF16)
            DMA("pool", wgb, wcols(w_in, OFF_GB, 16), [], [r_wgb], r_wgb)
            beta, r_beta = AR.alloc("beta", [8, NCH], F32, parts=64)
            gtab, r_g = AR.alloc("gtab", [8, NCH], F32, parts=64)
            gcol, r_gcol = AR.alloc("gcol", [8, NCH], F32, parts=64)
            negegc, r_negegc = AR.alloc("negegc", [8, NCH], F32, parts=64)
            kdecay, r_kdecay = AR.alloc("kdecay", [8, NCH], F32, parts=64)
            egl, r_egl = AR.alloc("egl", [8, NCH], F32)
            sptmp, r_sptmp = AR.alloc("sptmp", [NCH, 8], F32, parts=64)
            pb, r_pb = BKg.next()
            for n in range(NCH):
                for c in range(8):
                    MM(pb[0:64, n * 16:(n + 1) * 16], hnT[:, c, n * 64:(n + 1) * 64], wgb[:, c, :], c == 0, c == 7,
                       [r_hn, r_wgb], [r_pb])
            pv = pb[0:64, 0:NCH * 16].rearrange("p (n k) -> p n k", k=16)
            ACT(beta.rearrange("p h n -> p n h"), pv[:, :, 0:8], AF.Sigmoid, [r_pb], [r_beta])
            TT("dve", sptmp, pv[:, :, 8:16], vecs[0:64, V_DTB:V_DTB + 8].unsqueeze(1).to_broadcast([64, NCH, 8]),
               ALU.add, [r_pb, r_vecs], [r_sptmp])
            ACT(sptmp, sptmp, AF.Exp, [r_sptmp], [r_sptmp])
            ACT(sptmp, sptmp, AF.Ln, [r_sptmp, r_small], [r_sptmp], bias=c_one[0:64, :], scale=1.0)
            TT("dve", gtab.rearrange("p h n -> p n h"), sptmp, c_nA[0:64, :].unsqueeze(1).to_broadcast([64, NCH, 8]),
               ALU.mult, [r_sptmp, r_small], [r_g])
            gflat = gtab.rearrange("p h n -> p (h n)")
            pb, r_pb = BKg.next()
            MM(pb[0:64, 0:8 * NCH], triU, gflat, True, True, [r_g, r_cF], [r_pb])
            CP("dve", gcol.rearrange("p h n -> p (h n)"), pb[0:64, 0:8 * NCH], [r_pb], [r_gcol])
            ACT(negegc.rearrange("p h n -> p (h n)"), pb[0:64, 0:8 * NCH], AF.Exp, [r_pb], [r_negegc])
            TS("dve", negegc.rearrange("p h n -> p (h n)"), negegc.rearrange("p h n -> p (h n)"), -1.0, ALU.mult,
               [r_negegc], [r_negegc])
            pb2, r_pb2 = BKg.next()
            MM(pb2[:, 0:8 * NCH], ones_f[0:64, :], gflat, True, True, [r_g, r_ones], [r_pb2])
            ACT(egl.rearrange("p h n -> p (h n)"), pb2[:, 0:8 * NCH], AF.Exp, [r_pb2], [r_egl])
            TT("dve", kdecay.rearrange("p h n -> p (h n)"), pb2[0:64, 0:8 * NCH], gcol.rearrange("p h n -> p (h n)"),
               ALU.subtract, [r_pb2, r_gcol], [r_kdecay])
            ACT(kdecay.rearrange("p h n -> p (h n)"), kdecay.rearrange("p h n -> p (h n)"), AF.Exp, [r_kdecay], [r_kdecay])

            ub = [AR.alloc("ub%d" % i, [T + 4], BF16) for i in range(2)]
            for u_ap, r_u in ub:
                MSET("pool", u_ap[:, 0:4], 0.0, [r_u])
            qT, r_qT = AR.alloc("qT", [T], BF16)
            kT, r_kT = AR.alloc("kT", [T], BF16)
            vTf, r_vTf = AR.alloc("vTf", [T], BF16)
            zs, r_zs = AR.alloc("zs", [T], BF16)
            qdT, r_qdT = AR.alloc("qdT", [T], BF16)
            tmpq, r_tmpq = AR.alloc("tmpq", [T], BF16)
            r_qdTg = [getres("qdT_g%d" % g) for g in range(NG)]
            sq2 = [AR.alloc("sq2%d" % i, [512], BF16) for i in range(2)]
            rs2 = [AR.alloc("rs2%d" % i, [512], F32) for i in range(2)]
            Kd, r_Kd = AR.alloc("Kd", [NCH, 128], BF16, parts=64)
            Vt, r_Vt = AR.alloc("Vt", [NCH, 128], BF16, parts=64)
            gA, r_gA = AR.alloc("gA", [8, 64], F32, parts=64)
            decT, r_decT = AR.alloc("decT", [8, 64], F32, parts=64)
            nbs, r_nbs = AR.alloc("nbs", [8, 64], F32, parts=64)
            gB, r_gB = AR.alloc("gB", [8, 64], F32, parts=64)
            egcb, r_egcb = AR.alloc("egcb", [512], F32)
            Pb = [AR.alloc("P%d" % i, [8, 64], BF16, parts=64) for i in range(4)]
            Xb = [AR.alloc("X%d" % i, [8, 64], BF16, parts=64) for i in range(2)]
            Xall, r_Xall = AR.alloc("Xall", [NCH, 64], BF16, parts=64)
            intraT, r_intraT = AR.alloc("intraT", [NCH, 64], BF16, parts=64)
            oT, r_oT = AR.alloc("oT", [T], BF16)
            r_Xg = [getres("Xall_g%d" % g) for g in range(NG)]
            r_iTg = [getres("intraT_g%d" % g) for g in range(NG)]
            Sf, r_Sf = AR.alloc("Sf", [128], F32)
            Sb, r_Sb = AR.alloc("Sb", [128], BF16)
            Rb = [AR.alloc("R%d" % i, [128], BF16, parts=64) for i in range(2)]
            vnb = [AR.alloc("vn%d" % i, [128], BF16, parts=64) for i in range(2)]
            dg = [AR.alloc("dg%d" % i, [4, 128], BF16) for i in range(2)]
            ytmp = [AR.alloc("ytmp%d" % i, [512], F32) for i in range(1)] * 2
            specs = []
            for h in range(NH):
                for off in (OFF_GQ, OFF_GK, OFF_GV, OFF_GZ):
                    specs.append(wcols(w_in, off + h * 128))
            WS = WStream("wg", 3, 8, specs)
            dgi = 0
            for h in range(NH):
                for ti, (dst, r_dst) in enumerate(((tmpq, r_tmpq), (kT, r_kT), (vTf, r_vTf))):
                    wt, r_wt = WS.get(h * 4 + ti)
                    u_ap, r_u = ub[ti % 2]
                    for blk in range(NB):
                        pb, r_pb = BKg.next()
                        for c in range(8):
                            MM(pb[:, :], wt[:, c, :], hnT[:, c, blk * 512:(blk + 1) * 512], c == 0, c == 7,
                               [r_wt, r_hn], [r_pb])
                        CP("act" if blk % 2 == 0 else "dve", u_ap[:, 4 + blk * 512:4 + (blk + 1) * 512], pb[:, :], [r_pb], [r_u])
                    dga, r_dga = dg[dgi % 2]; dgi += 1
                    tile = ti * 8 + h
                    for j in range(4):
                        TS("dve", dga[:, j, :], ident, vecs[:, V_GCW + tile * 4 + j:V_GCW + tile * 4 + j + 1], ALU.mult,
                           [r_cB, r_vecs], [r_dga])
                    for blk in range(NB):
                        pb, r_pb = BKg.next()
                        for j in range(4):
                            MM(pb[:, :], dga[:, j, :], u_ap[:, 1 + j + blk * 512:1 + j + (blk + 1) * 512], j == 0, j == 3,
                               [r_dga, r_u], [r_pb])
                        ACT(dst[:, blk * 512:(blk + 1) * 512], pb[:, :], AF.Silu, [r_pb], [r_dst])
                wt, r_wt = WS.get(h * 4 + 3)
                for blk in range(NB):
                    pb, r_pb = BKg.next()
                    for c in range(8):
                        MM(pb[:, :], wt[:, c, :], hnT[:, c, blk * 512:(blk + 1) * 512], c == 0, c == 7, [r_wt, r_hn], [r_pb])
                    ACT(zs[:, blk * 512:(blk + 1) * 512], pb[:, :], AF.Silu, [r_pb], [r_zs])
                ii = 0
                for (src, r_src, dst, r_dst, scl) in ((tmpq, r_tmpq, qT, r_qT, 128 ** -0.5), (kT, r_kT, kT, r_kT, 1.0)):
                    for blk in range(NB):
                        sl = slice(blk * 512, (blk + 1) * 512)
                        sqa, r_sqa = sq2[ii % 2]; rsa, r_rsa = rs2[ii % 2]; ii += 1
                        TT("pool", sqa, src[:, sl], src[:, sl], ALU.mult, [r_src], [r_sqa])
                        pb, r_pb = BKg.next()
                        MM(pb[:, :], ones_b[:, :], sqa, True, True, [r_sqa, r_ones], [r_pb])
                        ACT(rsa, pb[:, :], AF.Ln, [r_pb, r_small], [r_rsa], scale=1.0, bias=c_eps)
                        ACT(rsa, rsa, AF.Exp, [r_rsa], [r_rsa], scale=-0.5)
                        STT("dve", dst[:, sl], src[:, sl], scl, rsa, ALU.mult, ALU.mult, [r_src, r_rsa], [r_dst])
                for (src, r_src, dst, r_dst, isk) in ((kT, r_kT, Kd, r_Kd, True), (vTf, r_vTf, Vt, r_Vt, False)):
                    for g in range(NG):
                        pb, r_pb = BKg.next()
                        pbb = pb[:].bitcast(BF16)
                        for j in range(8):
                            n = g * 8 + j
                            TR(pbb[0:64, j * 128:(j + 1) * 128], src[:, n * 64:(n + 1) * 64], ident, [r_src, r_cB], [r_pb])
                        pv3 = pbb[0:64, 0:1024].rearrange("p (a b) -> p a b", a=8)
                        if isk:
                            TT("dve", dst[:, g * 8:(g + 1) * 8, :], pv3,
                               kdecay[:, h, g * 8:(g + 1) * 8].unsqueeze(2).to_broadcast([64, 8, 128]), ALU.mult,
                               [r_pb, r_kdecay], [r_dst])
                        else:
                            CP("act", dst[:, g * 8:(g + 1) * 8, :], pv3, [r_pb], [r_dst])
                def inv_group(g, h=h):
                    ns = slice(g * 8, (g + 1) * 8)
                    cs = slice(g * 512, (g + 1) * 512)
                    TT("dve", gA, triU.unsqueeze(1).to_broadcast([64, 8, 64]),
                       gtab[:, h, ns].unsqueeze(2).to_broadcast([64, 8, 64]), ALU.mult, [r_cF, r_g], [r_gA])
                    pbc, r_pbc = BKg.next()
                    MM(pbc[:, :], ones_f[0:64, :], gA.rearrange("p a b -> p (a b)"), True, True, [r_gA, r_ones], [r_pbc])
                    pbc3 = pbc[0:64, :].rearrange("p (a b) -> p a b", a=8)
                    ACT(egcb, pbc[:, :], AF.Exp, [r_pbc], [r_egcb])
                    TT("dve", qdT[:, cs], qT[:, cs], egcb, ALU.mult, [r_qT, r_egcb], [r_qdTg[g]])
                    TT("dve", gA, pbc3, gcol[:, h, ns].unsqueeze(2).to_broadcast([64, 8, 64]), ALU.subtract,
                       [r_pbc, r_gcol], [r_gA])
                    TT("dve", gA, gA, maskneg.unsqueeze(1).to_broadcast([64, 8, 64]), ALU.min, [r_gA, r_cF], [r_gA])
                    ACT(decT, gA, AF.Exp, [r_gA], [r_decT])
                    TT("pool", nbs, strict.unsqueeze(1).to_broadcast([64, 8, 64]),
                       beta[:, h, ns].unsqueeze(2).to_broadcast([64, 8, 64]), ALU.mult, [r_cF, r_beta], [r_nbs])
                    pkk, r_pkk = BKg.next()
                    pqk, r_pqk = BKg.next()
                    for j in range(8):
                        n = g * 8 + j
                        tsl = slice(n * 64, (n + 1) * 64)
                        MM(pkk[0:64, j * 64:(j + 1) * 64], kT[:, tsl], kT[:, tsl], True, True, [r_kT], [r_pkk])
                    for j in range(8):
                        n = g * 8 + j
                        tsl = slice(n * 64, (n + 1) * 64)
                        MM(pqk[0:64, j * 64:(j + 1) * 64], kT[:, tsl], qT[:, tsl], True, True, [r_kT, r_qT], [r_pqk])
                    pkk3 = pkk[0:64, :].rearrange("p (a b) -> p a b", a=8)
                    pqk3 = pqk[0:64, :].rearrange("p (a b) -> p a b", a=8)
                    TT("dve", gB, pkk3, decT, ALU.mult, [r_pkk, r_decT], [r_gB])
                    P1, r_P1 = Pb[0]; P1T, r_P1T = Pb[1]
                    STT("dve", P1, gB, -1.0, nbs, ALU.mult, ALU.mult, [r_gB, r_nbs], [r_P1])
                    TT("dve", intraT[:, ns, :], pqk3, decT, ALU.mult, [r_pqk, r_decT], [r_iTg[g]])
                    ptb, r_ptb = BKg.next()
                    ptbb = ptb[:].bitcast(BF16)
                    for j in range(8):
                        TR(ptbb[0:64, j * 64:(j + 1) * 64], P1[:, j, :], ident[0:64, 0:64], [r_P1, r_cB], [r_ptb])
                    CP("act", P1T, ptbb[0:64, 0:512].rearrange("p (a b) -> p a b", a=8), [r_ptb], [r_P1T])
                    Xc, r_Xc = Xb[0]
                    TT("pool", Xc, P1, ident[0:64, 0:64].unsqueeze(1).to_broadcast([64, 8, 64]), ALU.add, [r_P1, r_cB], [r_Xc])
                    cur = 0
                    xi = 0
                    for lev in range(5):
                        (Pc, r_Pc), (PTc, r_PTc) = Pb[cur * 2], Pb[cur * 2 + 1]
                        (Pn, r_Pn), (PTn, r_PTn) = Pb[(1 - cur) * 2], Pb[(1 - cur) * 2 + 1]
                        last = lev == 4
                        if not last:
                            pa, r_pa = BKg.next()
                            for j in range(8):
                                MM(pa[0:64, j * 64:(j + 1) * 64], PTc[:, j, :], Pc[:, j, :], True, True, [r_Pc, r_PTc], [r_pa])
                        pt, r_pt = BKg.next()
                        for j in range(8):
                            MM(pt[0:64, j * 64:(j + 1) * 64], Pc[:, j, :], PTc[:, j, :], True, True, [r_Pc, r_PTc], [r_pt])
                        if not last:
                            CP("act", Pn, pa[0:64, :].rearrange("p (a b) -> p a b", a=8), [r_pa], [r_Pn])
                        CP("dve", PTn, pt[0:64, :].rearrange("p (a b) -> p a b", a=8), [r_pt], [r_PTn])
                        (Xc, r_Xc), (Xn, r_Xn) = Xb[xi], Xb[1 - xi]
                        px, r_px = BKg.next()
                        for j in range(8):
                            MM(px[0:64, j * 64:(j + 1) * 64], PTn[:, j, :], Xc[:, j, :], True, True, [r_PTn, r_Xc], [r_px])
                        if last:
                            TT("dve", Xall[:, ns, :], px[0:64, :].rearrange("p (a b) -> p a b", a=8), Xc, ALU.add,
                               [r_px, r_Xc], [r_Xg[g]])
                        else:
                            TT("dve", Xn, px[0:64, :].rearrange("p (a b) -> p a b", a=8), Xc, ALU.add, [r_px, r_Xc], [r_Xn])
                        xi = 1 - xi
                        cur = 1 - cur
                def scan_steps(g, h=h):
                    for n in range(g * 8, g * 8 + 8):
                        tsl = slice(n * 64, (n + 1) * 64)
                        pks, r_pks = psum[0][0:64, 0:128], r_bank[0]
                        py, r_py = psum[0][0:64, 128:256], r_bank[0]
                        po, r_po = psum[1][:, 0:64], r_bank[1]
                        pd, r_pd = psum[1][:, 64:192], r_bank[1]
                        Ra, r_Ra = Rb[n % 2]; vna, r_vna = vnb[n % 2]
                        col = slice(n, n + 1)
                        MM(pks, kT[:, tsl], Sb, True, True, [r_kT, r_Sb], [r_pks])
                        STT("dve", Ra, pks, negegc[:, h, col], Vt[:, n, :], ALU.mult, ALU.add,
                            [r_pks, r_negegc, r_Vt], [r_Ra])
                        MM(py, Xall[:, n, :], Ra, True, True, [r_Xg[g], r_Ra], [r_py])
                        TS("dve", vna, py, beta[:, h, col], ALU.mult, [r_py, r_beta], [r_vna])
                        MM(po, Sb, qdT[:, tsl], True, False, [r_Sb, r_qdTg[g]], [r_po])
                        MM(po, vna, intraT[:, n, :], False, True, [r_vna, r_iTg[g]], [r_po])
                        MM(pd, Kd[:, n, :], vna, True, True, [r_Kd, r_vna], [r_pd])
                        STT("dve", Sb, Sf, egl[:, h, col], pd, ALU.mult, ALU.add, [r_Sf, r_egl, r_pd], [r_Sb])
                        STT("dve", Sf, Sf, egl[:, h, col], pd, ALU.mult, ALU.add, [r_Sf, r_egl, r_pd], [r_Sf])
                        CP("act", oT[:, tsl], po, [r_po], [r_oT])

                def record(f):
                    saved = CUR[0]
                    CUR[0] = []
                    f()
                    out = CUR[0]
                    CUR[0] = saved
                    return out

                def splice(A, B):
                    ca = sum(x[4] for x in A) or 1.0
                    cb = sum(x[4] for x in B) or 1.0
                    ia = ib = 0
                    fa = fb_ = 0.0
                    while ia < len(A) or ib < len(B):
                        if ib >= len(B) or (ia < len(A) and fa / ca <= fb_ / cb):
                            CUR[0].append(A[ia]); fa += A[ia][4]; ia += 1
                        else:
                            CUR[0].append(B[ib]); fb_ += B[ib][4]; ib += 1

                BKg.set([2, 3])
                inv_group(0)
                MSET("dve", Sf, 0.0, [r_Sf])
                MSET("pool", Sb, 0.0, [r_Sb])
                for g in range(NG):
                    A = record(lambda: inv_group(g + 1)) if g + 1 < NG else []
                    B = record(lambda: scan_steps(g))
                    splice(A, B)
                BKg.set([0, 1, 2, 3])
                for blk in range(NB):
                    sl = slice(blk * 512, (blk + 1) * 512)
                    sqa, r_sqa = sq2[blk % 2]; rsa, r_rsa = rs2[blk % 2]; yta, r_yta = ytmp[blk % 2]
                    ACT(sqa, oT[:, sl], AF.Square, [r_oT], [r_sqa])
                    pb, r_pb = BKg.next()
                    MM(pb[:, :], ones_b[:, :], sqa, True, True, [r_sqa, r_ones], [r_pb])
                    ACT(rsa, pb[:, :], AF.Ln, [r_pb, r_small], [r_rsa], scale=1.0 / 128, bias=c_eps)
                    ACT(rsa, rsa, AF.Exp, [r_rsa], [r_rsa], scale=-0.5)
                    STT("dve", yta, oT[:, sl], vecs[:, V_GNW:V_GNW + 1], rsa, ALU.mult, ALU.mult, [r_oT, r_rsa, r_vecs], [r_yta])
                    yt_, r_yt_ = ytl[blk % 2]
                    TT("pool", yt_, yta, zs[:, sl], ALU.mult, [r_yta, r_zs], [r_yt_])
                    DMA("sp", yg_scr[s][h * 128:(h + 1) * 128, sl], yt_, [r_yt_], [r_scr], r_yt_)

        def diff_stream():
            qz = [AR.alloc("dqz%d" % i, [T], BF16) for i in range(2)]
            MSET("pool", qz[0][0][64:128, :], 0.0, [qz[0][1]])
            MSET("pool", qz[1][0][0:64, :], 0.0, [qz[1][1]])
            qT, r_qT = None, None
            kT, r_kT = AR.alloc("dkT", [T], BF16)
            Vk, r_Vk = AR.alloc("Vk", [NTB, 128], BF16)
            sq3 = [AR.alloc("sq3%d" % i, [512], BF16) for i in range(2)]
            rs3 = [AR.alloc("rs3%d" % i, [512], F32) for i in range(2)]
            qn = [AR.alloc("qn%d" % i, [512], BF16) for i in range(2)]
            t1b = [AR.alloc("t1%d" % i, [512], F32) for i in range(2)]
            t2b = [AR.alloc("t2%d" % i, [512], F32) for i in range(2)]
            pTb = [AR.alloc("pT%d" % i, [512], BF16) for i in range(3)]
            rcp = [AR.alloc("rcp%d" % i, [512], F32) for i in range(1)] * 2
            a12 = [AR.alloc("a12%d" % i, [512], F32) for i in range(2)]
            dTt, r_dT = AR.alloc("dTt", [512], F32)
            specs = []
            for h in range(NH):
                for off in (OFF_DQ, OFF_DK, OFF_DV):
                    specs.append(wcols(w_in, off + h * 128))
            WS = WStream("wd", 3, 8, specs)
            ii = 0
            pti = 0
            for h in range(NH):
                BKd.set([4, 5, 6, 7])
                for ti, (dst, r_dst, vcol) in enumerate(((qT, r_qT, V_QNW), (kT, r_kT, V_KNW))):
                    wt, r_wt = WS.get(h * 3 + ti)
                    for blk in range(NB):
                        sl = slice(blk * 512, (blk + 1) * 512)
                        sqa, r_sqa = sq3[ii % 2]; rsa, r_rsa = rs3[ii % 2]; qna, r_qna = qn[ii % 2]
                        t1, r_t1 = t1b[ii % 2]; t2, r_t2 = t2b[ii % 2]; ii += 1
                        pu, r_pu = BKd.next()
                        for c in range(8):
                            MM(pu[:, :], wt[:, c, :], hnT[:, c, sl], c == 0, c == 7, [r_wt, r_hn], [r_pu])
                        ACT(sqa, pu[:, :], AF.Square, [r_pu], [r_sqa])
                        pss, r_pss = BKd.next()
                        MM(pss[:, :], blockones, sqa, True, True, [r_sqa, r_cB], [r_pss])
                        ACT(rsa, pss[:, :], AF.Ln, [r_pss, r_small], [r_rsa], scale=1.0 / 64, bias=c_eps)
                        ACT(rsa, rsa, AF.Exp, [r_rsa], [r_rsa], scale=-0.5)
                        STT("dve", qna, pu[:, :], vecs[:, vcol:vcol + 1], rsa, ALU.mult, ALU.mult, [r_pu, r_rsa, r_vecs], [r_qna])
                        pr, r_pr = BKd.next()
                        MM(pr[:, :], ropeP, qna, True, True, [r_qna, r_cB], [r_pr])
                        TT("dve", t1, pr[:, :], ropeS[:, sl], ALU.mult, [r_pr, r_cB], [r_t1])
                        TT("pool", t2, qna, ropeC[:, sl], ALU.mult, [r_qna, r_cB], [r_t2])
                        if ti == 0:
                            TT("pool", qz[0][0][0:64, sl], t1[0:64, :], t2[0:64, :], ALU.add, [r_t1, r_t2], [qz[0][1]])
                            TT("pool", qz[1][0][64:128, sl], t1[64:128, :], t2[64:128, :], ALU.add, [r_t1, r_t2], [qz[1][1]])
                        else:
                            TT("pool", dst[:, sl], t1, t2, ALU.add, [r_t1, r_t2], [r_dst])
                wt, r_wt = WS.get(h * 3 + 2)
                for tb4 in range(NTB // 4):
                    pv_, r_pv = BKd.next()
                    for q4 in range(4):
                        tb = tb4 * 4 + q4
                        for c in range(8):
                            MM(pv_[:, q4 * 128:(q4 + 1) * 128], hnT[:, c, tb * 128:(tb + 1) * 128], wt[:, c, :], c == 0, c == 7,
                               [r_hn, r_wt], [r_pv])
                    CP("act" if tb4 % 2 == 0 else "dve", Vk[:, tb4 * 4:(tb4 + 1) * 4, :],
                       pv_[:, :].rearrange("p (a b) -> p a b", a=4), [r_pv], [r_Vk])
                BKd.set([6, 7])
                for qb in range(NB):
                    sl = slice(qb * 512, (qb + 1) * 512)
                    for m in range(2):
                        pO, r_pO = psum[4], r_bank[4]
                        pL, r_pL = psum[5], r_bank[5]
                        nkb = (qb + 1) * 4
                        ms = slice(64 * m, 64 * m + 64)
                        def stage1(kb):
                            nonlocal pti
                            j = kb - qb * 4
                            c0 = j * 128 if j >= 0 else 0
                            psc, r_psc = BKd.next()
                            pT, r_pT = pTb[pti % 3]; pti += 1
                            MM(psc[:, c0:512], kT[:, kb * 128:(kb + 1) * 128], qz[m][0][:, qb * 512 + c0:(qb + 1) * 512], True, True,
                               [r_kT, qz[m][1]], [r_psc])
                            ACT(pT[:, c0:512], psc[:, c0:512], AF.Exp, [r_psc], [r_pT], scale=0.125)
                            if j >= 0:
                                TT("pool", pT[:, c0:c0 + 128], pT[:, c0:c0 + 128], attmask, ALU.mult, [r_pT, r_cB], [r_pT])
                            return pT, r_pT, c0

                        cur_st = stage1(0)
                        for kb in range(nkb):
                            nxt_st = stage1(kb + 1) if kb + 1 < nkb else None
                            pT, r_pT, c0 = cur_st
                            MM(pO[:, c0:512], Vk[:, kb, :], pT[:, c0:512], kb == 0, kb == nkb - 1, [r_Vk, r_pT], [r_pO])
                            MM(pL[:, c0:512], ones_b[:, :], pT[:, c0:512], kb == 0, kb == nkb - 1, [r_ones, r_pT], [r_pL])
                            cur_st = nxt_st
                        rc, r_rc = rcp[m]; aa, r_aa = a12[m]
                        ACT(rc, pL[:, :], AF.Ln, [r_pL], [r_rc])
                        ACT(rc, rc, AF.Exp, [r_rc], [r_rc], scale=-1.0)
                        TT("dve", aa, pO[:, :], rc, ALU.mult, [r_pO, r_rc], [r_aa])
                    STT("dve", dTt, a12[1][0], c_nlam, a12[0][0], ALU.mult, ALU.add, [a12[1][1], a12[0][1], r_small], [r_dT])
                    sqa, r_sqa = sq3[ii % 2]; rsa, r_rsa = rs3[ii % 2]; ii += 1
                    ACT(sqa, dTt, AF.Square, [r_dT], [r_sqa])
                    pss, r_pss = BKd.next()
                    MM(pss[:, :], ones_b[:, :], sqa, True, True, [r_sqa, r_ones], [r_pss])
                    ACT(rsa, pss[:, :], AF.Ln, [r_pss, r_small], [r_rsa], scale=1.0 / 128, bias=c_eps)
                    ACT(rsa, rsa, AF.Exp, [r_rsa], [r_rsa], scale=-0.5)
                    yt_, r_yt_ = ytl[2 + qb % 2]
                    STT("dve", yt_, dTt, c_slw8, rsa, ALU.mult, ALU.mult, [r_dT, r_rsa, r_small], [r_yt_])
                    DMA("sp", yd_scr[s][h * 128:(h + 1) * 128, sl], yt_, [r_yt_], [r_scr], r_yt_)

        run_streams([gdn_stream, diff_stream])

        if stop_after < 4:
            continue
        S.barrier()
        BK.set(range(8))
        AR.off = mark_mixer
        ygT, r_yg = AR.alloc("ygT", [8, T], BF16)
        ydT, r_yd = AR.alloc("ydT", [8, T], BF16)
        for c in range(8):
            DMA("sp", ygT[:, c, :], yg_scr[s][c * 128:(c + 1) * 128, :], [r_scr], [r_yg], r_yg)
            DMA("sp", ydT[:, c, :], yd_scr[s][c * 128:(c + 1) * 128, :], [r_scr], [r_yd], r_yd)
        dbg_dump("ygT", s, ygT, [r_yg], True)
        dbg_dump("ydT", s, ydT, [r_yd], True)
        mgT, r_mg = AR.alloc("mgT", [8, T], BF16)
        gsb = [AR.alloc("gs%d" % i, [512], F32) for i in range(4)]
        m12 = [AR.alloc("m12%d" % i, [512], F32) for i in range(4)]
        specs = []
        for m in range(8):
            specs += [wcols(w_in, OFF_GATE + m * 128), wcols(w_in, OFF_GATE + D + m * 128),
                      wcols(w_go, m * 128), wcols(w_do, m * 128)]
        WS = WStream("wm", 8, 8, specs, hold=4)
        ii = 0
        for m in range(8):
            wts = [WS.get(m * 4 + k) for k in range(4)]
            for blk in range(NB):
                sl = slice(blk * 512, (blk + 1) * 512)
                mm_ = []
                for k in range(2):
                    (wg_, r_wg_), (wy_, r_wy_) = wts[k], wts[2 + k]
                    ysrc, r_ysrc = ((ygT, r_yg), (ydT, r_yd))[k]
                    ga, r_ga = gsb[ii % 4]; ma, r_ma = m12[ii % 4]; ii += 1
                    pg, r_pg = BK.next()
                    for c in range(8):
                        MM(pg[:, :], wg_[:, c, :], hnT[:, c, sl], c == 0, c == 7, [r_wg_, r_hn], [r_pg])
                    ACT(ga, pg[:, :], AF.Sigmoid, [r_pg, r_vecs], [r_ga], bias=vecs[:, V_BG + k * 8 + m:V_BG + k * 8 + m + 1], scale=1.0)
                    py_, r_py_ = BK.next()
                    for c in range(8):
                        MM(py_[:, :], wy_[:, c, :], ysrc[:, c, sl], c == 0, c == 7, [r_wy_, r_ysrc], [r_py_])
                    TT("dve", ma, py_[:, :], ga, ALU.mult, [r_py_, r_ga], [r_ma])
                    mm_.append((ma, r_ma))
                TT("pool", mgT[:, m, sl], mm_[0][0], mm_[1][0], ALU.add, [mm_[0][1], mm_[1][1]], [r_mg])
        dbg_dump("mgT", s, mgT, [r_mg], True)

        S.barrier()
        AR.off = 0
        x1T = AR.alloc("x1T", [8, T], F32)[0]
        r_x1 = [getres("x1_%d" % c) for c in range(8)]
        assert AR.off <= mark_mixer + 2 * 8 * T * 2
        AR.off = mark_mixer + 3 * 8 * T * 2
        specs = [wcols(w_o, m * 128) for m in range(8)]
        WS = WStream("wo", 3, 8, specs)
        for m in range(8):
            DMA("sp", x1T[:, m, :], xsrc[:, m, :], [], [r_x1[m]], r_x1[m])
        for m in range(8):
            wt, r_wt = WS.get(m)
            for blk in range(NB):
                sl = slice(blk * 512, (blk + 1) * 512)
                pb, r_pb = BK.next()
                for c in range(8):
                    MM(pb[:, :], wt[:, c, :], mgT[:, c, sl], c == 0, c == 7, [r_wt, r_mg], [r_pb])
                TT("dve", x1T[:, m, sl], x1T[:, m, sl], pb[:, :], ALU.add, [r_x1[m], r_pb], [r_x1[m]])
        if dbg:
            dst = dbg_t["x1T"][s].rearrange("(c p) t -> p c t", p=128)
            rr = getres("dbg_x1")
            for c in range(8):
                DMA("sp", dst[:, c, :], x1T[:, c, :], [r_x1[c]], [rr], rr)
            if rr not in out_res:
                out_res.append(rr)

        if stop_after < 5:
            continue
        S.barrier()
        AR.off = 8 * T * 4
        h2T, r_h2 = AR.alloc("h2T", [8, T], BF16)
        sqb, r_sqb = AR.alloc("sq5", [8, 512], BF16)
        rt5, r_rt5 = AR.alloc("rt5", [512], F32)
        for blk in range(NB):
            sl = slice(blk * 512, (blk + 1) * 512)
            ACT(sqb, x1T[:, :, sl], AF.Square, r_x1, [r_sqb])
            pb, r_pb = BK.next()
            for c in range(8):
                MM(pb[:, :], ones_b[:, :], sqb[:, c, :], c == 0, c == 7, [r_sqb, r_ones], [r_pb])
            ACT(rt5, pb[:, :], AF.Ln, [r_pb, r_small], [r_rt5], scale=1.0 / D, bias=c_eps)
            ACT(rt5, rt5, AF.Exp, [r_rt5], [r_rt5], scale=-0.5)
            for c in range(8):
                STT("dve", h2T[:, c, sl], x1T[:, c, sl], vecs[:, V_N2 + c:V_N2 + c + 1], rt5, ALU.mult, ALU.mult,
                    [r_x1[c], r_rt5, r_vecs], [r_h2])
        S.barrier()
        AR.off = 8 * T * 4 + 8 * T * 2
        actT = AR.alloc("actT", [22, FB], BF16)[0]
        r_act = [getres("act0"), getres("act1")]
        stg = [AR.alloc("stg%d" % i, [FB + 4], BF16) for i in range(4)]
        sgb = [AR.alloc("sg%d" % i, [512], BF16) for i in range(4)]
        dgf = [AR.alloc("dgf%d" % i, [3, 128], BF16) for i in range(4)]
        r_out = getres("out")
        if r_out not in out_res:
            out_res.append(r_out)
        odst = outT[s].rearrange("(c p) t -> p c t", p=128)
        mark_w = AR.off
        for fb in range(NFB):
            t0 = fb * FB
            AR.off = mark_w
            WD = WStream("wdn", 2, 22, [wcols(w_dn, m * 128) for m in range(8)])

            def up_stream(par):
                jl = list(range(par, 22, 2))
                specs = []
                for j in jl:
                    specs += [wcols(w_up, j * 128), wcols(w_up, D_FF + j * 128)]
                WSx = WStream("wu%d" % par, 3, 8, specs)
                bk = Banks(); bk.set([0, 1, 2, 3] if par == 0 else [4, 5, 6, 7])
                si = 0
                for ji, j in enumerate(jl):
                    for k in range(2):
                        wt, r_wt = WSx.get(ji * 2 + k)
                        st, r_st = stg[par * 2 + si % 2]; dga, r_dga = dgf[par * 2 + si % 2]; si += 1
                        tile = k * 22 + j
                        for tap in range(3):
                            TS("dve", dga[:, tap, :], ident, vecs[:, V_FCW + tile * 3 + tap:V_FCW + tile * 3 + tap + 1],
                               ALU.mult, [r_cB, r_vecs], [r_dga])
                        if fb == 0:
                            MSET("pool", st[:, 0:4], 0.0, [r_st])
                        else:
                            ph, r_ph = bk.next()
                            for c in range(8):
                                MM(ph[:, 0:2], wt[:, c, :], h2T[:, c, t0 - 2:t0], c == 0, c == 7, [r_wt, r_h2], [r_ph])
                            CP("dve", st[:, 2:4], ph[:, 0:2], [r_ph], [r_st])
                        for sub in range(FSUB):
                            pb, r_pb = bk.next()
                            for c in range(8):
                                MM(pb[:, :], wt[:, c, :], h2T[:, c, t0 + sub * 512:t0 + (sub + 1) * 512], c == 0, c == 7,
                                   [r_wt, r_h2], [r_pb])
                            CP("act" if (sub + k) % 2 == 0 else "dve", st[:, 4 + sub * 512:4 + (sub + 1) * 512], pb[:, :], [r_pb], [r_st])
                        for sub in range(FSUB):
                            pb, r_pb = bk.next()
                            for tap in range(3):
                                MM(pb[:, :], dga[:, tap, :], st[:, 2 + tap + sub * 512:2 + tap + (sub + 1) * 512], tap == 0, tap == 2,
                                   [r_dga, r_st], [r_pb])
                            sg_, r_sg_ = sgb[par * 2 + sub % 2]
                            if k == 0:
                                ACT(sg_, pb[:, :], AF.Silu, [r_pb, r_vecs], [r_sg_], bias=vecs[:, V_FCB + tile:V_FCB + tile + 1], scale=1.0)
                            else:
                                STT("dve", actT[:, j, sub * 512:(sub + 1) * 512], pb[:, :], vecs[:, V_FCB + tile:V_FCB + tile + 1], sg_,
                                    ALU.add, ALU.mult, [r_pb, r_sg_, r_vecs], [r_act[par]])

            run_streams([lambda: up_stream(0), lambda: up_stream(1)])
            for m in range(8):
                wt, r_wt = WD.get(m)
                for sub in range(FSUB):
                    sl = slice(t0 + sub * 512, t0 + (sub + 1) * 512)
                    pb, r_pb = BK.next()
                    for c in range(22):
                        MM(pb[:, :], wt[:, c, :], actT[:, c, sub * 512:(sub + 1) * 512], c == 0, c == 21, [r_wt, r_act[c % 2]], [r_pb])
                    TT("dve", x1T[:, m, sl], x1T[:, m, sl], pb[:, :], ALU.add, [r_x1[m], r_pb], [r_x1[m]])
                DMA("sp", odst[:, m, t0:t0 + FB], x1T[:, m, t0:t0 + FB], [r_x1[m]], [r_out], r_out)

    S.emit(es, final_res=out_res)
    es.close()
    return nc, S


def _consts(T):
    cF = np.zeros((128, 192), np.float32)
    cR = np.zeros((128, 2 * T), np.float32)
    k = np.arange(64)
    cF[:64, 0:64] = (k[:, None] <= k[None, :]).astype(np.float32)
    cF[:64, 64:128] = np.where(k[None, :] >= k[:, None], 0.0, -1e30).astype(np.float32)
    cF[:64, 128:192] = (k[:, None] < k[None, :]).astype(np.float32)
    pos = np.arange(T, dtype=np.float32)
    inv_freq = (np.float32(500000.0) ** (-(np.arange(0, 16, 2, dtype=np.float32)) / np.float32(16))).astype(np.float32)
    ang = (pos[:, None] * inv_freq[None, :]).astype(np.float32)
    cos = np.cos(ang.astype(np.float64)).astype(np.float32).T
    sin = np.sin(ang.astype(np.float64)).astype(np.float32).T
    C = np.ones((128, T), np.float32)
    Sg = np.zeros((128, T), np.float32)
    P = np.zeros((128, 128), np.float32)
    for base in (0, 64):
        C[base:base + 8] = cos; C[base + 8:base + 16] = cos
        Sg[base:base + 8] = -sin; Sg[base + 8:base + 16] = sin
        for r in range(8):
            P[base + r, base + r + 8] = 1.0
            P[base + r + 8, base + r] = 1.0
    cR[:, 0:T] = C
    cR[:, T:] = Sg
    cB = np.zeros((128, 512), np.float32)
    cB[:, 0:128] = np.eye(128, dtype=np.float32)
    cB[:64, 128:192] = 1.0
    cB[64:, 192:256] = 1.0
    cB[:, 256:384] = P
    p = np.arange(128)
    cB[:, 384:512] = (p[:, None] <= p[None, :]).astype(np.float32)
    return cF, cB, cR


def _vecs(inp):
    v = np.zeros((128, NV), np.float32)
    g = lambda k: np.asarray(inp[k], np.float32)[0]
    v[:, V_N1:V_N1 + 8] = g("norm1_w").reshape(8, 128).T
    v[:, V_N2:V_N2 + 8] = g("norm2_w").reshape(8, 128).T
    v[:, V_BG:V_BG + 16] = g("b_gate").reshape(16, 128).T
    v[:, V_GCW:V_GCW + 96] = g("gdn_conv_w").reshape(4, 24, 128).transpose(2, 1, 0).reshape(128, 96)
    v[:, V_FCW:V_FCW + 132] = g("ffn_conv_w").reshape(3, 44, 128).transpose(2, 1, 0).reshape(128, 132)
    v[:, V_FCB:V_FCB + 44] = g("ffn_conv_b").reshape(44, 128).T
    v[:, V_GNW] = g("gdn_norm_w")
    v[:, V_SLW] = g("diff_subln_w")
    v[:, V_QNW] = np.tile(g("diff_q_norm_w"), 2)
    v[:, V_KNW] = np.tile(g("diff_k_norm_w"), 2)
    v[:, V_ALOG:V_ALOG + 8] = g("gdn_A_log")[None, :]
    v[:, V_DTB:V_DTB + 8] = g("gdn_dt_bias")[None, :]
    v[:64, V_LAM + 0] = g("lambda_q1"); v[:64, V_LAM + 1] = g("lambda_k1")
    v[:64, V_LAM + 2] = g("lambda_q2"); v[:64, V_LAM + 3] = g("lambda_k2")
    return v


_CACHE = {}


def run(inputs, T, B, FB, dbg=False, ncores=8):
    NSEQ = B // ncores
    key = (T, NSEQ, FB, dbg)
    if key not in _CACHE:
        _CACHE[key] = build_program(T, NSEQ, FB, dbg)
    nc, S = _CACHE[key]
    x = np.asarray(inputs["x"], np.float32)
    cF, cB, cR = _consts(T)
    vecs = _vecs(inputs)
    shared = {
        "w_in": np.ascontiguousarray(np.asarray(inputs["w_in"], np.float32)[0]),
        "w_gdn_out": np.ascontiguousarray(np.asarray(inputs["w_gdn_out"], np.float32)[0]),
        "w_diff_out": np.ascontiguousarray(np.asarray(inputs["w_diff_out"], np.float32)[0]),
        "w_o": np.ascontiguousarray(np.asarray(inputs["w_o"], np.float32)[0]),
        "w_up": np.ascontiguousarray(np.asarray(inputs["w_up"], np.float32)[0]),
        "w_down": np.ascontiguousarray(np.asarray(inputs["w_down"], np.float32)[0]),
        "vecs": vecs, "cF": cF, "cB": cB, "cR": cR,
    }
    in_maps = []
    for c in range(ncores):
        m = dict(shared)
        m["xT"] = np.ascontiguousarray(x[c * NSEQ:(c + 1) * NSEQ].transpose(0, 2, 1))
        in_maps.append(m)
    res = run_bass_kernel_spmd(nc, in_maps, core_ids=list(range(ncores)))
    out = np.concatenate([r["outT"].transpose(0, 2, 1) for r in res.results], axis=0)
    if dbg:
        extra = {}
        for nm in ("ygT", "ydT", "x1T", "hnT", "mgT"):
            extra[nm] = np.concatenate([r["dbg_" + nm].transpose(0, 2, 1) for r in res.results], axis=0)
        extra["small"] = res.results[0]["dbg_small"]
        return np.ascontiguousarray(out), extra
    return np.ascontiguousarray(out)


def kernel(**inputs):
    return run(inputs, T=2048, B=16, FB=1024)
```

```python
import contextlib
import math
import numpy as np
import concourse.bass as bass
import concourse.mybir as mybir
from concourse.bass_utils import run_bass_kernel_spmd

F32 = mybir.dt.float32
BF16 = mybir.dt.bfloat16
AF = mybir.ActivationFunctionType
ALU = mybir.AluOpType

D = 1024
D_IN = 9232
D_FF = 2816
NH = 8
EPS = 1e-6
OFF_GQ, OFF_GK, OFF_GV, OFF_GZ, OFF_GB, OFF_GA = 0, 1024, 2048, 3072, 4096, 4104
OFF_DQ, OFF_DK, OFF_DV, OFF_GATE = 4112, 5136, 6160, 7184
LAMBDA_INIT = 0.8 - 0.6 * math.exp(0.0)
ARENA_BYTES = 188 * 1024
V_N1, V_N2, V_BG, V_GCW, V_FCW, V_FCB = 0, 8, 16, 32, 128, 260
V_GNW, V_SLW, V_QNW, V_KNW, V_ALOG, V_DTB, V_LAM = 304, 305, 306, 307, 308, 316, 324
NV = 328


class Res:
    __slots__ = ("name", "last_w", "readers", "sem", "cnt", "excl")

    def __init__(self, name, excl=False):
        self.name = name
        self.excl = excl
        self.last_w = None
        self.readers = {}
        self.sem = None
        self.cnt = 0


class Op:
    __slots__ = ("eng", "fn", "deps", "dma", "sig", "sig_cnt", "marked", "tick", "idx", "pre_drain", "t_end", "cost")


class Sched:
    ENGS = ("pe", "act", "dve", "pool", "sp")
    SAME_WIN = 3
    LAT = 100.0

    def __init__(self, nc):
        self.nc = nc
        self.ops = {e: [] for e in self.ENGS}
        self.last_dma = {}
        self.free = {e: 0.0 for e in self.ENGS}

    def _ready(self, eng, reads, writes):
        t = self.free[eng]
        for r in reads:
            d = r.last_w
            if d is not None:
                t = max(t, d.t_end + (0.0 if (d.eng == eng and not d.dma) else self.LAT))
        for r in writes:
            d = r.last_w
            if d is not None:
                t = max(t, d.t_end + (0.0 if (d.eng == eng and not d.dma) else self.LAT))
            for d in r.readers.values():
                t = max(t, d.t_end + (0.0 if (d.eng == eng and not d.dma) else self.LAT))
        return t

    def peek(self, eng, reads, writes):
        if any(r.excl for r in reads):
            writes = list(writes) + [r for r in reads if r.excl]
        return self._ready(eng, reads, writes)

    def op(self, eng, fn, reads=(), writes=(), dma=False, sig=None, extra=(), cost=300.0):
        o = Op()
        o.eng = eng; o.fn = fn; o.dma = dma; o.marked = False; o.tick = 0
        o.sig = None; o.sig_cnt = 0; o.pre_drain = False
        deps = list(extra)
        if any(r.excl for r in reads):
            writes = list(writes) + [r for r in reads if r.excl]
            reads = [r for r in reads if not r.excl]
        t0 = self._ready(eng, reads, writes)
        for d in extra:
            t0 = max(t0, d.t_end + self.LAT)
        if dma:
            self.free[eng] = t0 + 60.0
        else:
            self.free[eng] = t0 + cost
        o.t_end = t0 + cost
        o.cost = cost
        for r in reads:
            if r.last_w is not None:
                deps.append(r.last_w)
        for r in writes:
            if r.last_w is not None:
                deps.append(r.last_w)
            deps.extend(r.readers.values())
        o.deps = []
        seen = set()
        for d in deps:
            if id(d) in seen:
                continue
            seen.add(id(d))
            if d.dma:
                o.deps.append(d)
            elif d.eng != eng:
                d.marked = True
                o.deps.append(d)
            elif eng != "pe" and len(self.ops[eng]) - d.idx <= self.SAME_WIN and d.cost < 300.0:
                o.pre_drain = True
        o.idx = len(self.ops[eng])
        if dma:
            sig.cnt += 1
            o.sig = sig; o.sig_cnt = sig.cnt
            self.last_dma[id(sig)] = o
        for r in writes:
            r.last_w = o
            r.readers = {}
        for r in reads:
            if r.last_w is not o:
                r.readers[("d", id(sig)) if dma else eng] = o
        self.ops[eng].append(o)
        return o

    def barrier(self):
        lasts = []
        for e in ("pe", "act", "dve", "pool"):
            if self.ops[e]:
                lasts.append(self.op(e, lambda eng: eng.drain(), writes=[Res("bar")]))
        b = self.op("sp", lambda eng: eng.nop(), writes=[Res("bar")],
                    extra=lasts + list(self.last_dma.values()))
        for e in ("pe", "act", "dve", "pool"):
            self.op(e, lambda eng: eng.nop(), extra=[b])

    def emit(self, es, final_res=()):
        nc = self.nc
        esem = {e: es.enter_context(nc.semaphore("eng_" + e)) for e in self.ENGS}
        dres = {}
        for e in self.ENGS:
            for o in self.ops[e]:
                if o.dma and id(o.sig) not in dres:
                    dres[id(o.sig)] = o.sig
        for i, r in enumerate(dres.values()):
            r.sem = es.enter_context(nc.semaphore("d%d_%s" % (i, r.name)))
        self.nsem = len(dres) + len(self.ENGS)
        for e in self.ENGS:
            t = 0
            for o in self.ops[e]:
                if o.marked and not o.dma:
                    t += 1
                o.tick = t
        block = es.enter_context(nc.Block())
        engobj = {"pe": block.tensor, "act": block.scalar, "dve": block.vector,
                  "pool": block.gpsimd, "sp": block.sync}
        self.stats = {}

        def make(e):
            def body(eng):
                seen_e = {}
                seen_d = {}
                nw = 0
                nd = [0]
                for o in self.ops[e]:
                    for d in o.deps:
                        if d.dma:
                            k = id(d.sig); v = d.sig_cnt * 16
                            if seen_d.get(k, 0) >= v:
                                continue
                            seen_d[k] = v
                            eng.wait_ge(d.sig.sem, v); nw += 1
                        else:
                            if seen_e.get(d.eng, 0) >= d.tick:
                                continue
                            seen_e[d.eng] = d.tick
                            eng.wait_ge(esem[d.eng], d.tick); nw += 1
                    if o.pre_drain:
                        eng.drain(); nd[0] += 1
                    ins = o.fn(eng)
                    if o.dma:
                        ins.then_inc(o.sig.sem, 16)
                    elif o.marked:
                        ins.then_inc(esem[e], 1)
                if e == "sp":
                    for r in final_res:
                        eng.wait_ge(r.sem, r.cnt * 16)
                self.stats[e] = (len(self.ops[e]), nw, nd[0])
            return body

        for e in self.ENGS:
            engobj[e](make(e))


def build_program(T, NSEQ, FB, dbg=False, stop_after=9):
    nc = bass.Bass("TRN2", target_bir_lowering=False)
    NB = T // 512
    NCH = T // 64
    NG = NCH // 8
    NTB = T // 128
    NFB = T // FB
    FSUB = FB // 512
    NCF = 192
    NCB = 512

    def din(name, shape):
        return nc.dram_tensor(name, shape, F32, kind="ExternalInput").ap()

    xT = din("xT", [NSEQ, D, T])
    w_in = din("w_in", [D, D_IN])
    w_go = din("w_gdn_out", [D, D])
    w_do = din("w_diff_out", [D, D])
    w_o = din("w_o", [D, D])
    w_up = din("w_up", [D, 2 * D_FF])
    w_dn = din("w_down", [D_FF, D])
    vecs_d = din("vecs", [128, NV])
    cF_d = din("cF", [128, NCF])
    cB_d = din("cB", [128, NCB])
    cR_d = din("cR", [128, 2 * T])
    outT = nc.dram_tensor("outT", [NSEQ, D, T], F32, kind="ExternalOutput").ap()
    dbg_t = {}
    if dbg:
        for nm in ("ygT", "ydT", "x1T", "hnT", "mgT"):
            dbg_t[nm] = nc.dram_tensor("dbg_" + nm, [NSEQ, D, T], F32, kind="ExternalOutput").ap()

    es = contextlib.ExitStack()
    S = Sched(nc)
    RES = {}

    def getres(name):
        if name not in RES:
            RES[name] = Res(name)
        return RES[name]

    def sb(name, shape, dt):
        return es.enter_context(nc.sbuf_tensor(name, shape, dt))

    vecs = sb("vecs_sb", [128, NV], F32); r_vecs = Res("vecs")
    cF = sb("cF_sb", [128, NCF], F32); r_cF = Res("cF")
    cB = sb("cB_sb", [128, NCB], BF16); r_cB = Res("cB")
    cR = sb("cR_sb", [128, 2 * T], BF16)
    ones_f = sb("ones_f", [128, 128], F32)
    ones_b = sb("ones_b", [128, 128], BF16)
    small = sb("small", [128, 64], F32); r_small = Res("small")
    r_ones = Res("ones")
    arena = sb("arena", [128, ARENA_BYTES // 4], F32)
    psum = [es.enter_context(nc.psum_tensor("ps%d" % i, [128, 512], F32)) for i in range(8)]
    r_bank = [Res("bank%d" % i, excl=True) for i in range(8)]

    ident = cB[:, 0:128]
    blockones = cB[:, 128:256]
    ropeP = cB[:, 256:384]
    attmask = cB[:, 384:512]
    triU = cF[0:64, 0:64]
    maskneg = cF[0:64, 64:128]
    strict = cF[0:64, 128:192]
    ropeC = cR[:, 0:T]
    ropeS = cR[:, T:2 * T]
    c_eps = small[:, 0:1]; c_one = small[:, 1:2]; c_nlam = small[:, 2:3]; c_slw8 = small[:, 3:4]
    c_e12 = small[:, 4:6]; c_prod = small[:, 6:8]; c_nA = small[:, 8:16]
    out_res = []

    CUR = [None]

    def EM(eng, fn, reads, writes, cost, dma=False, sig=None):
        if CUR[0] is None:
            S.op(eng, fn, reads, writes, dma=dma, sig=sig, cost=cost)
        else:
            CUR[0].append((eng, fn, tuple(reads), tuple(writes), cost, dma, sig))

    def fsz(ap):
        n = 1
        for d in ap.shape[1:]:
            n *= d
        return n

    def ecost(eng, ap):
        f = fsz(ap)
        if eng == "act":
            return 200.0 + 0.8 * f
        if eng == "pool":
            return 150.0 + 2.0 * f
        return 100.0 + 1.1 * f

    def MM(out, lhsT, rhs, start, stop, reads, writes):
        c = max(64.0, fsz(out) / 3.5) * (4.0 if lhsT.dtype == F32 else 1.0)
        EM("pe", lambda e: e.matmul(out, lhsT, rhs, start=start, stop=stop), reads, writes, c)

    def TR(out, in_, idn, reads, writes):
        EM("pe", lambda e: e.transpose(out, in_, idn), reads, writes, 100.0)

    def ACT(out, in_, func, reads, writes, **kw):
        EM("act", lambda e: e.activation(out=out, in_=in_, func=func, **kw), reads, writes, ecost("act", out))

    def TT(eng, out, in0, in1, op, reads, writes):
        EM(eng, lambda e: e.tensor_tensor(out=out, in0=in0, in1=in1, op=op), reads, writes, ecost(eng, out))

    def STT(eng, out, in0, scalar, in1, op0, op1, reads, writes):
        EM(eng, lambda e: e.scalar_tensor_tensor(out=out, in0=in0, scalar=scalar, in1=in1,
                                                 op0=op0, op1=op1), reads, writes, ecost(eng, out))

    def TS(eng, out, in0, s1, op0, reads, writes, s2=None, op1=None):
        if op1 is None:
            EM(eng, lambda e: e.tensor_scalar(out=out, in0=in0, scalar1=s1, scalar2=None, op0=op0),
               reads, writes, ecost(eng, out))
        else:
            EM(eng, lambda e: e.tensor_scalar(out=out, in0=in0, scalar1=s1, scalar2=s2, op0=op0,
                                              op1=op1), reads, writes, ecost(eng, out))

    def CP(eng, out, in_, reads, writes):
        if eng == "act":
            EM("act", lambda e: e.activation(out=out, in_=in_, func=AF.Copy), reads, writes, ecost("act", out))
        else:
            EM(eng, lambda e: e.tensor_copy(out=out, in_=in_), reads, writes, ecost(eng, out))

    def MSET(eng, ap, val, writes):
        EM(eng, lambda e: e.memset(ap, val), (), writes, 100.0)

    def DMA(q, out, in_, reads, writes, sig):
        EM(q, lambda e: e.dma_start(out=out, in_=in_), reads, writes, 2000.0 + 2.0 * fsz(out), dma=True, sig=sig)

    class Arena:
        def __init__(self):
            self.off = 0

        def alloc(self, name, free_shape, dt, parts=128):
            esz = 4 if dt == F32 else 2
            n = 1
            for s in free_shape:
                n *= s
            nbytes = (n * esz + 63) // 64 * 64
            assert self.off + nbytes <= ARENA_BYTES, (name, self.off, nbytes)
            w0 = self.off // 4
            ap = arena[0:parts, w0:w0 + nbytes // 4]
            if dt != F32:
                ap = ap.bitcast(dt)
            ap = ap[:, 0:n]
            if len(free_shape) == 2:
                ap = ap.rearrange("p (a b) -> p a b", a=free_shape[0])
            elif len(free_shape) == 3:
                ap = ap.rearrange("p (a b c) -> p a b c", a=free_shape[0], b=free_shape[1])
            self.off += nbytes
            return ap, getres(name)

    AR = Arena()

    class Banks:
        def __init__(self):
            self.pool = list(range(8)); self.i = 0

        def set(self, pool):
            self.pool = list(pool); self.i = 0

        def next(self):
            b = self.pool[self.i % len(self.pool)]
            self.i += 1
            return psum[b], r_bank[b]

    BK = Banks()

    class WStream:
        def __init__(self, name, nslots, kchunks, specs, hold=1):
            self.hold = hold
            self.slots = [AR.alloc("%s%d" % (name, i), [kchunks, 128], BF16) for i in range(nslots)]
            self.specs = specs
            self.issued = 0
            self.ns = nslots

        def get(self, i):
            while self.issued < len(self.specs) and self.issued <= i + self.ns - self.hold:
                k = self.issued
                ap, r = self.slots[k % self.ns]
                DMA("pool", ap, self.specs[k], [], [r], r)
                self.issued += 1
            return self.slots[i % self.ns]

    def run_streams(funcs):
        lists = []
        for f in funcs:
            CUR[0] = []
            f()
            lists.append(CUR[0])
        CUR[0] = None
        idx = [0] * len(lists)
        live = [i for i in range(len(lists)) if lists[i]]
        while live:
            best, bt = None, None
            for i in live:
                eng, fn, reads, writes, cost, dma, sig = lists[i][idx[i]]
                t = S.peek(eng, reads, writes)
                if bt is None or t < bt:
                    best, bt = i, t
            eng, fn, reads, writes, cost, dma, sig = lists[best][idx[best]]
            S.op(eng, fn, reads, writes, dma=dma, sig=sig, cost=cost)
            idx[best] += 1
            if idx[best] == len(lists[best]):
                live.remove(best)

    yg_scr = nc.dram_tensor("yg_scr", [NSEQ, D, T], BF16, kind="Internal").ap()
    yd_scr = nc.dram_tensor("yd_scr", [NSEQ, D, T], BF16, kind="Internal").ap()

    def wcols(w, c0, n=128):
        return w[:, c0:c0 + n].rearrange("(c p) n -> p c n", p=128)

    DMA("sp", vecs[:], vecs_d[:, :], [], [r_vecs], r_vecs)
    DMA("sp", cF[:], cF_d[:, :], [], [r_cF], r_cF)
    DMA("pool", cB[:], cB_d[:, :], [], [r_cB], r_cB)
    DMA("pool", cR[:], cR_d[:, :], [], [r_cB], r_cB)
    MSET("dve", ones_f[:], 1.0, [r_ones])
    MSET("dve", ones_b[:], 1.0, [r_ones])
    MSET("dve", small[:], 0.0, [r_small])
    MSET("dve", c_eps, EPS, [r_small])
    MSET("dve", c_one, 1.0, [r_small])
    TS("dve", c_slw8, vecs[:, V_SLW:V_SLW + 1], 1.0 - LAMBDA_INIT, ALU.mult, [r_vecs, r_small], [r_small])
    ACT(c_nA, vecs[:, V_ALOG:V_ALOG + 8], AF.Exp, [r_vecs, r_small], [r_small])
    TS("dve", c_nA, c_nA, -1.0, ALU.mult, [r_small], [r_small])
    TT("dve", c_prod[:, 0:1], vecs[:, V_LAM:V_LAM + 1], vecs[:, V_LAM + 1:V_LAM + 2], ALU.mult, [r_vecs, r_small], [r_small])
    TT("dve", c_prod[:, 1:2], vecs[:, V_LAM + 2:V_LAM + 3], vecs[:, V_LAM + 3:V_LAM + 4], ALU.mult, [r_vecs, r_small], [r_small])
    MM(psum[0][:, 0:64], ones_f[:, :], small[:, 0:64], True, True, [r_small, r_ones], [r_bank[0]])
    ACT(c_e12, psum[0][:, 6:8], AF.Exp, [r_bank[0], r_small], [r_small])
    TT("dve", c_nlam, c_e12[:, 1:2], c_e12[:, 0:1], ALU.subtract, [r_small], [r_small])
    TS("dve", c_nlam, c_nlam, -LAMBDA_INIT, ALU.add, [r_small], [r_small])
    r_const = [r_vecs, r_cF, r_cB, r_ones, r_small]
    if dbg:
        dsm = nc.dram_tensor("dbg_small", [128, 64], F32, kind="ExternalOutput").ap()
        rr_ = getres("dbg_small")
        DMA("sp", dsm[:, :], small[:, :], [r_small], [rr_], rr_)
        out_res.append(rr_)

    def dbg_dump(nm, s, src, r_src, is_bf16):
        if not dbg:
            return
        dst = dbg_t[nm][s].rearrange("(c p) t -> p c t", p=128)
        rr = getres("dbg_" + nm)
        for c in range(8):
            DMA("pool" if is_bf16 else "sp", dst[:, c, :], src[:, c, :], r_src, [rr], rr)
        if rr not in out_res:
            out_res.append(rr)

    for s in range(NSEQ):
        S.barrier()
        AR.off = 0
        hnT, r_hn = AR.alloc("hnT", [8, T], BF16)
        mark_mixer = AR.off
        xsrc = xT[s].rearrange("(c p) t -> p c t", p=128)

        BK.set(range(8))
        xb = [AR.alloc("xb%d" % i, [8, 512], F32) for i in range(2)]
        sqb, r_sqb = AR.alloc("sq1", [8, 512], BF16)
        rt1, r_rt1 = AR.alloc("rt1", [512], F32)
        for blk in range(NB):
            xa, r_xa = xb[blk % 2]
            for c in range(8):
                DMA("sp", xa[:, c, :], xsrc[:, c, blk * 512:(blk + 1) * 512], [], [r_xa], r_xa)
            ACT(sqb, xa, AF.Square, [r_xa], [r_sqb])
            pb, r_pb = BK.next()
            for c in range(8):
                MM(pb[:, :], ones_b[:, :], sqb[:, c, :], c == 0, c == 7, [r_sqb, r_ones], [r_pb])
            ACT(rt1, pb[:, :], AF.Ln, [r_pb, r_small], [r_rt1], scale=1.0 / D, bias=c_eps)
            ACT(rt1, rt1, AF.Exp, [r_rt1], [r_rt1], scale=-0.5)
            for c in range(8):
                STT("dve", hnT[:, c, blk * 512:(blk + 1) * 512], xa[:, c, :], vecs[:, V_N1 + c:V_N1 + c + 1],
                    rt1, ALU.mult, ALU.mult, [r_xa, r_rt1, r_vecs], [r_hn])
        dbg_dump("hnT", s, hnT, [r_hn], True)

        if stop_after < 2:
            continue
        S.barrier()
        AR.off = mark_mixer
        BKg = Banks(); BKg.set([0, 1, 2, 3])
        BKd = Banks(); BKd.set([6, 7])
        ytl = [AR.alloc("ytl%d" % i, [512], BF16) for i in range(4)]
        r_scr = getres("yscr")

        def gdn_stream():
            wgb, r_wgb = AR.alloc("wgb", [8, 16], BF16)
            DMA("pool", wgb, wcols(w_in, OFF_GB, 16), [], [r_wgb], r_wgb)
            beta, r_beta = AR.alloc("beta", [8, NCH], F32, parts=64)
            gtab, r_g = AR.alloc("gtab", [8, NCH], F32, parts=64)
            gcol, r_gcol = AR.alloc("gcol", [8, NCH], F32, parts=64)
            negegc, r_negegc = AR.alloc("negegc", [8, NCH], F32, parts=64)
            kdecay, r_kdecay = AR.alloc("kdecay", [8, NCH], F32, parts=64)
            egl, r_egl = AR.alloc("egl", [8, NCH], F32)
            sptmp, r_sptmp = AR.alloc("sptmp", [NCH, 8], F32, parts=64)
            pb, r_pb = BKg.next()
            for n in range(NCH):
                for c in range(8):
                    MM(pb[0:64, n * 16:(n + 1) * 16], hnT[:, c, n * 64:(n + 1) * 64], wgb[:, c, :], c == 0, c == 7,
                       [r_hn, r_wgb], [r_pb])
            pv = pb[0:64, 0:NCH * 16].rearrange("p (n k) -> p n k", k=16)
            ACT(beta.rearrange("p h n -> p n h"), pv[:, :, 0:8], AF.Sigmoid, [r_pb], [r_beta])
            TT("dve", sptmp, pv[:, :, 8:16], vecs[0:64, V_DTB:V_DTB + 8].unsqueeze(1).to_broadcast([64, NCH, 8]),
               ALU.add, [r_pb, r_vecs], [r_sptmp])
            ACT(sptmp, sptmp, AF.Exp, [r_sptmp], [r_sptmp])
            ACT(sptmp, sptmp, AF.Ln, [r_sptmp, r_small], [r_sptmp], bias=c_one[0:64, :], scale=1.0)
            TT("dve", gtab.rearrange("p h n -> p n h"), sptmp, c_nA[0:64, :].unsqueeze(1).to_broadcast([64, NCH, 8]),
               ALU.mult, [r_sptmp, r_small], [r_g])
            gflat = gtab.rearrange("p h n -> p (h n)")
            pb, r_pb = BKg.next()
            MM(pb[0:64, 0:8 * NCH], triU, gflat, True, True, [r_g, r_cF], [r_pb])
            CP("dve", gcol.rearrange("p h n -> p (h n)"), pb[0:64, 0:8 * NCH], [r_pb], [r_gcol])
            ACT(negegc.rearrange("p h n -> p (h n)"), pb[0:64, 0:8 * NCH], AF.Exp, [r_pb], [r_negegc])
            TS("dve", negegc.rearrange("p h n -> p (h n)"), negegc.rearrange("p h n -> p (h n)"), -1.0, ALU.mult,
               [r_negegc], [r_negegc])
            pb2, r_pb2 = BKg.next()
            MM(pb2[:, 0:8 * NCH], ones_f[0:64, :], gflat, True, True, [r_g, r_ones], [r_pb2])
            ACT(egl.rearrange("p h n -> p (h n)"), pb2[:, 0:8 * NCH], AF.Exp, [r_pb2], [r_egl])
            TT("dve", kdecay.rearrange("p h n -> p (h n)"), pb2[0:64, 0:8 * NCH], gcol.rearrange("p h n -> p (h n)"),
               ALU.subtract, [r_pb2, r_gcol], [r_kdecay])
            ACT(kdecay.rearrange("p h n -> p (h n)"), kdecay.rearrange("p h n -> p (h n)"), AF.Exp, [r_kdecay], [r_kdecay])

            ub = [AR.alloc("ub%d" % i, [T + 4], BF16) for i in range(2)]
            for u_ap, r_u in ub:
                MSET("pool", u_ap[:, 0:4], 0.0, [r_u])
            qT, r_qT = AR.alloc("qT", [T], BF16)
            kT, r_kT = AR.alloc("kT", [T], BF16)
            vTf, r_vTf = AR.alloc("vTf", [T], BF16)
            zs, r_zs = AR.alloc("zs", [T], BF16)
            qdT, r_qdT = AR.alloc("qdT", [T], BF16)
            tmpq, r_tmpq = AR.alloc("tmpq", [T], BF16)
            r_qdTg = [getres("qdT_g%d" % g) for g in range(NG)]
            sq2 = [AR.alloc("sq2%d" % i, [512], BF16) for i in range(2)]
            rs2 = [AR.alloc("rs2%d" % i, [512], F32) for i in range(2)]
            Kd, r_Kd = AR.alloc("Kd", [NCH, 128], BF16, parts=64)
            Vt, r_Vt = AR.alloc("Vt", [NCH, 128], BF16, parts=64)
            gA, r_gA = AR.alloc("gA", [8, 64], F32, parts=64)
            decT, r_decT = AR.alloc("decT", [8, 64], F32, parts=64)
            nbs, r_nbs = AR.alloc("nbs", [8, 64], F32, parts=64)
            gB, r_gB = AR.alloc("gB", [8, 64], F32, parts=64)
            egcb, r_egcb = AR.alloc("egcb", [512], F32)
            Pb = [AR.alloc("P%d" % i, [8, 64], BF16, parts=64) for i in range(4)]
            Xb = [AR.alloc("X%d" % i, [8, 64], BF16, parts=64) for i in range(2)]
            Xall, r_Xall = AR.alloc("Xall", [NCH, 64], BF16, parts=64)
            intraT, r_intraT = AR.alloc("intraT", [NCH, 64], BF16, parts=64)
            oT, r_oT = AR.alloc("oT", [T], BF16)
            r_Xg = [getres("Xall_g%d" % g) for g in range(NG)]
            r_iTg = [getres("intraT_g%d" % g) for g in range(NG)]
            Sf, r_Sf = AR.alloc("Sf", [128], F32)
            Sb, r_Sb = AR.alloc("Sb", [128], BF16)
            Rb = [AR.alloc("R%d" % i, [128], BF16, parts=64) for i in range(2)]
            vnb = [AR.alloc("vn%d" % i, [128], BF16, parts=64) for i in range(2)]
            dg = [AR.alloc("dg%d" % i, [4, 128], BF16) for i in range(2)]
            ytmp = [AR.alloc("ytmp%d" % i, [512], F32) for i in range(1)] * 2
            specs = []
            for h in range(NH):
                for off in (OFF_GQ, OFF_GK, OFF_GV, OFF_GZ):
                    specs.append(wcols(w_in, off + h * 128))
            WS = WStream("wg", 3, 8, specs)
            dgi = 0
            for h in range(NH):
                for ti, (dst, r_dst) in enumerate(((tmpq, r_tmpq), (kT, r_kT), (vTf, r_vTf))):
                    wt, r_wt = WS.get(h * 4 + ti)
                    u_ap, r_u = ub[ti % 2]
                    for blk in range(NB):
                        pb, r_pb = BKg.next()
                        for c in range(8):
                            MM(pb[:, :], wt[:, c, :], hnT[:, c, blk * 512:(blk + 1) * 512], c == 0, c == 7,
                               [r_wt, r_hn], [r_pb])
                        CP("act" if blk % 2 == 0 else "dve", u_ap[:, 4 + blk * 512:4 + (blk + 1) * 512], pb[:, :], [r_pb], [r_u])
                    dga, r_dga = dg[dgi % 2]; dgi += 1
                    tile = ti * 8 + h
                    for j in range(4):
                        TS("dve", dga[:, j, :], ident, vecs[:, V_GCW + tile * 4 + j:V_GCW + tile * 4 + j + 1], ALU.mult,
                           [r_cB, r_vecs], [r_dga])
                    for blk in range(NB):
                        pb, r_pb = BKg.next()
                        for j in range(4):
                            MM(pb[:, :], dga[:, j, :], u_ap[:, 1 + j + blk * 512:1 + j + (blk + 1) * 512], j == 0, j == 3,
                               [r_dga, r_u], [r_pb])
                        ACT(dst[:, blk * 512:(blk + 1) * 512], pb[:, :], AF.Silu, [r_pb], [r_dst])
                wt, r_wt = WS.get(h * 4 + 3)
                for blk in range(NB):
                    pb, r_pb = BKg.next()
                    for c in range(8):
                        MM(pb[:, :], wt[:, c, :], hnT[:, c, blk * 512:(blk + 1) * 512], c == 0, c == 7, [r_wt, r_hn], [r_pb])
                    ACT(zs[:, blk * 512:(blk + 1) * 512], pb[:, :], AF.Silu, [r_pb], [r_zs])
                ii = 0
                for (src, r_src, dst, r_dst, scl) in ((tmpq, r_tmpq, qT, r_qT, 128 ** -0.5), (kT, r_kT, kT, r_kT, 1.0)):
                    for blk in range(NB):
                        sl = slice(blk * 512, (blk + 1) * 512)
                        sqa, r_sqa = sq2[ii % 2]; rsa, r_rsa = rs2[ii % 2]; ii += 1
                        TT("pool", sqa, src[:, sl], src[:, sl], ALU.mult, [r_src], [r_sqa])
                        pb, r_pb = BKg.next()
                        MM(pb[:, :], ones_b[:, :], sqa, True, True, [r_sqa, r_ones], [r_pb])
                        ACT(rsa, pb[:, :], AF.Ln, [r_pb, r_small], [r_rsa], scale=1.0, bias=c_eps)
                        ACT(rsa, rsa, AF.Exp, [r_rsa], [r_rsa], scale=-0.5)
                        STT("dve", dst[:, sl], src[:, sl], scl, rsa, ALU.mult, ALU.mult, [r_src, r_rsa], [r_dst])
                for (src, r_src, dst, r_dst, isk) in ((kT, r_kT, Kd, r_Kd, True), (vTf, r_vTf, Vt, r_Vt, False)):
                    for g in range(NG):
                        pb, r_pb = BKg.next()
                        pbb = pb[:].bitcast(BF16)
                        for j in range(8):
                            n = g * 8 + j
                            TR(pbb[0:64, j * 128:(j + 1) * 128], src[:, n * 64:(n + 1) * 64], ident, [r_src, r_cB], [r_pb])
                        pv3 = pbb[0:64, 0:1024].rearrange("p (a b) -> p a b", a=8)
                        if isk:
                            TT("dve", dst[:, g * 8:(g + 1) * 8, :], pv3,
                               kdecay[:, h, g * 8:(g + 1) * 8].unsqueeze(2).to_broadcast([64, 8, 128]), ALU.mult,
                               [r_pb, r_kdecay], [r_dst])
                        else:
                            CP("act", dst[:, g * 8:(g + 1) * 8, :], pv3, [r_pb], [r_dst])
                def inv_group(g, h=h):
                    ns = slice(g * 8, (g + 1) * 8)
                    cs = slice(g * 512, (g + 1) * 512)
                    TT("dve", gA, triU.unsqueeze(1).to_broadcast([64, 8, 64]),
                       gtab[:, h, ns].unsqueeze(2).to_broadcast([64, 8, 64]), ALU.mult, [r_cF, r_g], [r_gA])
                    pbc, r_pbc = BKg.next()
                    MM(pbc[:, :], ones_f[0:64, :], gA.rearrange("p a b -> p (a b)"), True, True, [r_gA, r_ones], [r_pbc])
                    pbc3 = pbc[0:64, :].rearrange("p (a b) -> p a b", a=8)
                    ACT(egcb, pbc[:, :], AF.Exp, [r_pbc], [r_egcb])
                    TT("dve", qdT[:, cs], qT[:, cs], egcb, ALU.mult, [r_qT, r_egcb], [r_qdTg[g]])
                    TT("dve", gA, pbc3, gcol[:, h, ns].unsqueeze(2).to_broadcast([64, 8, 64]), ALU.subtract,
                       [r_pbc, r_gcol], [r_gA])
                    TT("dve", gA, gA, maskneg.unsqueeze(1).to_broadcast([64, 8, 64]), ALU.min, [r_gA, r_cF], [r_gA])
                    ACT(decT, gA, AF.Exp, [r_gA], [r_decT])
                    TT("pool", nbs, strict.unsqueeze(1).to_broadcast([64, 8, 64]),
                       beta[:, h, ns].unsqueeze(2).to_broadcast([64, 8, 64]), ALU.mult, [r_cF, r_beta], [r_nbs])
                    pkk, r_pkk = BKg.next()
                    pqk, r_pqk = BKg.next()
                    for j in range(8):
                        n = g * 8 + j
                        tsl = slice(n * 64, (n + 1) * 64)
                        MM(pkk[0:64, j * 64:(j + 1) * 64], kT[:, tsl], kT[:, tsl], True, True, [r_kT], [r_pkk])
                    for j in range(8):
                        n = g * 8 + j
                        tsl = slice(n * 64, (n + 1) * 64)
                        MM(pqk[0:64, j * 64:(j + 1) * 64], kT[:, tsl], qT[:, tsl], True, True, [r_kT, r_qT], [r_pqk])
                    pkk3 = pkk[0:64, :].rearrange("p (a b) -> p a b", a=8)
                    pqk3 = pqk[0:64, :].rearrange("p (a b) -> p a b", a=8)
                    TT("dve", gB, pkk3, decT, ALU.mult, [r_pkk, r_decT], [r_gB])
                    P1, r_P1 = Pb[0]; P1T, r_P1T = Pb[1]
                    STT("dve", P1, gB, -1.0, nbs, ALU.mult, ALU.mult, [r_gB, r_nbs], [r_P1])
                    TT("dve", intraT[:, ns, :], pqk3, decT, ALU.mult, [r_pqk, r_decT], [r_iTg[g]])
                    ptb, r_ptb = BKg.next()
                    ptbb = ptb[:].bitcast(BF16)
                    for j in range(8):
                        TR(ptbb[0:64, j * 64:(j + 1) * 64], P1[:, j, :], ident[0:64, 0:64], [r_P1, r_cB], [r_ptb])
                    CP("act", P1T, ptbb[0:64, 0:512].rearrange("p (a b) -> p a b", a=8), [r_ptb], [r_P1T])
                    Xc, r_Xc = Xb[0]
                    TT("pool", Xc, P1, ident[0:64, 0:64].unsqueeze(1).to_broadcast([64, 8, 64]), ALU.add, [r_P1, r_cB], [r_Xc])
                    cur = 0
                    xi = 0
                    for lev in range(5):
                        (Pc, r_Pc), (PTc, r_PTc) = Pb[cur * 2], Pb[cur * 2 + 1]
                        (Pn, r_Pn), (PTn, r_PTn) = Pb[(1 - cur) * 2], Pb[(1 - cur) * 2 + 1]
                        last = lev == 4
                        if not last:
                            pa, r_pa = BKg.next()
                            for j in range(8):
                                MM(pa[0:64, j * 64:(j + 1) * 64], PTc[:, j, :], Pc[:, j, :], True, True, [r_Pc, r_PTc], [r_pa])
                        pt, r_pt = BKg.next()
                        for j in range(8):
                            MM(pt[0:64, j * 64:(j + 1) * 64], Pc[:, j, :], PTc[:, j, :], True, True, [r_Pc, r_PTc], [r_pt])
                        if not last:
                            CP("act", Pn, pa[0:64, :].rearrange("p (a b) -> p a b", a=8), [r_pa], [r_Pn])
                        CP("dve", PTn, pt[0:64, :].rearrange("p (a b) -> p a b", a=8), [r_pt], [r_PTn])
                        (Xc, r_Xc), (Xn, r_Xn) = Xb[xi], Xb[1 - xi]
                        px, r_px = BKg.next()
                        for j in range(8):
                            MM(px[0:64, j * 64:(j + 1) * 64], PTn[:, j, :], Xc[:, j, :], True, True, [r_PTn, r_Xc], [r_px])
                        if last:
                            TT("dve", Xall[:, ns, :], px[0:64, :].rearrange("p (a b) -> p a b", a=8), Xc, ALU.add,
                               [r_px, r_Xc], [r_Xg[g]])
                        else:
                            TT("dve", Xn, px[0:64, :].rearrange("p (a b) -> p a b", a=8), Xc, ALU.add, [r_px, r_Xc], [r_Xn])
                        xi = 1 - xi
                        cur = 1 - cur
                def scan_steps(g, h=h):
                    for n in range(g * 8, g * 8 + 8):
                        tsl = slice(n * 64, (n + 1) * 64)
                        pks, r_pks = psum[0][0:64, 0:128], r_bank[0]
                        py, r_py = psum[0][0:64, 128:256], r_bank[0]
                        po, r_po = psum[1][:, 0:64], r_bank[1]
                        pd, r_pd = psum[1][:, 64:192], r_bank[1]
                        Ra, r_Ra = Rb[n % 2]; vna, r_vna = vnb[n % 2]
                        col = slice(n, n + 1)
                        MM(pks, kT[:, tsl], Sb, True, True, [r_kT, r_Sb], [r_pks])
                        STT("dve", Ra, pks, negegc[:, h, col], Vt[:, n, :], ALU.mult, ALU.add,
                            [r_pks, r_negegc, r_Vt], [r_Ra])
                        MM(py, Xall[:, n, :], Ra, True, True, [r_Xg[g], r_Ra], [r_py])
                        TS("dve", vna, py, beta[:, h, col], ALU.mult, [r_py, r_beta], [r_vna])
                        MM(po, Sb, qdT[:, tsl], True, False, [r_Sb, r_qdTg[g]], [r_po])
                        MM(po, vna, intraT[:, n, :], False, True, [r_vna, r_iTg[g]], [r_po])
                        MM(pd, Kd[:, n, :], vna, True, True, [r_Kd, r_vna], [r_pd])
                        STT("dve", Sb, Sf, egl[:, h, col], pd, ALU.mult, ALU.add, [r_Sf, r_egl, r_pd], [r_Sb])
                        STT("dve", Sf, Sf, egl[:, h, col], pd, ALU.mult, ALU.add, [r_Sf, r_egl, r_pd], [r_Sf])
                        CP("act", oT[:, tsl], po, [r_po], [r_oT])

                def record(f):
                    saved = CUR[0]
                    CUR[0] = []
                    f()
                    out = CUR[0]
                    CUR[0] = saved
                    return out

                def splice(A, B):
                    ca = sum(x[4] for x in A) or 1.0
                    cb = sum(x[4] for x in B) or 1.0
                    ia = ib = 0
                    fa = fb_ = 0.0
                    while ia < len(A) or ib < len(B):
                        if ib >= len(B) or (ia < len(A) and fa / ca <= fb_ / cb):
                            CUR[0].append(A[ia]); fa += A[ia][4]; ia += 1
                        else:
                            CUR[0].append(B[ib]); fb_ += B[ib][4]; ib += 1

                BKg.set([2, 3])
                inv_group(0)
                MSET("dve", Sf, 0.0, [r_Sf])
                MSET("pool", Sb, 0.0, [r_Sb])
                for g in range(NG):
                    A = record(lambda: inv_group(g + 1)) if g + 1 < NG else []
                    B = record(lambda: scan_steps(g))
                    splice(A, B)
                BKg.set([0, 1, 2, 3])
                for blk in range(NB):
                    sl = slice(blk * 512, (blk + 1) * 512)
                    sqa, r_sqa = sq2[blk % 2]; rsa, r_rsa = rs2[blk % 2]; yta, r_yta = ytmp[blk % 2]
                    ACT(sqa, oT[:, sl], AF.Square, [r_oT], [r_sqa])
                    pb, r_pb = BKg.next()
                    MM(pb[:, :], ones_b[:, :], sqa, True, True, [r_sqa, r_ones], [r_pb])
                    ACT(rsa, pb[:, :], AF.Ln, [r_pb, r_small], [r_rsa], scale=1.0 / 128, bias=c_eps)
                    ACT(rsa, rsa, AF.Exp, [r_rsa], [r_rsa], scale=-0.5)
                    STT("dve", yta, oT[:, sl], vecs[:, V_GNW:V_GNW + 1], rsa, ALU.mult, ALU.mult, [r_oT, r_rsa, r_vecs], [r_yta])
                    yt_, r_yt_ = ytl[blk % 2]
                    TT("pool", yt_, yta, zs[:, sl], ALU.mult, [r_yta, r_zs], [r_yt_])
                    DMA("sp", yg_scr[s][h * 128:(h + 1) * 128, sl], yt_, [r_yt_], [r_scr], r_yt_)

        def diff_stream():
            qz = [AR.alloc("dqz%d" % i, [T], BF16) for i in range(2)]
            MSET("pool", qz[0][0][64:128, :], 0.0, [qz[0][1]])
            MSET("pool", qz[1][0][0:64, :], 0.0, [qz[1][1]])
            qT, r_qT = None, None
            kT, r_kT = AR.alloc("dkT", [T], BF16)
            Vk, r_Vk = AR.alloc("Vk", [NTB, 128], BF16)
            sq3 = [AR.alloc("sq3%d" % i, [512], BF16) for i in range(2)]
            rs3 = [AR.alloc("rs3%d" % i, [512], F32) for i in range(2)]
            qn = [AR.alloc("qn%d" % i, [512], BF16) for i in range(2)]
            t1b = [AR.alloc("t1%d" % i, [512], F32) for i in range(2)]
            t2b = [AR.alloc("t2%d" % i, [512], F32) for i in range(2)]
            pTb = [AR.alloc("pT%d" % i, [512], BF16) for i in range(3)]
            rcp = [AR.alloc("rcp%d" % i, [512], F32) for i in range(1)] * 2
            a12 = [AR.alloc("a12%d" % i, [512], F32) for i in range(2)]
            dTt, r_dT = AR.alloc("dTt", [512], F32)
            specs = []
            for h in range(NH):
                for off in (OFF_DQ, OFF_DK, OFF_DV):
                    specs.append(wcols(w_in, off + h * 128))
            WS = WStream("wd", 3, 8, specs)
            ii = 0
            pti = 0
            for h in range(NH):
                BKd.set([4, 5, 6, 7])
                for ti, (dst, r_dst, vcol) in enumerate(((qT, r_qT, V_QNW), (kT, r_kT, V_KNW))):
                    wt, r_wt = WS.get(h * 3 + ti)
                    for blk in range(NB):
                        sl = slice(blk * 512, (blk + 1) * 512)
                        sqa, r_sqa = sq3[ii % 2]; rsa, r_rsa = rs3[ii % 2]; qna, r_qna = qn[ii % 2]
                        t1, r_t1 = t1b[ii % 2]; t2, r_t2 = t2b[ii % 2]; ii += 1
                        pu, r_pu = BKd.next()
                        for c in range(8):
                            MM(pu[:, :], wt[:, c, :], hnT[:, c, sl], c == 0, c == 7, [r_wt, r_hn], [r_pu])
                        ACT(sqa, pu[:, :], AF.Square, [r_pu], [r_sqa])
                        pss, r_pss = BKd.next()
                        MM(pss[:, :], blockones, sqa, True, True, [r_sqa, r_cB], [r_pss])
                        ACT(rsa, pss[:, :], AF.Ln, [r_pss, r_small], [r_rsa], scale=1.0 / 64, bias=c_eps)
                        ACT(rsa, rsa, AF.Exp, [r_rsa], [r_rsa], scale=-0.5)
                        STT("dve", qna, pu[:, :], vecs[:, vcol:vcol + 1], rsa, ALU.mult, ALU.mult, [r_pu, r_rsa, r_vecs], [r_qna])
                        pr, r_pr = BKd.next()
                        MM(pr[:, :], ropeP, qna, True, True, [r_qna, r_cB], [r_pr])
                        TT("dve", t1, pr[:, :], ropeS[:, sl], ALU.mult, [r_pr, r_cB], [r_t1])
                        TT("pool", t2, qna, ropeC[:, sl], ALU.mult, [r_qna, r_cB], [r_t2])
                        if ti == 0:
                            TT("pool", qz[0][0][0:64, sl], t1[0:64, :], t2[0:64, :], ALU.add, [r_t1, r_t2], [qz[0][1]])
                            TT("pool", qz[1][0][64:128, sl], t1[64:128, :], t2[64:128, :], ALU.add, [r_t1, r_t2], [qz[1][1]])
                        else:
                            TT("pool", dst[:, sl], t1, t2, ALU.add, [r_t1, r_t2], [r_dst])
                wt, r_wt = WS.get(h * 3 + 2)
                for tb4 in range(NTB // 4):
                    pv_, r_pv = BKd.next()
                    for q4 in range(4):
                        tb = tb4 * 4 + q4
                        for c in range(8):
                            MM(pv_[:, q4 * 128:(q4 + 1) * 128], hnT[:, c, tb * 128:(tb + 1) * 128], wt[:, c, :], c == 0, c == 7,
                               [r_hn, r_wt], [r_pv])
                    CP("act" if tb4 % 2 == 0 else "dve", Vk[:, tb4 * 4:(tb4 + 1) * 4, :],
                       pv_[:, :].rearrange("p (a b) -> p a b", a=4), [r_pv], [r_Vk])
                BKd.set([6, 7])
                for qb in range(NB):
                    sl = slice(qb * 512, (qb + 1) * 512)
                    for m in range(2):
                        pO, r_pO = psum[4], r_bank[4]
                        pL, r_pL = psum[5], r_bank[5]
                        nkb = (qb + 1) * 4
                        ms = slice(64 * m, 64 * m + 64)
                        def stage1(kb):
                            nonlocal pti
                            j = kb - qb * 4
                            c0 = j * 128 if j >= 0 else 0
                            psc, r_psc = BKd.next()
                            pT, r_pT = pTb[pti % 3]; pti += 1
                            MM(psc[:, c0:512], kT[:, kb * 128:(kb + 1) * 128], qz[m][0][:, qb * 512 + c0:(qb + 1) * 512], True, True,
                               [r_kT, qz[m][1]], [r_psc])
                            ACT(pT[:, c0:512], psc[:, c0:512], AF.Exp, [r_psc], [r_pT], scale=0.125)
                            if j >= 0:
                                TT("pool", pT[:, c0:c0 + 128], pT[:, c0:c0 + 128], attmask, ALU.mult, [r_pT, r_cB], [r_pT])
                            return pT, r_pT, c0

                        cur_st = stage1(0)
                        for kb in range(nkb):
                            nxt_st = stage1(kb + 1) if kb + 1 < nkb else None
                            pT, r_pT, c0 = cur_st
                            MM(pO[:, c0:512], Vk[:, kb, :], pT[:, c0:512], kb == 0, kb == nkb - 1, [r_Vk, r_pT], [r_pO])
                            MM(pL[:, c0:512], ones_b[:, :], pT[:, c0:512], kb == 0, kb == nkb - 1, [r_ones, r_pT], [r_pL])
                            cur_st = nxt_st
                        rc, r_rc = rcp[m]; aa, r_aa = a12[m]
                        ACT(rc, pL[:, :], AF.Ln, [r_pL], [r_rc])
                        ACT(rc, rc, AF.Exp, [r_rc], [r_rc], scale=-1.0)
                        TT("dve", aa, pO[:, :], rc, ALU.mult, [r_pO, r_rc], [r_aa])
                    STT("dve", dTt, a12[1][0], c_nlam, a12[0][0], ALU.mult, ALU.add, [a12[1][1], a12[0][1], r_small], [r_dT])
                    sqa, r_sqa = sq3[ii % 2]; rsa, r_rsa = rs3[ii % 2]; ii += 1
                    ACT(sqa, dTt, AF.Square, [r_dT], [r_sqa])
                    pss, r_pss = BKd.next()
                    MM(pss[:, :], ones_b[:, :], sqa, True, True, [r_sqa, r_ones], [r_pss])
                    ACT(rsa, pss[:, :], AF.Ln, [r_pss, r_small], [r_rsa], scale=1.0 / 128, bias=c_eps)
                    ACT(rsa, rsa, AF.Exp, [r_rsa], [r_rsa], scale=-0.5)
                    yt_, r_yt_ = ytl[2 + qb % 2]
                    STT("dve", yt_, dTt, c_slw8, rsa, ALU.mult, ALU.mult, [r_dT, r_rsa, r_small], [r_yt_])
                    DMA("sp", yd_scr[s][h * 128:(h + 1) * 128, sl], yt_, [r_yt_], [r_scr], r_yt_)

        run_streams([gdn_stream, diff_stream])

        if stop_after < 4:
            continue
        S.barrier()
        BK.set(range(8))
        AR.off = mark_mixer
        ygT, r_yg = AR.alloc("ygT", [8, T], BF16)
        ydT, r_yd = AR.alloc("ydT", [8, T], BF16)
        for c in range(8):
            DMA("sp", ygT[:, c, :], yg_scr[s][c * 128:(c + 1) * 128, :], [r_scr], [r_yg], r_yg)
            DMA("sp", ydT[:, c, :], yd_scr[s][c * 128:(c + 1) * 128, :], [r_scr], [r_yd], r_yd)
        dbg_dump("ygT", s, ygT, [r_yg], True)
        dbg_dump("ydT", s, ydT, [r_yd], True)
        mgT, r_mg = AR.alloc("mgT", [8, T], BF16)
        gsb = [AR.alloc("gs%d" % i, [512], F32) for i in range(4)]
        m12 = [AR.alloc("m12%d" % i, [512], F32) for i in range(4)]
        specs = []
        for m in range(8):
            specs += [wcols(w_in, OFF_GATE + m * 128), wcols(w_in, OFF_GATE + D + m * 128),
                      wcols(w_go, m * 128), wcols(w_do, m * 128)]
        WS = WStream("wm", 8, 8, specs, hold=4)
        ii = 0
        for m in range(8):
            wts = [WS.get(m * 4 + k) for k in range(4)]
            for blk in range(NB):
                sl = slice(blk * 512, (blk + 1) * 512)
                mm_ = []
                for k in range(2):
                    (wg_, r_wg_), (wy_, r_wy_) = wts[k], wts[2 + k]
                    ysrc, r_ysrc = ((ygT, r_yg), (ydT, r_yd))[k]
                    ga, r_ga = gsb[ii % 4]; ma, r_ma = m12[ii % 4]; ii += 1
                    pg, r_pg = BK.next()
                    for c in range(8):
                        MM(pg[:, :], wg_[:, c, :], hnT[:, c, sl], c == 0, c == 7, [r_wg_, r_hn], [r_pg])
                    ACT(ga, pg[:, :], AF.Sigmoid, [r_pg, r_vecs], [r_ga], bias=vecs[:, V_BG + k * 8 + m:V_BG + k * 8 + m + 1], scale=1.0)
                    py_, r_py_ = BK.next()
                    for c in range(8):
                        MM(py_[:, :], wy_[:, c, :], ysrc[:, c, sl], c == 0, c == 7, [r_wy_, r_ysrc], [r_py_])
                    TT("dve", ma, py_[:, :], ga, ALU.mult, [r_py_, r_ga], [r_ma])
                    mm_.append((ma, r_ma))
                TT("pool", mgT[:, m, sl], mm_[0][0], mm_[1][0], ALU.add, [mm_[0][1], mm_[1][1]], [r_mg])
        dbg_dump("mgT", s, mgT, [r_mg], True)

        S.barrier()
        AR.off = 0
        x1T = AR.alloc("x1T", [8, T], F32)[0]
        r_x1 = [getres("x1_%d" % c) for c in range(8)]
        assert AR.off <= mark_mixer + 2 * 8 * T * 2
        AR.off = mark_mixer + 3 * 8 * T * 2
        specs = [wcols(w_o, m * 128) for m in range(8)]
        WS = WStream("wo", 3, 8, specs)
        for m in range(8):
            DMA("sp", x1T[:, m, :], xsrc[:, m, :], [], [r_x1[m]], r_x1[m])
        for m in range(8):
            wt, r_wt = WS.get(m)
            for blk in range(NB):
                sl = slice(blk * 512, (blk + 1) * 512)
                pb, r_pb = BK.next()
                for c in range(8):
                    MM(pb[:, :], wt[:, c, :], mgT[:, c, sl], c == 0, c == 7, [r_wt, r_mg], [r_pb])
                TT("dve", x1T[:, m, sl], x1T[:, m, sl], pb[:, :], ALU.add, [r_x1[m], r_pb], [r_x1[m]])
        if dbg:
            dst = dbg_t["x1T"][s].rearrange("(c p) t -> p c t", p=128)
            rr = getres("dbg_x1")
            for c in range(8):
                DMA("sp", dst[:, c, :], x1T[:, c, :], [r_x1[c]], [rr], rr)
            if rr not in out_res:
                out_res.append(rr)

        if stop_after < 5:
            continue
        S.barrier()
        AR.off = 8 * T * 4
        h2T, r_h2 = AR.alloc("h2T", [8, T], BF16)
        sqb, r_sqb = AR.alloc("sq5", [8, 512], BF16)
        rt5, r_rt5 = AR.alloc("rt5", [512], F32)
        for blk in range(NB):
            sl = slice(blk * 512, (blk + 1) * 512)
            ACT(sqb, x1T[:, :, sl], AF.Square, r_x1, [r_sqb])
            pb, r_pb = BK.next()
            for c in range(8):
                MM(pb[:, :], ones_b[:, :], sqb[:, c, :], c == 0, c == 7, [r_sqb, r_ones], [r_pb])
            ACT(rt5, pb[:, :], AF.Ln, [r_pb, r_small], [r_rt5], scale=1.0 / D, bias=c_eps)
            ACT(rt5, rt5, AF.Exp, [r_rt5], [r_rt5], scale=-0.5)
            for c in range(8):
                STT("dve", h2T[:, c, sl], x1T[:, c, sl], vecs[:, V_N2 + c:V_N2 + c + 1], rt5, ALU.mult, ALU.mult,
                    [r_x1[c], r_rt5, r_vecs], [r_h2])
        S.barrier()
        AR.off = 8 * T * 4 + 8 * T * 2
        actT = AR.alloc("actT", [22, FB], BF16)[0]
        r_act = [getres("act0"), getres("act1")]
        stg = [AR.alloc("stg%d" % i, [FB + 4], BF16) for i in range(4)]
        sgb = [AR.alloc("sg%d" % i, [512], BF16) for i in range(4)]
        dgf = [AR.alloc("dgf%d" % i, [3, 128], BF16) for i in range(4)]
        r_out = getres("out")
        if r_out not in out_res:
            out_res.append(r_out)
        odst = outT[s].rearrange("(c p) t -> p c t", p=128)
        mark_w = AR.off
        for fb in range(NFB):
            t0 = fb * FB
            AR.off = mark_w
            WD = WStream("wdn", 2, 22, [wcols(w_dn, m * 128) for m in range(8)])

            def up_stream(par):
                jl = list(range(par, 22, 2))
                specs = []
                for j in jl:
                    specs += [wcols(w_up, j * 128), wcols(w_up, D_FF + j * 128)]
                WSx = WStream("wu%d" % par, 3, 8, specs)
                bk = Banks(); bk.set([0, 1, 2, 3] if par == 0 else [4, 5, 6, 7])
                si = 0
                for ji, j in enumerate(jl):
                    for k in range(2):
                        wt, r_wt = WSx.get(ji * 2 + k)
                        st, r_st = stg[par * 2 + si % 2]; dga, r_dga = dgf[par * 2 + si % 2]; si += 1
                        tile = k * 22 + j
                        for tap in range(3):
                            TS("dve", dga[:, tap, :], ident, vecs[:, V_FCW + tile * 3 + tap:V_FCW + tile * 3 + tap + 1],
                               ALU.mult, [r_cB, r_vecs], [r_dga])
                        if fb == 0:
                            MSET("pool", st[:, 0:4], 0.0, [r_st])
                        else:
                            ph, r_ph = bk.next()
                            for c in range(8):
                                MM(ph[:, 0:2], wt[:, c, :], h2T[:, c, t0 - 2:t0], c == 0, c == 7, [r_wt, r_h2], [r_ph])
                            CP("dve", st[:, 2:4], ph[:, 0:2], [r_ph], [r_st])
                        for sub in range(FSUB):
                            pb, r_pb = bk.next()
                            for c in range(8):
                                MM(pb[:, :], wt[:, c, :], h2T[:, c, t0 + sub * 512:t0 + (sub + 1) * 512], c == 0, c == 7,
                                   [r_wt, r_h2], [r_pb])
                            CP("act" if (sub + k) % 2 == 0 else "dve", st[:, 4 + sub * 512:4 + (sub + 1) * 512], pb[:, :], [r_pb], [r_st])
                        for sub in range(FSUB):
                            pb, r_pb = bk.next()
                            for tap in range(3):
                                MM(pb[:, :], dga[:, tap, :], st[:, 2 + tap + sub * 512:2 + tap + (sub + 1) * 512], tap == 0, tap == 2,
                                   [r_dga, r_st], [r_pb])
                            sg_, r_sg_ = sgb[par * 2 + sub % 2]
                            if k == 0:
                                ACT(sg_, pb[:, :], AF.Silu, [r_pb, r_vecs], [r_sg_], bias=vecs[:, V_FCB + tile:V_FCB + tile + 1], scale=1.0)
                            else:
                                STT("dve", actT[:, j, sub * 512:(sub + 1) * 512], pb[:, :], vecs[:, V_FCB + tile:V_FCB + tile + 1], sg_,
                                    ALU.add, ALU.mult, [r_pb, r_sg_, r_vecs], [r_act[par]])

            run_streams([lambda: up_stream(0), lambda: up_stream(1)])
            for m in range(8):
                wt, r_wt = WD.get(m)
                for sub in range(FSUB):
                    sl = slice(t0 + sub * 512, t0 + (sub + 1) * 512)
                    pb, r_pb = BK.next()
                    for c in range(22):
                        MM(pb[:, :], wt[:, c, :], actT[:, c, sub * 512:(sub + 1) * 512], c == 0, c == 21, [r_wt, r_act[c % 2]], [r_pb])
                    TT("dve", x1T[:, m, sl], x1T[:, m, sl], pb[:, :], ALU.add, [r_x1[m], r_pb], [r_x1[m]])
                DMA("sp", odst[:, m, t0:t0 + FB], x1T[:, m, t0:t0 + FB], [r_x1[m]], [r_out], r_out)

    S.emit(es, final_res=out_res)
    es.close()
    return nc, S


def _consts(T):
    cF = np.zeros((128, 192), np.float32)
    cR = np.zeros((128, 2 * T), np.float32)
    k = np.arange(64)
    cF[:64, 0:64] = (k[:, None] <= k[None, :]).astype(np.float32)
    cF[:64, 64:128] = np.where(k[None, :] >= k[:, None], 0.0, -1e30).astype(np.float32)
    cF[:64, 128:192] = (k[:, None] < k[None, :]).astype(np.float32)
    pos = np.arange(T, dtype=np.float32)
    inv_freq = (np.float32(500000.0) ** (-(np.arange(0, 16, 2, dtype=np.float32)) / np.float32(16))).astype(np.float32)
    ang = (pos[:, None] * inv_freq[None, :]).astype(np.float32)
    cos = np.cos(ang.astype(np.float64)).astype(np.float32).T
    sin = np.sin(ang.astype(np.float64)).astype(np.float32).T
    C = np.ones((128, T), np.float32)
    Sg = np.zeros((128, T), np.float32)
    P = np.zeros((128, 128), np.float32)
    for base in (0, 64):
        C[base:base + 8] = cos; C[base + 8:base + 16] = cos
        Sg[base:base + 8] = -sin; Sg[base + 8:base + 16] = sin
        for r in range(8):
            P[base + r, base + r + 8] = 1.0
            P[base + r + 8, base + r] = 1.0
    cR[:, 0:T] = C
    cR[:, T:] = Sg
    cB = np.zeros((128, 512), np.float32)
    cB[:, 0:128] = np.eye(128, dtype=np.float32)
    cB[:64, 128:192] = 1.0
    cB[64:, 192:256] = 1.0
    cB[:, 256:384] = P
    p = np.arange(128)
    cB[:, 384:512] = (p[:, None] <= p[None, :]).astype(np.float32)
    return cF, cB, cR


def _vecs(inp):
    v = np.zeros((128, NV), np.float32)
    g = lambda k: np.asarray(inp[k], np.float32)[0]
    v[:, V_N1:V_N1 + 8] = g("norm1_w").reshape(8, 128).T
    v[:, V_N2:V_N2 + 8] = g("norm2_w").reshape(8, 128).T
    v[:, V_BG:V_BG + 16] = g("b_gate").reshape(16, 128).T
    v[:, V_GCW:V_GCW + 96] = g("gdn_conv_w").reshape(4, 24, 128).transpose(2, 1, 0).reshape(128, 96)
    v[:, V_FCW:V_FCW + 132] = g("ffn_conv_w").reshape(3, 44, 128).transpose(2, 1, 0).reshape(128, 132)
    v[:, V_FCB:V_FCB + 44] = g("ffn_conv_b").reshape(44, 128).T
    v[:, V_GNW] = g("gdn_norm_w")
    v[:, V_SLW] = g("diff_subln_w")
    v[:, V_QNW] = np.tile(g("diff_q_norm_w"), 2)
    v[:, V_KNW] = np.tile(g("diff_k_norm_w"), 2)
    v[:, V_ALOG:V_ALOG + 8] = g("gdn_A_log")[None, :]
    v[:, V_DTB:V_DTB + 8] = g("gdn_dt_bias")[None, :]
    v[:64, V_LAM + 0] = g("lambda_q1"); v[:64, V_LAM + 1] = g("lambda_k1")
    v[:64, V_LAM + 2] = g("lambda_q2"); v[:64, V_LAM + 3] = g("lambda_k2")
    return v


_CACHE = {}


def run(inputs, T, B, FB, dbg=False, ncores=8):
    NSEQ = B // ncores
    key = (T, NSEQ, FB, dbg)
    if key not in _CACHE:
        _CACHE[key] = build_program(T, NSEQ, FB, dbg)
    nc, S = _CACHE[key]
    x = np.asarray(inputs["x"], np.float32)
    cF, cB, cR = _consts(T)
    vecs = _vecs(inputs)
    shared = {
        "w_in": np.ascontiguousarray(np.asarray(inputs["w_in"], np.float32)[0]),
        "w_gdn_out": np.ascontiguousarray(np.asarray(inputs["w_gdn_out"], np.float32)[0]),
        "w_diff_out": np.ascontiguousarray(np.asarray(inputs["w_diff_out"], np.float32)[0]),
        "w_o": np.ascontiguousarray(np.asarray(inputs["w_o"], np.float32)[0]),
        "w_up": np.ascontiguousarray(np.asarray(inputs["w_up"], np.float32)[0]),
        "w_down": np.ascontiguousarray(np.asarray(inputs["w_down"], np.float32)[0]),
        "vecs": vecs, "cF": cF, "cB": cB, "cR": cR,
    }
    in_maps = []
    for c in range(ncores):
        m = dict(shared)
        m["xT"] = np.ascontiguousarray(x[c * NSEQ:(c + 1) * NSEQ].transpose(0, 2, 1))
        in_maps.append(m)
    res = run_bass_kernel_spmd(nc, in_maps, core_ids=list(range(ncores)))
    out = np.concatenate([r["outT"].transpose(0, 2, 1) for r in res.results], axis=0)
    if dbg:
        extra = {}
        for nm in ("ygT", "ydT", "x1T", "hnT", "mgT"):
            extra[nm] = np.concatenate([r["dbg_" + nm].transpose(0, 2, 1) for r in res.results], axis=0)
        extra["small"] = res.results[0]["dbg_small"]
        return np.ascontiguousarray(out), extra
    return np.ascontiguousarray(out)


def kernel(**inputs):
    return run(inputs, T=2048, B=16, FB=1024)
```

```python
import contextlib
import math
import numpy as np
import concourse.bass as bass
import concourse.mybir as mybir
from concourse.bass_utils import run_bass_kernel_spmd

F32 = mybir.dt.float32
BF16 = mybir.dt.bfloat16
AF = mybir.ActivationFunctionType
ALU = mybir.AluOpType

D = 1024
D_IN = 9232
D_FF = 2816
NH = 8
EPS = 1e-6
OFF_GQ, OFF_GK, OFF_GV, OFF_GZ, OFF_GB, OFF_GA = 0, 1024, 2048, 3072, 4096, 4104
OFF_DQ, OFF_DK, OFF_DV, OFF_GATE = 4112, 5136, 6160, 7184
LAMBDA_INIT = 0.8 - 0.6 * math.exp(0.0)
ARENA_BYTES = 188 * 1024
V_N1, V_N2, V_BG, V_GCW, V_FCW, V_FCB = 0, 8, 16, 32, 128, 260
V_GNW, V_SLW, V_QNW, V_KNW, V_ALOG, V_DTB, V_LAM = 304, 305, 306, 307, 308, 316, 324
NV = 328


class Res:
    __slots__ = ("name", "last_w", "readers", "sem", "cnt", "excl")

    def __init__(self, name, excl=False):
        self.name = name
        self.excl = excl
        self.last_w = None
        self.readers = {}
        self.sem = None
        self.cnt = 0


class Op:
    __slots__ = ("eng", "fn", "deps", "dma", "sig", "sig_cnt", "marked", "tick", "idx", "pre_drain", "t_end", "cost")


class Sched:
    ENGS = ("pe", "act", "dve", "pool", "sp")
    SAME_WIN = 3
    SHORT = {"act": 251.0, "dve": 170.0, "pool": 278.0, "sp": 0.0}
    LAT = 150.0

    def __init__(self, nc):
        self.nc = nc
        self.ops = {e: [] for e in self.ENGS}
        self.last_dma = {}
        self.free = {e: 0.0 for e in self.ENGS}

    def _ready(self, eng, reads, writes):
        t = self.free[eng]
        for r in reads:
            d = r.last_w
            if d is not None:
                t = max(t, d.t_end + (0.0 if (d.eng == eng and not d.dma) else self.LAT))
        for r in writes:
            d = r.last_w
            if d is not None:
                t = max(t, d.t_end + (0.0 if (d.eng == eng and not d.dma) else self.LAT))
            for d in r.readers.values():
                t = max(t, d.t_end + (0.0 if (d.eng == eng and not d.dma) else self.LAT))
        return t

    def peek(self, eng, reads, writes):
        if any(r.excl for r in reads):
            writes = list(writes) + [r for r in reads if r.excl]
        return self._ready(eng, reads, writes)

    def op(self, eng, fn, reads=(), writes=(), dma=False, sig=None, extra=(), cost=300.0):
        o = Op()
        o.eng = eng; o.fn = fn; o.dma = dma; o.marked = False; o.tick = 0
        o.sig = None; o.sig_cnt = 0; o.pre_drain = False
        deps = list(extra)
        if any(r.excl for r in reads):
            writes = list(writes) + [r for r in reads if r.excl]
            reads = [r for r in reads if not r.excl]
        t0 = self._ready(eng, reads, writes)
        for d in extra:
            t0 = max(t0, d.t_end + self.LAT)
        if dma:
            self.free[eng] = t0 + 60.0
        else:
            self.free[eng] = t0 + cost
        o.t_end = t0 + cost
        o.cost = cost
        for r in reads:
            if r.last_w is not None:
                deps.append(r.last_w)
        for r in writes:
            if r.last_w is not None:
                deps.append(r.last_w)
            deps.extend(r.readers.values())
        o.deps = []
        seen = set()
        for d in deps:
            if id(d) in seen:
                continue
            seen.add(id(d))
            if d.dma:
                o.deps.append(d)
            elif d.eng != eng:
                d.marked = True
                o.deps.append(d)
            elif eng != "pe" and len(self.ops[eng]) - d.idx <= self.SAME_WIN and d.cost < self.SHORT[eng]:
                o.pre_drain = True
        o.idx = len(self.ops[eng])
        if dma:
            sig.cnt += 1
            o.sig = sig; o.sig_cnt = sig.cnt
            self.last_dma[id(sig)] = o
        for r in writes:
            r.last_w = o
            r.readers = {}
        for r in reads:
            if r.last_w is not o:
                r.readers[("d", id(sig)) if dma else eng] = o
        self.ops[eng].append(o)
        return o

    def barrier(self):
        lasts = []
        for e in ("pe", "act", "dve", "pool"):
            if self.ops[e]:
                lasts.append(self.op(e, lambda eng: eng.drain(), writes=[Res("bar")]))
        b = self.op("sp", lambda eng: eng.nop(), writes=[Res("bar")],
                    extra=lasts + list(self.last_dma.values()))
        for e in ("pe", "act", "dve", "pool"):
            self.op(e, lambda eng: eng.nop(), extra=[b])

    def emit(self, es, final_res=()):
        nc = self.nc
        esem = {e: es.enter_context(nc.semaphore("eng_" + e)) for e in self.ENGS}
        dres = {}
        for e in self.ENGS:
            for o in self.ops[e]:
                if o.dma and id(o.sig) not in dres:
                    dres[id(o.sig)] = o.sig
        for i, r in enumerate(dres.values()):
            r.sem = es.enter_context(nc.semaphore("d%d_%s" % (i, r.name)))
        self.nsem = len(dres) + len(self.ENGS)
        for e in self.ENGS:
            t = 0
            for o in self.ops[e]:
                if o.marked and not o.dma:
                    t += 1
                o.tick = t
        block = es.enter_context(nc.Block())
        engobj = {"pe": block.tensor, "act": block.scalar, "dve": block.vector,
                  "pool": block.gpsimd, "sp": block.sync}
        self.stats = {}

        def make(e):
            def body(eng):
                seen_e = {}
                seen_d = {}
                nw = 0
                nd = [0]
                for o in self.ops[e]:
                    for d in o.deps:
                        if d.dma:
                            k = id(d.sig); v = d.sig_cnt * 16
                            if seen_d.get(k, 0) >= v:
                                continue
                            seen_d[k] = v
                            eng.wait_ge(d.sig.sem, v); nw += 1
                        else:
                            if seen_e.get(d.eng, 0) >= d.tick:
                                continue
                            seen_e[d.eng] = d.tick
                            eng.wait_ge(esem[d.eng], d.tick); nw += 1
                    if o.pre_drain:
                        eng.drain(); nd[0] += 1
                    ins = o.fn(eng)
                    if o.dma:
                        ins.then_inc(o.sig.sem, 16)
                    elif o.marked:
                        ins.then_inc(esem[e], 1)
                if e == "sp":
                    for r in final_res:
                        eng.wait_ge(r.sem, r.cnt * 16)
                self.stats[e] = (len(self.ops[e]), nw, nd[0])
            return body

        for e in self.ENGS:
            engobj[e](make(e))


def build_program(T, NSEQ, FB, dbg=False, stop_after=9):
    nc = bass.Bass("TRN2", target_bir_lowering=False)
    NB = T // 512
    NCH = T // 64
    NG = NCH // 8
    NTB = T // 128
    NFB = T // FB
    FSUB = FB // 512
    NCF = 192
    NCB = 512

    def din(name, shape):
        return nc.dram_tensor(name, shape, F32, kind="ExternalInput").ap()

    xT = din("xT", [NSEQ, D, T])
    w_in = din("w_in", [D, D_IN])
    w_go = din("w_gdn_out", [D, D])
    w_do = din("w_diff_out", [D, D])
    w_o = din("w_o", [D, D])
    w_up = din("w_up", [D, 2 * D_FF])
    w_dn = din("w_down", [D_FF, D])
    vecs_d = din("vecs", [128, NV])
    cF_d = din("cF", [128, NCF])
    cB_d = din("cB", [128, NCB])
    cR_d = din("cR", [128, 2 * T])
    outT = nc.dram_tensor("outT", [NSEQ, D, T], F32, kind="ExternalOutput").ap()
    dbg_t = {}
    if dbg:
        for nm in ("ygT", "ydT", "x1T", "hnT", "mgT"):
            dbg_t[nm] = nc.dram_tensor("dbg_" + nm, [NSEQ, D, T], F32, kind="ExternalOutput").ap()

    es = contextlib.ExitStack()
    S = Sched(nc)
    RES = {}

    def getres(name):
        if name not in RES:
            RES[name] = Res(name)
        return RES[name]

    def sb(name, shape, dt):
        return es.enter_context(nc.sbuf_tensor(name, shape, dt))

    vecs = sb("vecs_sb", [128, NV], F32); r_vecs = Res("vecs")
    cF = sb("cF_sb", [128, NCF], F32); r_cF = Res("cF")
    cB = sb("cB_sb", [128, NCB], BF16); r_cB = Res("cB")
    cR = sb("cR_sb", [128, 2 * T], BF16)
    ones_f = sb("ones_f", [128, 128], F32)
    ones_b = sb("ones_b", [128, 128], BF16)
    small = sb("small", [128, 64], F32); r_small = Res("small")
    r_ones = Res("ones")
    arena = sb("arena", [128, ARENA_BYTES // 4], F32)
    psum = [es.enter_context(nc.psum_tensor("ps%d" % i, [128, 512], F32)) for i in range(8)]
    r_bank = [Res("bank%d" % i, excl=True) for i in range(8)]

    ident = cB[:, 0:128]
    blockones = cB[:, 128:256]
    ropeP = cB[:, 256:384]
    attmask = cB[:, 384:512]
    triU = cF[0:64, 0:64]
    maskneg = cF[0:64, 64:128]
    strict = cF[0:64, 128:192]
    ropeC = cR[:, 0:T]
    ropeS = cR[:, T:2 * T]
    c_eps = small[:, 0:1]; c_one = small[:, 1:2]; c_nlam = small[:, 2:3]; c_slw8 = small[:, 3:4]
    c_e12 = small[:, 4:6]; c_prod = small[:, 6:8]; c_nA = small[:, 8:16]
    out_res = []

    CUR = [None]

    def EM(eng, fn, reads, writes, cost, dma=False, sig=None):
        if CUR[0] is None:
            S.op(eng, fn, reads, writes, dma=dma, sig=sig, cost=cost)
        else:
            CUR[0].append((eng, fn, tuple(reads), tuple(writes), cost, dma, sig))

    def fsz(ap):
        n = 1
        for d in ap.shape[1:]:
            n *= d
        return n

    def ecost(eng, ap):
        f = fsz(ap)
        if eng == "act":
            return 200.0 + 0.8 * f
        if eng == "pool":
            return 150.0 + 2.0 * f
        return 100.0 + 1.1 * f

    def MM(out, lhsT, rhs, start, stop, reads, writes):
        c = max(64.0, fsz(out) / 3.5) * (4.0 if lhsT.dtype == F32 else 1.0)
        EM("pe", lambda e: e.matmul(out, lhsT, rhs, start=start, stop=stop), reads, writes, c)

    def TR(out, in_, idn, reads, writes):
        EM("pe", lambda e: e.transpose(out, in_, idn), reads, writes, 100.0)

    def ACT(out, in_, func, reads, writes, **kw):
        EM("act", lambda e: e.activation(out=out, in_=in_, func=func, **kw), reads, writes, ecost("act", out))

    def TT(eng, out, in0, in1, op, reads, writes):
        EM(eng, lambda e: e.tensor_tensor(out=out, in0=in0, in1=in1, op=op), reads, writes, ecost(eng, out))

    def STT(eng, out, in0, scalar, in1, op0, op1, reads, writes):
        EM(eng, lambda e: e.scalar_tensor_tensor(out=out, in0=in0, scalar=scalar, in1=in1,
                                                 op0=op0, op1=op1), reads, writes, ecost(eng, out))

    def TS(eng, out, in0, s1, op0, reads, writes, s2=None, op1=None):
        if op1 is None:
            EM(eng, lambda e: e.tensor_scalar(out=out, in0=in0, scalar1=s1, scalar2=None, op0=op0),
               reads, writes, ecost(eng, out))
        else:
            EM(eng, lambda e: e.tensor_scalar(out=out, in0=in0, scalar1=s1, scalar2=s2, op0=op0,
                                              op1=op1), reads, writes, ecost(eng, out))

    def CP(eng, out, in_, reads, writes):
        if eng == "act":
            EM("act", lambda e: e.activation(out=out, in_=in_, func=AF.Copy), reads, writes, ecost("act", out))
        else:
            EM(eng, lambda e: e.tensor_copy(out=out, in_=in_), reads, writes, ecost(eng, out))

    def MSET(eng, ap, val, writes):
        EM(eng, lambda e: e.memset(ap, val), (), writes, 100.0)

    def DMA(q, out, in_, reads, writes, sig):
        EM(q, lambda e: e.dma_start(out=out, in_=in_), reads, writes, 2000.0 + 2.0 * fsz(out), dma=True, sig=sig)

    class Arena:
        def __init__(self):
            self.off = 0

        def alloc(self, name, free_shape, dt, parts=128):
            esz = 4 if dt == F32 else 2
            n = 1
            for s in free_shape:
                n *= s
            nbytes = (n * esz + 63) // 64 * 64
            assert self.off + nbytes <= ARENA_BYTES, (name, self.off, nbytes)
            w0 = self.off // 4
            ap = arena[0:parts, w0:w0 + nbytes // 4]
            if dt != F32:
                ap = ap.bitcast(dt)
            ap = ap[:, 0:n]
            if len(free_shape) == 2:
                ap = ap.rearrange("p (a b) -> p a b", a=free_shape[0])
            elif len(free_shape) == 3:
                ap = ap.rearrange("p (a b c) -> p a b c", a=free_shape[0], b=free_shape[1])
            self.off += nbytes
            return ap, getres(name)

    AR = Arena()

    class Banks:
        def __init__(self):
            self.pool = list(range(8)); self.i = 0

        def set(self, pool):
            self.pool = list(pool); self.i = 0

        def next(self):
            b = self.pool[self.i % len(self.pool)]
            self.i += 1
            return psum[b], r_bank[b]

    BK = Banks()

    class WStream:
        def __init__(self, name, nslots, kchunks, specs, hold=1):
            self.hold = hold
            self.slots = [AR.alloc("%s%d" % (name, i), [kchunks, 128], BF16) for i in range(nslots)]
            self.specs = specs
            self.issued = 0
            self.ns = nslots

        def get(self, i):
            while self.issued < len(self.specs) and self.issued <= i + self.ns - self.hold:
                k = self.issued
                ap, r = self.slots[k % self.ns]
                DMA("pool", ap, self.specs[k], [], [r], r)
                self.issued += 1
            return self.slots[i % self.ns]

    def run_streams(funcs):
        lists = []
        for f in funcs:
            CUR[0] = []
            f()
            lists.append(CUR[0])
        CUR[0] = None
        idx = [0] * len(lists)
        live = [i for i in range(len(lists)) if lists[i]]
        while live:
            best, bt = None, None
            for i in live:
                eng, fn, reads, writes, cost, dma, sig = lists[i][idx[i]]
                t = S.peek(eng, reads, writes)
                if bt is None or t < bt:
                    best, bt = i, t
            eng, fn, reads, writes, cost, dma, sig = lists[best][idx[best]]
            S.op(eng, fn, reads, writes, dma=dma, sig=sig, cost=cost)
            idx[best] += 1
            if idx[best] == len(lists[best]):
                live.remove(best)

    yg_scr = nc.dram_tensor("yg_scr", [NSEQ, D, T], BF16, kind="Internal").ap()
    yd_scr = nc.dram_tensor("yd_scr", [NSEQ, D, T], BF16, kind="Internal").ap()

    def wcols(w, c0, n=128):
        return w[:, c0:c0 + n].rearrange("(c p) n -> p c n", p=128)

    DMA("sp", vecs[:], vecs_d[:, :], [], [r_vecs], r_vecs)
    DMA("sp", cF[:], cF_d[:, :], [], [r_cF], r_cF)
    DMA("pool", cB[:], cB_d[:, :], [], [r_cB], r_cB)
    DMA("pool", cR[:], cR_d[:, :], [], [r_cB], r_cB)
    MSET("dve", ones_f[:], 1.0, [r_ones])
    MSET("dve", ones_b[:], 1.0, [r_ones])
    MSET("dve", small[:], 0.0, [r_small])
    MSET("dve", c_eps, EPS, [r_small])
    MSET("dve", c_one, 1.0, [r_small])
    TS("dve", c_slw8, vecs[:, V_SLW:V_SLW + 1], 1.0 - LAMBDA_INIT, ALU.mult, [r_vecs, r_small], [r_small])
    ACT(c_nA, vecs[:, V_ALOG:V_ALOG + 8], AF.Exp, [r_vecs, r_small], [r_small])
    TS("dve", c_nA, c_nA, -1.0, ALU.mult, [r_small], [r_small])
    TT("dve", c_prod[:, 0:1], vecs[:, V_LAM:V_LAM + 1], vecs[:, V_LAM + 1:V_LAM + 2], ALU.mult, [r_vecs, r_small], [r_small])
    TT("dve", c_prod[:, 1:2], vecs[:, V_LAM + 2:V_LAM + 3], vecs[:, V_LAM + 3:V_LAM + 4], ALU.mult, [r_vecs, r_small], [r_small])
    MM(psum[0][:, 0:64], ones_f[:, :], small[:, 0:64], True, True, [r_small, r_ones], [r_bank[0]])
    ACT(c_e12, psum[0][:, 6:8], AF.Exp, [r_bank[0], r_small], [r_small])
    TT("dve", c_nlam, c_e12[:, 1:2], c_e12[:, 0:1], ALU.subtract, [r_small], [r_small])
    TS("dve", c_nlam, c_nlam, -LAMBDA_INIT, ALU.add, [r_small], [r_small])
    r_const = [r_vecs, r_cF, r_cB, r_ones, r_small]
    if dbg:
        dsm = nc.dram_tensor("dbg_small", [128, 64], F32, kind="ExternalOutput").ap()
        rr_ = getres("dbg_small")
        DMA("sp", dsm[:, :], small[:, :], [r_small], [rr_], rr_)
        out_res.append(rr_)

    def dbg_dump(nm, s, src, r_src, is_bf16):
        if not dbg:
            return
        dst = dbg_t[nm][s].rearrange("(c p) t -> p c t", p=128)
        rr = getres("dbg_" + nm)
        for c in range(8):
            DMA("pool" if is_bf16 else "sp", dst[:, c, :], src[:, c, :], r_src, [rr], rr)
        if rr not in out_res:
            out_res.append(rr)

    for s in range(NSEQ):
        S.barrier()
        AR.off = 0
        hnT, r_hn = AR.alloc("hnT", [8, T], BF16)
        mark_mixer = AR.off
        xsrc = xT[s].rearrange("(c p) t -> p c t", p=128)

        BK.set(range(8))
        xb = [AR.alloc("xb%d" % i, [8, 512], F32) for i in range(2)]
        sqb, r_sqb = AR.alloc("sq1", [8, 512], BF16)
        rt1, r_rt1 = AR.alloc("rt1", [512], F32)
        for blk in range(NB):
            xa, r_xa = xb[blk % 2]
            for c in range(8):
                DMA("sp", xa[:, c, :], xsrc[:, c, blk * 512:(blk + 1) * 512], [], [r_xa], r_xa)
            ACT(sqb, xa, AF.Square, [r_xa], [r_sqb])
            pb, r_pb = BK.next()
            for c in range(8):
                MM(pb[:, :], ones_b[:, :], sqb[:, c, :], c == 0, c == 7, [r_sqb, r_ones], [r_pb])
            ACT(rt1, pb[:, :], AF.Ln, [r_pb, r_small], [r_rt1], scale=1.0 / D, bias=c_eps)
            ACT(rt1, rt1, AF.Exp, [r_rt1], [r_rt1], scale=-0.5)
            for c in range(8):
                STT("dve", hnT[:, c, blk * 512:(blk + 1) * 512], xa[:, c, :], vecs[:, V_N1 + c:V_N1 + c + 1],
                    rt1, ALU.mult, ALU.mult, [r_xa, r_rt1, r_vecs], [r_hn])
        dbg_dump("hnT", s, hnT, [r_hn], True)

        if stop_after < 2:
            continue
        S.barrier()
        AR.off = mark_mixer
        BKg = Banks(); BKg.set([0, 1, 2, 3])
        BKd = Banks(); BKd.set([6, 7])
        ytl = [AR.alloc("ytl%d" % i, [512], BF16) for i in range(4)]
        r_scr = getres("yscr")

        def gdn_stream():
            wgb, r_wgb = AR.alloc("wgb", [8, 16], BF16)
            DMA("pool", wgb, wcols(w_in, OFF_GB, 16), [], [r_wgb], r_wgb)
            beta, r_beta = AR.alloc("beta", [8, NCH], F32, parts=64)
            gtab, r_g = AR.alloc("gtab", [8, NCH], F32, parts=64)
            gcol, r_gcol = AR.alloc("gcol", [8, NCH], F32, parts=64)
            negegc, r_negegc = AR.alloc("negegc", [8, NCH], F32, parts=64)
            kdecay, r_kdecay = AR.alloc("kdecay", [8, NCH], F32, parts=64)
            egl, r_egl = AR.alloc("egl", [8, NCH], F32)
            sptmp, r_sptmp = AR.alloc("sptmp", [NCH, 8], F32, parts=64)
            pb, r_pb = BKg.next()
            for n in range(NCH):
                for c in range(8):
                    MM(pb[0:64, n * 16:(n + 1) * 16], hnT[:, c, n * 64:(n + 1) * 64], wgb[:, c, :], c == 0, c == 7,
                       [r_hn, r_wgb], [r_pb])
            pv = pb[0:64, 0:NCH * 16].rearrange("p (n k) -> p n k", k=16)
            ACT(beta.rearrange("p h n -> p n h"), pv[:, :, 0:8], AF.Sigmoid, [r_pb], [r_beta])
            TT("dve", sptmp, pv[:, :, 8:16], vecs[0:64, V_DTB:V_DTB + 8].unsqueeze(1).to_broadcast([64, NCH, 8]),
               ALU.add, [r_pb, r_vecs], [r_sptmp])
            ACT(sptmp, sptmp, AF.Exp, [r_sptmp], [r_sptmp])
            ACT(sptmp, sptmp, AF.Ln, [r_sptmp, r_small], [r_sptmp], bias=c_one[0:64, :], scale=1.0)
            TT("dve", gtab.rearrange("p h n -> p n h"), sptmp, c_nA[0:64, :].unsqueeze(1).to_broadcast([64, NCH, 8]),
               ALU.mult, [r_sptmp, r_small], [r_g])
            gflat = gtab.rearrange("p h n -> p (h n)")
            pb, r_pb = BKg.next()
            MM(pb[0:64, 0:8 * NCH], triU, gflat, True, True, [r_g, r_cF], [r_pb])
            CP("dve", gcol.rearrange("p h n -> p (h n)"), pb[0:64, 0:8 * NCH], [r_pb], [r_gcol])
            ACT(negegc.rearrange("p h n -> p (h n)"), pb[0:64, 0:8 * NCH], AF.Exp, [r_pb], [r_negegc])
            TS("dve", negegc.rearrange("p h n -> p (h n)"), negegc.rearrange("p h n -> p (h n)"), -1.0, ALU.mult,
               [r_negegc], [r_negegc])
            pb2, r_pb2 = BKg.next()
            MM(pb2[:, 0:8 * NCH], ones_f[0:64, :], gflat, True, True, [r_g, r_ones], [r_pb2])
            ACT(egl.rearrange("p h n -> p (h n)"), pb2[:, 0:8 * NCH], AF.Exp, [r_pb2], [r_egl])
            TT("dve", kdecay.rearrange("p h n -> p (h n)"), pb2[0:64, 0:8 * NCH], gcol.rearrange("p h n -> p (h n)"),
               ALU.subtract, [r_pb2, r_gcol], [r_kdecay])
            ACT(kdecay.rearrange("p h n -> p (h n)"), kdecay.rearrange("p h n -> p (h n)"), AF.Exp, [r_kdecay], [r_kdecay])

            ub = [AR.alloc("ub%d" % i, [T + 4], BF16) for i in range(2)]
            for u_ap, r_u in ub:
                MSET("pool", u_ap[:, 0:4], 0.0, [r_u])
            qT, r_qT = AR.alloc("qT", [T], BF16)
            kT, r_kT = AR.alloc("kT", [T], BF16)
            vTf, r_vTf = AR.alloc("vTf", [T], BF16)
            zs, r_zs = AR.alloc("zs", [T], BF16)
            qdT, r_qdT = AR.alloc("qdT", [T], BF16)
            tmpq, r_tmpq = AR.alloc("tmpq", [T], BF16)
            r_qdTg = [getres("qdT_g%d" % g) for g in range(NG)]
            sq2 = [AR.alloc("sq2%d" % i, [512], BF16) for i in range(2)]
            rs2 = [AR.alloc("rs2%d" % i, [512], F32) for i in range(2)]
            Kd, r_Kd = AR.alloc("Kd", [NCH, 128], BF16, parts=64)
            Vt, r_Vt = AR.alloc("Vt", [NCH, 128], BF16, parts=64)
            gA, r_gA = AR.alloc("gA", [8, 64], F32, parts=64)
            decT, r_decT = AR.alloc("decT", [8, 64], F32, parts=64)
            nbs, r_nbs = AR.alloc("nbs", [8, 64], F32, parts=64)
            gB, r_gB = AR.alloc("gB", [8, 64], F32, parts=64)
            egcb, r_egcb = AR.alloc("egcb", [512], F32)
            Pb = [AR.alloc("P%d" % i, [8, 64], BF16, parts=64) for i in range(4)]
            Xb = [AR.alloc("X%d" % i, [8, 64], BF16, parts=64) for i in range(2)]
            Xall, r_Xall = AR.alloc("Xall", [NCH, 64], BF16, parts=64)
            intraT, r_intraT = AR.alloc("intraT", [NCH, 64], BF16, parts=64)
            oT, r_oT = AR.alloc("oT", [T], BF16)
            r_Xg = [getres("Xall_g%d" % g) for g in range(NG)]
            r_iTg = [getres("intraT_g%d" % g) for g in range(NG)]
            Sf, r_Sf = AR.alloc("Sf", [128], F32)
            Sb, r_Sb = AR.alloc("Sb", [128], BF16)
            Rb = [AR.alloc("R%d" % i, [128], BF16, parts=64) for i in range(2)]
            vnb = [AR.alloc("vn%d" % i, [128], BF16, parts=64) for i in range(2)]
            dg = [AR.alloc("dg%d" % i, [4, 128], BF16) for i in range(2)]
            ytmp = [AR.alloc("ytmp%d" % i, [512], F32) for i in range(1)] * 2
            specs = []
            for h in range(NH):
                for off in (OFF_GQ, OFF_GK, OFF_GV, OFF_GZ):
                    specs.append(wcols(w_in, off + h * 128))
            WS = WStream("wg", 3, 8, specs)
            dgi = 0
            for h in range(NH):
                for ti, (dst, r_dst) in enumerate(((tmpq, r_tmpq), (kT, r_kT), (vTf, r_vTf))):
                    wt, r_wt = WS.get(h * 4 + ti)
                    u_ap, r_u = ub[ti % 2]
                    for blk in range(NB):
                        pb, r_pb = BKg.next()
                        for c in range(8):
                            MM(pb[:, :], wt[:, c, :], hnT[:, c, blk * 512:(blk + 1) * 512], c == 0, c == 7,
                               [r_wt, r_hn], [r_pb])
                        CP("act" if blk % 2 == 0 else "dve", u_ap[:, 4 + blk * 512:4 + (blk + 1) * 512], pb[:, :], [r_pb], [r_u])
                    dga, r_dga = dg[dgi % 2]; dgi += 1
                    tile = ti * 8 + h
                    for j in range(4):
                        TS("dve", dga[:, j, :], ident, vecs[:, V_GCW + tile * 4 + j:V_GCW + tile * 4 + j + 1], ALU.mult,
                           [r_cB, r_vecs], [r_dga])
                    for blk in range(NB):
                        pb, r_pb = BKg.next()
                        for j in range(4):
                            MM(pb[:, :], dga[:, j, :], u_ap[:, 1 + j + blk * 512:1 + j + (blk + 1) * 512], j == 0, j == 3,
                               [r_dga, r_u], [r_pb])
                        ACT(dst[:, blk * 512:(blk + 1) * 512], pb[:, :], AF.Silu, [r_pb], [r_dst])
                wt, r_wt = WS.get(h * 4 + 3)
                for blk in range(NB):
                    pb, r_pb = BKg.next()
                    for c in range(8):
                        MM(pb[:, :], wt[:, c, :], hnT[:, c, blk * 512:(blk + 1) * 512], c == 0, c == 7, [r_wt, r_hn], [r_pb])
                    ACT(zs[:, blk * 512:(blk + 1) * 512], pb[:, :], AF.Silu, [r_pb], [r_zs])
                ii = 0
                for (src, r_src, dst, r_dst, scl) in ((tmpq, r_tmpq, qT, r_qT, 128 ** -0.5), (kT, r_kT, kT, r_kT, 1.0)):
                    for blk in range(NB):
                        sl = slice(blk * 512, (blk + 1) * 512)
                        sqa, r_sqa = sq2[ii % 2]; rsa, r_rsa = rs2[ii % 2]; ii += 1
                        TT("pool", sqa, src[:, sl], src[:, sl], ALU.mult, [r_src], [r_sqa])
                        pb, r_pb = BKg.next()
                        MM(pb[:, :], ones_b[:, :], sqa, True, True, [r_sqa, r_ones], [r_pb])
                        ACT(rsa, pb[:, :], AF.Ln, [r_pb, r_small], [r_rsa], scale=1.0, bias=c_eps)
                        ACT(rsa, rsa, AF.Exp, [r_rsa], [r_rsa], scale=-0.5)
                        STT("dve", dst[:, sl], src[:, sl], scl, rsa, ALU.mult, ALU.mult, [r_src, r_rsa], [r_dst])
                for (src, r_src, dst, r_dst, isk) in ((kT, r_kT, Kd, r_Kd, True), (vTf, r_vTf, Vt, r_Vt, False)):
                    for g in range(NG):
                        pb, r_pb = BKg.next()
                        pbb = pb[:].bitcast(BF16)
                        for j in range(8):
                            n = g * 8 + j
                            TR(pbb[0:64, j * 128:(j + 1) * 128], src[:, n * 64:(n + 1) * 64], ident, [r_src, r_cB], [r_pb])
                        pv3 = pbb[0:64, 0:1024].rearrange("p (a b) -> p a b", a=8)
                        if isk:
                            TT("dve", dst[:, g * 8:(g + 1) * 8, :], pv3,
                               kdecay[:, h, g * 8:(g + 1) * 8].unsqueeze(2).to_broadcast([64, 8, 128]), ALU.mult,
                               [r_pb, r_kdecay], [r_dst])
                        else:
                            CP("act", dst[:, g * 8:(g + 1) * 8, :], pv3, [r_pb], [r_dst])
                def inv_group(g, h=h):
                    ns = slice(g * 8, (g + 1) * 8)
                    cs = slice(g * 512, (g + 1) * 512)
                    TT("dve", gA, triU.unsqueeze(1).to_broadcast([64, 8, 64]),
                       gtab[:, h, ns].unsqueeze(2).to_broadcast([64, 8, 64]), ALU.mult, [r_cF, r_g], [r_gA])
                    pbc, r_pbc = BKg.next()
                    MM(pbc[:, :], ones_f[0:64, :], gA.rearrange("p a b -> p (a b)"), True, True, [r_gA, r_ones], [r_pbc])
                    pbc3 = pbc[0:64, :].rearrange("p (a b) -> p a b", a=8)
                    ACT(egcb, pbc[:, :], AF.Exp, [r_pbc], [r_egcb])
                    TT("dve", qdT[:, cs], qT[:, cs], egcb, ALU.mult, [r_qT, r_egcb], [r_qdTg[g]])
                    TT("dve", gA, pbc3, gcol[:, h, ns].unsqueeze(2).to_broadcast([64, 8, 64]), ALU.subtract,
                       [r_pbc, r_gcol], [r_gA])
                    TT("dve", gA, gA, maskneg.unsqueeze(1).to_broadcast([64, 8, 64]), ALU.min, [r_gA, r_cF], [r_gA])
                    ACT(decT, gA, AF.Exp, [r_gA], [r_decT])
                    TT("pool", nbs, strict.unsqueeze(1).to_broadcast([64, 8, 64]),
                       beta[:, h, ns].unsqueeze(2).to_broadcast([64, 8, 64]), ALU.mult, [r_cF, r_beta], [r_nbs])
                    pkk, r_pkk = BKg.next()
                    pqk, r_pqk = BKg.next()
                    for j in range(8):
                        n = g * 8 + j
                        tsl = slice(n * 64, (n + 1) * 64)
                        MM(pkk[0:64, j * 64:(j + 1) * 64], kT[:, tsl], kT[:, tsl], True, True, [r_kT], [r_pkk])
                    for j in range(8):
                        n = g * 8 + j
                        tsl = slice(n * 64, (n + 1) * 64)
                        MM(pqk[0:64, j * 64:(j + 1) * 64], kT[:, tsl], qT[:, tsl], True, True, [r_kT, r_qT], [r_pqk])
                    pkk3 = pkk[0:64, :].rearrange("p (a b) -> p a b", a=8)
                    pqk3 = pqk[0:64, :].rearrange("p (a b) -> p a b", a=8)
                    TT("dve", gB, pkk3, decT, ALU.mult, [r_pkk, r_decT], [r_gB])
                    P1, r_P1 = Pb[0]; P1T, r_P1T = Pb[1]
                    STT("dve", P1, gB, -1.0, nbs, ALU.mult, ALU.mult, [r_gB, r_nbs], [r_P1])
                    TT("dve", intraT[:, ns, :], pqk3, decT, ALU.mult, [r_pqk, r_decT], [r_iTg[g]])
                    ptb, r_ptb = BKg.next()
                    ptbb = ptb[:].bitcast(BF16)
                    for j in range(8):
                        TR(ptbb[0:64, j * 64:(j + 1) * 64], P1[:, j, :], ident[0:64, 0:64], [r_P1, r_cB], [r_ptb])
                    CP("act", P1T, ptbb[0:64, 0:512].rearrange("p (a b) -> p a b", a=8), [r_ptb], [r_P1T])
                    Xc, r_Xc = Xb[0]
                    TT("pool", Xc, P1, ident[0:64, 0:64].unsqueeze(1).to_broadcast([64, 8, 64]), ALU.add, [r_P1, r_cB], [r_Xc])
                    cur = 0
                    xi = 0
                    for lev in range(5):
                        (Pc, r_Pc), (PTc, r_PTc) = Pb[cur * 2], Pb[cur * 2 + 1]
                        (Pn, r_Pn), (PTn, r_PTn) = Pb[(1 - cur) * 2], Pb[(1 - cur) * 2 + 1]
                        last = lev == 4
                        if not last:
                            pa, r_pa = BKg.next()
                            for j in range(8):
                                MM(pa[0:64, j * 64:(j + 1) * 64], PTc[:, j, :], Pc[:, j, :], True, True, [r_Pc, r_PTc], [r_pa])
                        pt, r_pt = BKg.next()
                        for j in range(8):
                            MM(pt[0:64, j * 64:(j + 1) * 64], Pc[:, j, :], PTc[:, j, :], True, True, [r_Pc, r_PTc], [r_pt])
                        if not last:
                            CP("act", Pn, pa[0:64, :].rearrange("p (a b) -> p a b", a=8), [r_pa], [r_Pn])
                        CP("dve", PTn, pt[0:64, :].rearrange("p (a b) -> p a b", a=8), [r_pt], [r_PTn])
                        (Xc, r_Xc), (Xn, r_Xn) = Xb[xi], Xb[1 - xi]
                        px, r_px = BKg.next()
                        for j in range(8):
                            MM(px[0:64, j * 64:(j + 1) * 64], PTn[:, j, :], Xc[:, j, :], True, True, [r_PTn, r_Xc], [r_px])
                        if last:
                            TT("dve", Xall[:, ns, :], px[0:64, :].rearrange("p (a b) -> p a b", a=8), Xc, ALU.add,
                               [r_px, r_Xc], [r_Xg[g]])
                        else:
                            TT("dve", Xn, px[0:64, :].rearrange("p (a b) -> p a b", a=8), Xc, ALU.add, [r_px, r_Xc], [r_Xn])
                        xi = 1 - xi
                        cur = 1 - cur
                def scan_steps(g, h=h):
                    for n in range(g * 8, g * 8 + 8):
                        tsl = slice(n * 64, (n + 1) * 64)
                        pks, r_pks = psum[0][0:64, 0:128], r_bank[0]
                        py, r_py = psum[0][0:64, 128:256], r_bank[0]
                        po, r_po = psum[1][:, 0:64], r_bank[1]
                        pd, r_pd = psum[1][:, 64:192], r_bank[1]
                        Ra, r_Ra = Rb[n % 2]; vna, r_vna = vnb[n % 2]
                        col = slice(n, n + 1)
                        MM(pks, kT[:, tsl], Sb, True, True, [r_kT, r_Sb], [r_pks])
                        STT("dve", Ra, pks, negegc[:, h, col], Vt[:, n, :], ALU.mult, ALU.add,
                            [r_pks, r_negegc, r_Vt], [r_Ra])
                        MM(py, Xall[:, n, :], Ra, True, True, [r_Xg[g], r_Ra], [r_py])
                        TS("dve", vna, py, beta[:, h, col], ALU.mult, [r_py, r_beta], [r_vna])
                        MM(po, Sb, qdT[:, tsl], True, False, [r_Sb, r_qdTg[g]], [r_po])
                        MM(po, vna, intraT[:, n, :], False, True, [r_vna, r_iTg[g]], [r_po])
                        MM(pd, Kd[:, n, :], vna, True, True, [r_Kd, r_vna], [r_pd])
                        STT("dve", Sb, Sf, egl[:, h, col], pd, ALU.mult, ALU.add, [r_Sf, r_egl, r_pd], [r_Sb])
                        STT("dve", Sf, Sf, egl[:, h, col], pd, ALU.mult, ALU.add, [r_Sf, r_egl, r_pd], [r_Sf])
                        CP("act", oT[:, tsl], po, [r_po], [r_oT])

                def record(f):
                    saved = CUR[0]
                    CUR[0] = []
                    f()
                    out = CUR[0]
                    CUR[0] = saved
                    return out

                def splice(A, B):
                    ca = sum(x[4] for x in A) or 1.0
                    cb = sum(x[4] for x in B) or 1.0
                    ia = ib = 0
                    fa = fb_ = 0.0
                    while ia < len(A) or ib < len(B):
                        if ib >= len(B) or (ia < len(A) and fa / ca <= fb_ / cb):
                            CUR[0].append(A[ia]); fa += A[ia][4]; ia += 1
                        else:
                            CUR[0].append(B[ib]); fb_ += B[ib][4]; ib += 1

                BKg.set([2, 3])
                inv_group(0)
                MSET("dve", Sf, 0.0, [r_Sf])
                MSET("pool", Sb, 0.0, [r_Sb])
                for g in range(NG):
                    A = record(lambda: inv_group(g + 1)) if g + 1 < NG else []
                    B = record(lambda: scan_steps(g))
                    splice(A, B)
                BKg.set([0, 1, 2, 3])
                for blk in range(NB):
                    sl = slice(blk * 512, (blk + 1) * 512)
                    sqa, r_sqa = sq2[blk % 2]; rsa, r_rsa = rs2[blk % 2]; yta, r_yta = ytmp[blk % 2]
                    ACT(sqa, oT[:, sl], AF.Square, [r_oT], [r_sqa])
                    pb, r_pb = BKg.next()
                    MM(pb[:, :], ones_b[:, :], sqa, True, True, [r_sqa, r_ones], [r_pb])
                    ACT(rsa, pb[:, :], AF.Ln, [r_pb, r_small], [r_rsa], scale=1.0 / 128, bias=c_eps)
                    ACT(rsa, rsa, AF.Exp, [r_rsa], [r_rsa], scale=-0.5)
                    STT("dve", yta, oT[:, sl], vecs[:, V_GNW:V_GNW + 1], rsa, ALU.mult, ALU.mult, [r_oT, r_rsa, r_vecs], [r_yta])
                    yt_, r_yt_ = ytl[blk % 2]
                    TT("pool", yt_, yta, zs[:, sl], ALU.mult, [r_yta, r_zs], [r_yt_])
                    DMA("sp", yg_scr[s][h * 128:(h + 1) * 128, sl], yt_, [r_yt_], [r_scr], r_yt_)

        def diff_stream():
            qz = [AR.alloc("dqz%d" % i, [T], BF16) for i in range(2)]
            MSET("pool", qz[0][0][64:128, :], 0.0, [qz[0][1]])
            MSET("pool", qz[1][0][0:64, :], 0.0, [qz[1][1]])
            qT, r_qT = None, None
            kT, r_kT = AR.alloc("dkT", [T], BF16)
            Vk, r_Vk = AR.alloc("Vk", [NTB, 128], BF16)
            sq3 = [AR.alloc("sq3%d" % i, [512], BF16) for i in range(2)]
            rs3 = [AR.alloc("rs3%d" % i, [512], F32) for i in range(2)]
            qn = [AR.alloc("qn%d" % i, [512], BF16) for i in range(2)]
            t1b = [AR.alloc("t1%d" % i, [512], F32) for i in range(2)]
            t2b = [AR.alloc("t2%d" % i, [512], F32) for i in range(2)]
            pTb = [AR.alloc("pT%d" % i, [512], BF16) for i in range(3)]
            rcp = [AR.alloc("rcp%d" % i, [512], F32) for i in range(1)] * 2
            a12 = [AR.alloc("a12%d" % i, [512], F32) for i in range(2)]
            dTt, r_dT = AR.alloc("dTt", [512], F32)
            specs = []
            for h in range(NH):
                for off in (OFF_DQ, OFF_DK, OFF_DV):
                    specs.append(wcols(w_in, off + h * 128))
            WS = WStream("wd", 3, 8, specs)
            ii = 0
            pti = 0
            for h in range(NH):
                BKd.set([4, 5, 6, 7])
                for ti, (dst, r_dst, vcol) in enumerate(((qT, r_qT, V_QNW), (kT, r_kT, V_KNW))):
                    wt, r_wt = WS.get(h * 3 + ti)
                    for blk in range(NB):
                        sl = slice(blk * 512, (blk + 1) * 512)
                        sqa, r_sqa = sq3[ii % 2]; rsa, r_rsa = rs3[ii % 2]; qna, r_qna = qn[ii % 2]
                        t1, r_t1 = t1b[ii % 2]; t2, r_t2 = t2b[ii % 2]; ii += 1
                        pu, r_pu = BKd.next()
                        for c in range(8):
                            MM(pu[:, :], wt[:, c, :], hnT[:, c, sl], c == 0, c == 7, [r_wt, r_hn], [r_pu])
                        ACT(sqa, pu[:, :], AF.Square, [r_pu], [r_sqa])
                        pss, r_pss = BKd.next()
                        MM(pss[:, :], blockones, sqa, True, True, [r_sqa, r_cB], [r_pss])
                        ACT(rsa, pss[:, :], AF.Ln, [r_pss, r_small], [r_rsa], scale=1.0 / 64, bias=c_eps)
                        ACT(rsa, rsa, AF.Exp, [r_rsa], [r_rsa], scale=-0.5)
                        STT("dve", qna, pu[:, :], vecs[:, vcol:vcol + 1], rsa, ALU.mult, ALU.mult, [r_pu, r_rsa, r_vecs], [r_qna])
                        pr, r_pr = BKd.next()
                        MM(pr[:, :], ropeP, qna, True, True, [r_qna, r_cB], [r_pr])
                        TT("dve", t1, pr[:, :], ropeS[:, sl], ALU.mult, [r_pr, r_cB], [r_t1])
                        TT("pool", t2, qna, ropeC[:, sl], ALU.mult, [r_qna, r_cB], [r_t2])
                        if ti == 0:
                            TT("pool", qz[0][0][0:64, sl], t1[0:64, :], t2[0:64, :], ALU.add, [r_t1, r_t2], [qz[0][1]])
                            TT("pool", qz[1][0][64:128, sl], t1[64:128, :], t2[64:128, :], ALU.add, [r_t1, r_t2], [qz[1][1]])
                        else:
                            TT("pool", dst[:, sl], t1, t2, ALU.add, [r_t1, r_t2], [r_dst])
                wt, r_wt = WS.get(h * 3 + 2)
                for tb4 in range(NTB // 4):
                    pv_, r_pv = BKd.next()
                    for q4 in range(4):
                        tb = tb4 * 4 + q4
                        for c in range(8):
                            MM(pv_[:, q4 * 128:(q4 + 1) * 128], hnT[:, c, tb * 128:(tb + 1) * 128], wt[:, c, :], c == 0, c == 7,
                               [r_hn, r_wt], [r_pv])
                    CP("act" if tb4 % 2 == 0 else "dve", Vk[:, tb4 * 4:(tb4 + 1) * 4, :],
                       pv_[:, :].rearrange("p (a b) -> p a b", a=4), [r_pv], [r_Vk])
                BKd.set([6, 7])
                for qb in range(NB):
                    sl = slice(qb * 512, (qb + 1) * 512)
                    for m in range(2):
                        pO, r_pO = psum[4], r_bank[4]
                        pL, r_pL = psum[5], r_bank[5]
                        nkb = (qb + 1) * 4
                        ms = slice(64 * m, 64 * m + 64)
                        def stage1(kb):
                            nonlocal pti
                            j = kb - qb * 4
                            c0 = j * 128 if j >= 0 else 0
                            psc, r_psc = BKd.next()
                            pT, r_pT = pTb[pti % 3]; pti += 1
                            MM(psc[:, c0:512], kT[:, kb * 128:(kb + 1) * 128], qz[m][0][:, qb * 512 + c0:(qb + 1) * 512], True, True,
                               [r_kT, qz[m][1]], [r_psc])
                            ACT(pT[:, c0:512], psc[:, c0:512], AF.Exp, [r_psc], [r_pT], scale=0.125)
                            if j >= 0:
                                TT("pool", pT[:, c0:c0 + 128], pT[:, c0:c0 + 128], attmask, ALU.mult, [r_pT, r_cB], [r_pT])
                            return pT, r_pT, c0

                        cur_st = stage1(0)
                        for kb in range(nkb):
                            nxt_st = stage1(kb + 1) if kb + 1 < nkb else None
                            pT, r_pT, c0 = cur_st
                            MM(pO[:, c0:512], Vk[:, kb, :], pT[:, c0:512], kb == 0, kb == nkb - 1, [r_Vk, r_pT], [r_pO])
                            MM(pL[:, c0:512], ones_b[:, :], pT[:, c0:512], kb == 0, kb == nkb - 1, [r_ones, r_pT], [r_pL])
                            cur_st = nxt_st
                        rc, r_rc = rcp[m]; aa, r_aa = a12[m]
                        ACT(rc, pL[:, :], AF.Ln, [r_pL], [r_rc])
                        ACT(rc, rc, AF.Exp, [r_rc], [r_rc], scale=-1.0)
                        TT("dve", aa, pO[:, :], rc, ALU.mult, [r_pO, r_rc], [r_aa])
                    STT("dve", dTt, a12[1][0], c_nlam, a12[0][0], ALU.mult, ALU.add, [a12[1][1], a12[0][1], r_small], [r_dT])
                    sqa, r_sqa = sq3[ii % 2]; rsa, r_rsa = rs3[ii % 2]; ii += 1
                    ACT(sqa, dTt, AF.Square, [r_dT], [r_sqa])
                    pss, r_pss = BKd.next()
                    MM(pss[:, :], ones_b[:, :], sqa, True, True, [r_sqa, r_ones], [r_pss])
                    ACT(rsa, pss[:, :], AF.Ln, [r_pss, r_small], [r_rsa], scale=1.0 / 128, bias=c_eps)
                    ACT(rsa, rsa, AF.Exp, [r_rsa], [r_rsa], scale=-0.5)
                    yt_, r_yt_ = ytl[2 + qb % 2]
                    STT("dve", yt_, dTt, c_slw8, rsa, ALU.mult, ALU.mult, [r_dT, r_rsa, r_small], [r_yt_])
                    DMA("sp", yd_scr[s][h * 128:(h + 1) * 128, sl], yt_, [r_yt_], [r_scr], r_yt_)

        run_streams([gdn_stream, diff_stream])

        if stop_after < 4:
            continue
        S.barrier()
        BK.set(range(8))
        AR.off = mark_mixer
        ygT, r_yg = AR.alloc("ygT", [8, T], BF16)
        ydT, r_yd = AR.alloc("ydT", [8, T], BF16)
        for c in range(8):
            DMA("sp", ygT[:, c, :], yg_scr[s][c * 128:(c + 1) * 128, :], [r_scr], [r_yg], r_yg)
            DMA("sp", ydT[:, c, :], yd_scr[s][c * 128:(c + 1) * 128, :], [r_scr], [r_yd], r_yd)
        dbg_dump("ygT", s, ygT, [r_yg], True)
        dbg_dump("ydT", s, ydT, [r_yd], True)
        mgT, r_mg = AR.alloc("mgT", [8, T], BF16)
        gsb = [AR.alloc("gs%d" % i, [512], F32) for i in range(4)]
        m12 = [AR.alloc("m12%d" % i, [512], F32) for i in range(4)]
        specs = []
        for m in range(8):
            specs += [wcols(w_in, OFF_GATE + m * 128), wcols(w_in, OFF_GATE + D + m * 128),
                      wcols(w_go, m * 128), wcols(w_do, m * 128)]
        WS = WStream("wm", 8, 8, specs, hold=4)
        ii = 0
        for m in range(8):
            wts = [WS.get(m * 4 + k) for k in range(4)]
            for blk in range(NB):
                sl = slice(blk * 512, (blk + 1) * 512)
                mm_ = []
                for k in range(2):
                    (wg_, r_wg_), (wy_, r_wy_) = wts[k], wts[2 + k]
                    ysrc, r_ysrc = ((ygT, r_yg), (ydT, r_yd))[k]
                    ga, r_ga = gsb[ii % 4]; ma, r_ma = m12[ii % 4]; ii += 1
                    pg, r_pg = BK.next()
                    for c in range(8):
                        MM(pg[:, :], wg_[:, c, :], hnT[:, c, sl], c == 0, c == 7, [r_wg_, r_hn], [r_pg])
                    ACT(ga, pg[:, :], AF.Sigmoid, [r_pg, r_vecs], [r_ga], bias=vecs[:, V_BG + k * 8 + m:V_BG + k * 8 + m + 1], scale=1.0)
                    py_, r_py_ = BK.next()
                    for c in range(8):
                        MM(py_[:, :], wy_[:, c, :], ysrc[:, c, sl], c == 0, c == 7, [r_wy_, r_ysrc], [r_py_])
                    TT("dve", ma, py_[:, :], ga, ALU.mult, [r_py_, r_ga], [r_ma])
                    mm_.append((ma, r_ma))
                TT("pool", mgT[:, m, sl], mm_[0][0], mm_[1][0], ALU.add, [mm_[0][1], mm_[1][1]], [r_mg])
        dbg_dump("mgT", s, mgT, [r_mg], True)

        S.barrier()
        AR.off = 0
        x1T = AR.alloc("x1T", [8, T], F32)[0]
        r_x1 = [getres("x1_%d" % c) for c in range(8)]
        assert AR.off <= mark_mixer + 2 * 8 * T * 2
        AR.off = mark_mixer + 3 * 8 * T * 2
        specs = [wcols(w_o, m * 128) for m in range(8)]
        WS = WStream("wo", 3, 8, specs)
        for m in range(8):
            DMA("sp", x1T[:, m, :], xsrc[:, m, :], [], [r_x1[m]], r_x1[m])
        for m in range(8):
            wt, r_wt = WS.get(m)
            for blk in range(NB):
                sl = slice(blk * 512, (blk + 1) * 512)
                pb, r_pb = BK.next()
                for c in range(8):
                    MM(pb[:, :], wt[:, c, :], mgT[:, c, sl], c == 0, c == 7, [r_wt, r_mg], [r_pb])
                TT("dve", x1T[:, m, sl], x1T[:, m, sl], pb[:, :], ALU.add, [r_x1[m], r_pb], [r_x1[m]])
        if dbg:
            dst = dbg_t["x1T"][s].rearrange("(c p) t -> p c t", p=128)
            rr = getres("dbg_x1")
            for c in range(8):
                DMA("sp", dst[:, c, :], x1T[:, c, :], [r_x1[c]], [rr], rr)
            if rr not in out_res:
                out_res.append(rr)

        if stop_after < 5:
            continue
        S.barrier()
        AR.off = 8 * T * 4
        h2T, r_h2 = AR.alloc("h2T", [8, T], BF16)
        sqb, r_sqb = AR.alloc("sq5", [8, 512], BF16)
        rt5, r_rt5 = AR.alloc("rt5", [512], F32)
        for blk in range(NB):
            sl = slice(blk * 512, (blk + 1) * 512)
            ACT(sqb, x1T[:, :, sl], AF.Square, r_x1, [r_sqb])
            pb, r_pb = BK.next()
            for c in range(8):
                MM(pb[:, :], ones_b[:, :], sqb[:, c, :], c == 0, c == 7, [r_sqb, r_ones], [r_pb])
            ACT(rt5, pb[:, :], AF.Ln, [r_pb, r_small], [r_rt5], scale=1.0 / D, bias=c_eps)
            ACT(rt5, rt5, AF.Exp, [r_rt5], [r_rt5], scale=-0.5)
            for c in range(8):
                STT("dve", h2T[:, c, sl], x1T[:, c, sl], vecs[:, V_N2 + c:V_N2 + c + 1], rt5, ALU.mult, ALU.mult,
                    [r_x1[c], r_rt5, r_vecs], [r_h2])
        S.barrier()
        AR.off = 8 * T * 4 + 8 * T * 2
        actT = AR.alloc("actT", [22, FB], BF16)[0]
        r_act = [getres("act0"), getres("act1")]
        stg = [AR.alloc("stg%d" % i, [FB + 4], BF16) for i in range(4)]
        sgb = [AR.alloc("sg%d" % i, [512], BF16) for i in range(4)]
        dgf = [AR.alloc("dgf%d" % i, [3, 128], BF16) for i in range(4)]
        r_out = getres("out")
        if r_out not in out_res:
            out_res.append(r_out)
        odst = outT[s].rearrange("(c p) t -> p c t", p=128)
        mark_w = AR.off
        for fb in range(NFB):
            t0 = fb * FB
            AR.off = mark_w
            WD = WStream("wdn", 2, 22, [wcols(w_dn, m * 128) for m in range(8)])

            def up_stream(par):
                jl = list(range(par, 22, 2))
                specs = []
                for j in jl:
                    specs += [wcols(w_up, j * 128), wcols(w_up, D_FF + j * 128)]
                WSx = WStream("wu%d" % par, 3, 8, specs)
                bk = Banks(); bk.set([0, 1, 2, 3] if par == 0 else [4, 5, 6, 7])
                si = 0
                for ji, j in enumerate(jl):
                    for k in range(2):
                        wt, r_wt = WSx.get(ji * 2 + k)
                        st, r_st = stg[par * 2 + si % 2]; dga, r_dga = dgf[par * 2 + si % 2]; si += 1
                        tile = k * 22 + j
                        for tap in range(3):
                            TS("dve", dga[:, tap, :], ident, vecs[:, V_FCW + tile * 3 + tap:V_FCW + tile * 3 + tap + 1],
                               ALU.mult, [r_cB, r_vecs], [r_dga])
                        if fb == 0:
                            MSET("pool", st[:, 0:4], 0.0, [r_st])
                        else:
                            ph, r_ph = bk.next()
                            for c in range(8):
                                MM(ph[:, 0:2], wt[:, c, :], h2T[:, c, t0 - 2:t0], c == 0, c == 7, [r_wt, r_h2], [r_ph])
                            CP("dve", st[:, 2:4], ph[:, 0:2], [r_ph], [r_st])
                        for sub in range(FSUB):
                            pb, r_pb = bk.next()
                            for c in range(8):
                                MM(pb[:, :], wt[:, c, :], h2T[:, c, t0 + sub * 512:t0 + (sub + 1) * 512], c == 0, c == 7,
                                   [r_wt, r_h2], [r_pb])
                            CP("act" if (sub + k) % 2 == 0 else "dve", st[:, 4 + sub * 512:4 + (sub + 1) * 512], pb[:, :], [r_pb], [r_st])
                        for sub in range(FSUB):
                            pb, r_pb = bk.next()
                            for tap in range(3):
                                MM(pb[:, :], dga[:, tap, :], st[:, 2 + tap + sub * 512:2 + tap + (sub + 1) * 512], tap == 0, tap == 2,
                                   [r_dga, r_st], [r_pb])
                            sg_, r_sg_ = sgb[par * 2 + sub % 2]
                            if k == 0:
                                ACT(sg_, pb[:, :], AF.Silu, [r_pb, r_vecs], [r_sg_], bias=vecs[:, V_FCB + tile:V_FCB + tile + 1], scale=1.0)
                            else:
                                STT("dve", actT[:, j, sub * 512:(sub + 1) * 512], pb[:, :], vecs[:, V_FCB + tile:V_FCB + tile + 1], sg_,
                                    ALU.add, ALU.mult, [r_pb, r_sg_, r_vecs], [r_act[par]])

            run_streams([lambda: up_stream(0), lambda: up_stream(1)])
            for m in range(8):
                wt, r_wt = WD.get(m)
                for sub in range(FSUB):
                    sl = slice(t0 + sub * 512, t0 + (sub + 1) * 512)
                    pb, r_pb = BK.next()
                    for c in range(22):
                        MM(pb[:, :], wt[:, c, :], actT[:, c, sub * 512:(sub + 1) * 512], c == 0, c == 21, [r_wt, r_act[c % 2]], [r_pb])
                    TT("dve", x1T[:, m, sl], x1T[:, m, sl], pb[:, :], ALU.add, [r_x1[m], r_pb], [r_x1[m]])
                DMA("sp", odst[:, m, t0:t0 + FB], x1T[:, m, t0:t0 + FB], [r_x1[m]], [r_out], r_out)

    S.emit(es, final_res=out_res)
    es.close()
    return nc, S


def _consts(T):
    cF = np.zeros((128, 192), np.float32)
    cR = np.zeros((128, 2 * T), np.float32)
    k = np.arange(64)
    cF[:64, 0:64] = (k[:, None] <= k[None, :]).astype(np.float32)
    cF[:64, 64:128] = np.where(k[None, :] >= k[:, None], 0.0, -1e30).astype(np.float32)
    cF[:64, 128:192] = (k[:, None] < k[None, :]).astype(np.float32)
    pos = np.arange(T, dtype=np.float32)
    inv_freq = (np.float32(500000.0) ** (-(np.arange(0, 16, 2, dtype=np.float32)) / np.float32(16))).astype(np.float32)
    ang = (pos[:, None] * inv_freq[None, :]).astype(np.float32)
    cos = np.cos(ang.astype(np.float64)).astype(np.float32).T
    sin = np.sin(ang.astype(np.float64)).astype(np.float32).T
    C = np.ones((128, T), np.float32)
    Sg = np.zeros((128, T), np.float32)
    P = np.zeros((128, 128), np.float32)
    for base in (0, 64):
        C[base:base + 8] = cos; C[base + 8:base + 16] = cos
        Sg[base:base + 8] = -sin; Sg[base + 8:base + 16] = sin
        for r in range(8):
            P[base + r, base + r + 8] = 1.0
            P[base + r + 8, base + r] = 1.0
    cR[:, 0:T] = C
    cR[:, T:] = Sg
    cB = np.zeros((128, 512), np.float32)
    cB[:, 0:128] = np.eye(128, dtype=np.float32)
    cB[:64, 128:192] = 1.0
    cB[64:, 192:256] = 1.0
    cB[:, 256:384] = P
    p = np.arange(128)
    cB[:, 384:512] = (p[:, None] <= p[None, :]).astype(np.float32)
    return cF, cB, cR


def _vecs(inp):
    v = np.zeros((128, NV), np.float32)
    g = lambda k: np.asarray(inp[k], np.float32)[0]
    v[:, V_N1:V_N1 + 8] = g("norm1_w").reshape(8, 128).T
    v[:, V_N2:V_N2 + 8] = g("norm2_w").reshape(8, 128).T
    v[:, V_BG:V_BG + 16] = g("b_gate").reshape(16, 128).T
    v[:, V_GCW:V_GCW + 96] = g("gdn_conv_w").reshape(4, 24, 128).transpose(2, 1, 0).reshape(128, 96)
    v[:, V_FCW:V_FCW + 132] = g("ffn_conv_w").reshape(3, 44, 128).transpose(2, 1, 0).reshape(128, 132)
    v[:, V_FCB:V_FCB + 44] = g("ffn_conv_b").reshape(44, 128).T
    v[:, V_GNW] = g("gdn_norm_w")
    v[:, V_SLW] = g("diff_subln_w")
    v[:, V_QNW] = np.tile(g("diff_q_norm_w"), 2)
    v[:, V_KNW] = np.tile(g("diff_k_norm_w"), 2)
    v[:, V_ALOG:V_ALOG + 8] = g("gdn_A_log")[None, :]
    v[:, V_DTB:V_DTB + 8] = g("gdn_dt_bias")[None, :]
    v[:64, V_LAM + 0] = g("lambda_q1"); v[:64, V_LAM + 1] = g("lambda_k1")
    v[:64, V_LAM + 2] = g("lambda_q2"); v[:64, V_LAM + 3] = g("lambda_k2")
    return v


_CACHE = {}


def run(inputs, T, B, FB, dbg=False, ncores=8):
    NSEQ = B // ncores
    key = (T, NSEQ, FB, dbg)
    if key not in _CACHE:
        _CACHE[key] = build_program(T, NSEQ, FB, dbg)
    nc, S = _CACHE[key]
    x = np.asarray(inputs["x"], np.float32)
    cF, cB, cR = _consts(T)
    vecs = _vecs(inputs)
    shared = {
        "w_in": np.ascontiguousarray(np.asarray(inputs["w_in"], np.float32)[0]),
        "w_gdn_out": np.ascontiguousarray(np.asarray(inputs["w_gdn_out"], np.float32)[0]),
        "w_diff_out": np.ascontiguousarray(np.asarray(inputs["w_diff_out"], np.float32)[0]),
        "w_o": np.ascontiguousarray(np.asarray(inputs["w_o"], np.float32)[0]),
        "w_up": np.ascontiguousarray(np.asarray(inputs["w_up"], np.float32)[0]),
        "w_down": np.ascontiguousarray(np.asarray(inputs["w_down"], np.float32)[0]),
        "vecs": vecs, "cF": cF, "cB": cB, "cR": cR,
    }
    in_maps = []
    for c in range(ncores):
        m = dict(shared)
        m["xT"] = np.ascontiguousarray(x[c * NSEQ:(c + 1) * NSEQ].transpose(0, 2, 1))
        in_maps.append(m)
    res = run_bass_kernel_spmd(nc, in_maps, core_ids=list(range(ncores)))
    out = np.concatenate([r["outT"].transpose(0, 2, 1) for r in res.results], axis=0)
    if dbg:
        extra = {}
        for nm in ("ygT", "ydT", "x1T", "hnT", "mgT"):
            extra[nm] = np.concatenate([r["dbg_" + nm].transpose(0, 2, 1) for r in res.results], axis=0)
        extra["small"] = res.results[0]["dbg_small"]
        return np.ascontiguousarray(out), extra
    return np.ascontiguousarray(out)


def kernel(**inputs):
    return run(inputs, T=2048, B=16, FB=1024)
```
